# Optimizing a Trainium2 kernel written in Bass

```python
import jax, jax.numpy as jnp
from jax import lax
import numpy as np

D_MODEL = 2048
BATCH = 4
SEQ = 8192
DEPTH = 4

N_META = 16
CHUNK = 128
NORM_EPS = 1e-6

RET_HEADS = 8
RET_HEAD_DIM = 128
RET_WIDTH = RET_HEADS * RET_HEAD_DIM
ROPE_BASE = 10000.0

RWKV_HEADS = 16
RWKV_HEAD_DIM = 64
RWKV_WIDTH = RWKV_HEADS * RWKV_HEAD_DIM
RWKV_LORA_W = 64
RWKV_LORA_A = 64
RWKV_LORA_G = 160
RWKV_SHIFT_COLS = 3 * RWKV_WIDTH + RWKV_LORA_W + RWKV_LORA_A + RWKV_LORA_G
RWKV_LN_EPS = 64e-5

SSD_HEADS = 32
SSD_HEAD_DIM = 64
SSD_WIDTH = SSD_HEADS * SSD_HEAD_DIM
SSD_GROUPS = 4
SSD_STATE = 128
SSD_CONV = 4
SSD_CONV_DIM = SSD_WIDTH + 2 * SSD_GROUPS * SSD_STATE

FFN_HIDDEN = 256 * ((8 * D_MODEL + 3 * 256 - 1) // (3 * 256))

IN_SPLITS = (RET_WIDTH, RET_WIDTH, RET_WIDTH, RET_WIDTH, RWKV_SHIFT_COLS, SSD_WIDTH, SSD_CONV_DIM, SSD_HEADS, D_MODEL, D_MODEL, D_MODEL)
IN_COLS = sum(IN_SPLITS)

kernel_name = "hybrid_retention_rwkv7_ssd_gated_trunk"


def _split(a, sizes):
    return jnp.split(a, [int(s) for s in np.cumsum(sizes)[:-1]], axis=-1)


def _rms(x, eps):
    xf = x.astype(jnp.float32)
    return xf * lax.rsqrt(jnp.mean(xf * xf, axis=-1, keepdims=True) + eps)


def rms_norm(x, w):
    return (_rms(x, NORM_EPS) * w.astype(jnp.float32)).astype(x.dtype)


def pad_front(a, n):
    return jnp.pad(a, [(0, 0), (n, 0)] + [(0, 0)] * (a.ndim - 2))


def token_shift(a):
    return pad_front(a, 1)[:, :-1]


def rotary(x, pos):
    half = x.shape[-1] // 2
    inv = ROPE_BASE ** (-jnp.arange(half, dtype=jnp.float32) / half)
    ang = pos.astype(jnp.float32)[:, None] * inv[None, :]
    cos = jnp.cos(ang)[None, :, None, :]
    sin = jnp.sin(ang)[None, :, None, :]
    x1, x2 = x[..., :half], x[..., half:]
    return jnp.concatenate([x1 * cos - x2 * sin, x1 * sin + x2 * cos], axis=-1)


def inter_chunk_states(local, decay):
    def step(state, inp):
        s_loc, dec = inp
        return state * dec[..., None, None] + s_loc, state
    init = jnp.zeros_like(local[:, 0])
    _, entering = lax.scan(step, init, (jnp.moveaxis(local, 1, 0), jnp.moveaxis(decay, 1, 0)))
    return jnp.moveaxis(entering, 0, 1)


def retention(q, k, v):
    b, L = q.shape[:2]
    pos = jnp.arange(L)
    q = rotary(q, pos)
    k = rotary(k, pos) * (RET_HEAD_DIM ** -0.5)
    pad = CHUNK - N_META
    qc, kc, vc = [pad_front(t, pad).reshape(b, -1, CHUNK, RET_HEADS, RET_HEAD_DIM) for t in (q, k, v)]
    log_g = jnp.log(1.0 - 2.0 ** (-5.0 - jnp.arange(RET_HEADS, dtype=jnp.float32)))
    idx = jnp.arange(CHUNK, dtype=jnp.float32)
    diff = idx[:, None] - idx[None, :]
    causal = diff >= 0
    decay_mask = jnp.where(causal[None], jnp.exp(jnp.where(causal, diff, 0.0)[None] * log_g[:, None, None]), 0.0)
    scores = jnp.einsum('bnihd,bnjhd->bnhij', qc, kc) * decay_mask
    y_intra = jnp.einsum('bnhij,bnjhe->bnihe', scores, vc)
    k_decay = jnp.exp((CHUNK - 1 - idx)[None, :] * log_g[:, None])
    local = jnp.einsum('bnjhd,hj,bnjhe->bnhde', kc, k_decay, vc)
    chunk_decay = jnp.broadcast_to(jnp.exp(CHUNK * log_g), local.shape[:3])
    entering = inter_chunk_states(local, chunk_decay)
    q_decay = jnp.exp((idx + 1.0)[None, :] * log_g[:, None])
    y_inter = jnp.einsum('bnihd,hi,bnhde->bnihe', qc, q_decay, entering)
    y = (y_intra + y_inter).reshape(b, -1, RET_HEADS, RET_HEAD_DIM)
    return y[:, pad:]


def rwkv7_scan(r, w, k, v, kk, a):
    def step(S, inp):
        r_t, w_t, k_t, v_t, kk_t, a_t = inp
        sa = jnp.einsum('bhvk,bhk->bhv', S, -kk_t)
        S = S * w_t[:, :, None, :] + sa[..., None] * (kk_t * a_t)[:, :, None, :] + v_t[..., :, None] * k_t[:, :, None, :]
        return S, jnp.einsum('bhvk,bhk->bhv', S, r_t)
    b, L, H, N = r.shape
    S0 = jnp.zeros((b, H, N, N), jnp.float32)
    xs = tuple(jnp.moveaxis(t, 1, 0) for t in (r, w, k, v, kk, a))
    _, y = lax.scan(step, S0, xs)
    return jnp.moveaxis(y, 0, 1)


def rwkv7_branch(cols, mu, w0, w2, a0, a2, g2, k_k, k_a, r_k, ln_w, ln_b):
    b, L = cols.shape[:2]
    cols = cols.astype(jnp.float32)
    cols = cols + (token_shift(cols) - cols) * mu
    r, k, v, w_low, a_low, g_low = _split(cols, (RWKV_WIDTH, RWKV_WIDTH, RWKV_WIDTH, RWKV_LORA_W, RWKV_LORA_A, RWKV_LORA_G))
    w_log = -jax.nn.softplus(-(w0 + jnp.tanh(w_low) @ w2)) - 0.5
    decay = jnp.exp(-jnp.exp(w_log))
    a = jax.nn.sigmoid(a0 + a_low @ a2)
    g = jax.nn.sigmoid(g_low) @ g2
    hs = lambda t: t.reshape(b, L, RWKV_HEADS, RWKV_HEAD_DIM)
    kk = hs(k * k_k)
    kk = kk / jnp.maximum(jnp.sqrt(jnp.sum(kk * kk, axis=-1, keepdims=True)), 1e-12)
    k = k * (1.0 + (a - 1.0) * k_a)
    y = rwkv7_scan(hs(r), hs(decay), hs(k), hs(v), kk, hs(a))
    mean = jnp.mean(y, axis=-1, keepdims=True)
    var = jnp.mean(jnp.square(y - mean), axis=-1, keepdims=True)
    y = ((y - mean) * lax.rsqrt(var + RWKV_LN_EPS)).reshape(b, L, RWKV_WIDTH) * ln_w + ln_b
    bonus = jnp.sum(hs(r) * hs(k) * r_k, axis=-1, keepdims=True) * hs(v)
    return (y + bonus.reshape(b, L, RWKV_WIDTH)) * g


def causal_depthwise_conv(x, w, bias):
    out = lax.conv_general_dilated(x, w[:, None, :], window_strides=(1,), padding=[(SSD_CONV - 1, 0)],
                                   dimension_numbers=('NWC', 'WIO', 'NWC'), feature_group_count=x.shape[-1])
    return out + bias


def ssd_branch(z, xbc, dt_raw, conv_w, conv_b, dt_bias, a_log, d_skip, norm_w):
    b, L = z.shape[:2]
    G, R, P, N = SSD_GROUPS, SSD_HEADS // SSD_GROUPS, SSD_HEAD_DIM, SSD_STATE
    xbc = jax.nn.silu(causal_depthwise_conv(xbc.astype(jnp.float32), conv_w.astype(jnp.float32), conv_b.astype(jnp.float32)))
    xs, Bm, Cm = _split(xbc, (SSD_WIDTH, G * N, G * N))
    dt = jax.nn.softplus(dt_raw.astype(jnp.float32) + dt_bias)
    A = -jnp.exp(a_log.astype(jnp.float32))
    pad = CHUNK - N_META
    x_h = pad_front(xs, pad).reshape(b, -1, CHUNK, G, R, P)
    Bc = pad_front(Bm, pad).reshape(b, -1, CHUNK, G, N)
    Cc = pad_front(Cm, pad).reshape(b, -1, CHUNK, G, N)
    dtc = pad_front(dt, pad).reshape(b, -1, CHUNK, G, R)
    n = x_h.shape[1]
    xdt = x_h * dtc[..., None]
    cs = jnp.moveaxis(jnp.cumsum(dtc * A.reshape(G, R), axis=2), 2, -1)
    idx = jnp.arange(CHUNK)
    causal = idx[:, None] >= idx[None, :]
    Lmat = jnp.exp(jnp.where(causal, cs[..., :, None] - cs[..., None, :], -jnp.inf))
    cb = jnp.einsum('bnigs,bnjgs->bngij', Cc, Bc)
    y_diag = jnp.einsum('bngij,bngrij,bnjgrp->bnigrp', cb, Lmat, xdt)
    decay_states = jnp.exp(cs[..., -1:] - cs)
    local = jnp.einsum('bnlgs,bngrl,bnlgrp->bngrps', Bc, decay_states, xdt).reshape(b, n, SSD_HEADS, P, N)
    chunk_decay = jnp.exp(cs[..., -1]).reshape(b, n, SSD_HEADS)
    entering = inter_chunk_states(local, chunk_decay).reshape(b, n, G, R, P, N)
    y_off = jnp.einsum('bnlgs,bngrps,bngrl->bnlgrp', Cc, entering, jnp.exp(cs))
    y = y_diag + y_off + x_h * d_skip.reshape(G, R)[..., None]
    y = y.reshape(b, -1, SSD_WIDTH)[:, pad:]
    y = y * jax.nn.silu(z.astype(jnp.float32))
    y = _rms(y.reshape(b, L, G, SSD_WIDTH // G), NORM_EPS).reshape(b, L, SSD_WIDTH)
    return y * norm_w


def mixer_block(h, w_in, w_branch_ret, w_branch_rwkv, w_branch_ssd, w_out,
                rwkv_mu, rwkv_w0, rwkv_w2, rwkv_a0, rwkv_a2, rwkv_g2, rwkv_k_k, rwkv_k_a, rwkv_r_k, rwkv_ln_w, rwkv_ln_b,
                ssd_conv_w, ssd_conv_b, ssd_dt_bias, ssd_a_log, ssd_d, ssd_norm_w):
    b, L, _ = h.shape
    dtype = h.dtype
    rq, rk, rv, rg, rwkv_cols, z, xbc, dt_raw, gate_a, gate_b, gate_c = _split(h @ w_in, IN_SPLITS)
    hs = lambda t: t.astype(jnp.float32).reshape(b, L, RET_HEADS, RET_HEAD_DIM)
    y_ret = retention(hs(rq), hs(rk), hs(rv))
    y_ret = jax.nn.silu(rg.astype(jnp.float32)) * _rms(y_ret, NORM_EPS).reshape(b, L, RET_WIDTH)
    y_rwkv = rwkv7_branch(rwkv_cols, rwkv_mu, rwkv_w0, rwkv_w2, rwkv_a0, rwkv_a2, rwkv_g2,
                          rwkv_k_k, rwkv_k_a, rwkv_r_k, rwkv_ln_w, rwkv_ln_b)
    y_ssd = ssd_branch(z, xbc, dt_raw, ssd_conv_w, ssd_conv_b, ssd_dt_bias, ssd_a_log, ssd_d, ssd_norm_w)
    merged = (jax.nn.sigmoid(gate_a) * (y_ret.astype(dtype) @ w_branch_ret)
              + jax.nn.sigmoid(gate_b) * (y_rwkv.astype(dtype) @ w_branch_rwkv)
              + jax.nn.sigmoid(gate_c) * (y_ssd.astype(dtype) @ w_branch_ssd))
    return merged @ w_out


def swiglu(h, w_gate, w_up, w_down):
    return (jax.nn.silu(h @ w_gate) * (h @ w_up)) @ w_down


def setup_inputs(seed: int = 0) -> dict:
    key = jax.random.key(seed)
    ks = jax.random.split(key, 40)
    nrm = lambda k, shape, s: s * jax.random.normal(k, shape, jnp.float32)
    gain = lambda k, shape: 1.0 + 0.02 * jax.random.normal(k, shape, jnp.float32)
    dt_init = jnp.exp(jax.random.uniform(ks[30], (DEPTH, SSD_HEADS), jnp.float32, np.log(1e-3), np.log(1e-1)))
    return {
        'x': nrm(ks[0], (BATCH, SEQ, D_MODEL), 1.0),
        'meta_tokens': nrm(ks[1], (N_META, D_MODEL), 1.0),
        'norm_mix_pre': gain(ks[2], (DEPTH, D_MODEL)),
        'norm_mix_post': gain(ks[3], (DEPTH, D_MODEL)),
        'norm_ffn_pre': gain(ks[4], (DEPTH, D_MODEL)),
        'norm_ffn_post': gain(ks[5], (DEPTH, D_MODEL)),
        'w_in': nrm(ks[6], (DEPTH, D_MODEL, IN_COLS), D_MODEL ** -0.5),
        'w_branch_ret': nrm(ks[7], (DEPTH, RET_WIDTH, D_MODEL), RET_WIDTH ** -0.5),
        'w_branch_rwkv': nrm(ks[8], (DEPTH, RWKV_WIDTH, D_MODEL), RWKV_WIDTH ** -0.5),
        'w_branch_ssd': nrm(ks[9], (DEPTH, SSD_WIDTH, D_MODEL), SSD_WIDTH ** -0.5),
        'w_out': nrm(ks[10], (DEPTH, D_MODEL, D_MODEL), D_MODEL ** -0.5),
        'rwkv_mu': jax.random.uniform(ks[11], (DEPTH, RWKV_SHIFT_COLS), jnp.float32),
        'rwkv_w0': jnp.linspace(-6.0, -1.0, RWKV_WIDTH, dtype=jnp.float32)[None, :] + nrm(ks[12], (DEPTH, RWKV_WIDTH), 0.1),
        'rwkv_w2': nrm(ks[13], (DEPTH, RWKV_LORA_W, RWKV_WIDTH), 0.5 * RWKV_LORA_W ** -0.5),
        'rwkv_a0': nrm(ks[14], (DEPTH, RWKV_WIDTH), 0.1),
        'rwkv_a2': nrm(ks[15], (DEPTH, RWKV_LORA_A, RWKV_WIDTH), RWKV_LORA_A ** -0.5),
        'rwkv_g2': nrm(ks[16], (DEPTH, RWKV_LORA_G, RWKV_WIDTH), RWKV_LORA_G ** -0.5),
        'rwkv_k_k': 0.85 + nrm(ks[17], (DEPTH, RWKV_WIDTH), 0.02),
        'rwkv_k_a': gain(ks[18], (DEPTH, RWKV_WIDTH)),
        'rwkv_r_k': nrm(ks[19], (DEPTH, RWKV_HEADS, RWKV_HEAD_DIM), 0.1),
        'rwkv_ln_w': gain(ks[20], (DEPTH, RWKV_WIDTH)),
        'rwkv_ln_b': nrm(ks[21], (DEPTH, RWKV_WIDTH), 0.02),
        'ssd_conv_w': nrm(ks[22], (DEPTH, SSD_CONV, SSD_CONV_DIM), SSD_CONV ** -0.5),
        'ssd_conv_b': nrm(ks[23], (DEPTH, SSD_CONV_DIM), 0.02),
        'ssd_dt_bias': dt_init + jnp.log(-jnp.expm1(-dt_init)),
        'ssd_a_log': jnp.log(jax.random.uniform(ks[24], (DEPTH, SSD_HEADS), jnp.float32, 1.0, 16.0)),
        'ssd_d': gain(ks[25], (DEPTH, SSD_HEADS)),
        'ssd_norm_w': gain(ks[26], (DEPTH, SSD_WIDTH)),
        'ffn_w_gate': nrm(ks[27], (DEPTH, D_MODEL, FFN_HIDDEN), D_MODEL ** -0.5),
        'ffn_w_up': nrm(ks[28], (DEPTH, D_MODEL, FFN_HIDDEN), D_MODEL ** -0.5),
        'ffn_w_down': nrm(ks[29], (DEPTH, FFN_HIDDEN, D_MODEL), FFN_HIDDEN ** -0.5),
    }


def reference(x, meta_tokens, norm_mix_pre, norm_mix_post, norm_ffn_pre, norm_ffn_post,
              w_in, w_branch_ret, w_branch_rwkv, w_branch_ssd, w_out,
              rwkv_mu, rwkv_w0, rwkv_w2, rwkv_a0, rwkv_a2, rwkv_g2, rwkv_k_k, rwkv_k_a, rwkv_r_k, rwkv_ln_w, rwkv_ln_b,
              ssd_conv_w, ssd_conv_b, ssd_dt_bias, ssd_a_log, ssd_d, ssd_norm_w,
              ffn_w_gate, ffn_w_up, ffn_w_down):
    b = x.shape[0]
    meta = jnp.broadcast_to(meta_tokens[None].astype(x.dtype), (b, N_META, D_MODEL))
    h = jnp.concatenate([meta, x], axis=1)
    for i in range(DEPTH):
        m = mixer_block(rms_norm(h, norm_mix_pre[i]), w_in[i], w_branch_ret[i], w_branch_rwkv[i], w_branch_ssd[i], w_out[i],
                        rwkv_mu[i], rwkv_w0[i], rwkv_w2[i], rwkv_a0[i], rwkv_a2[i], rwkv_g2[i], rwkv_k_k[i], rwkv_k_a[i],
                        rwkv_r_k[i], rwkv_ln_w[i], rwkv_ln_b[i],
                        ssd_conv_w[i], ssd_conv_b[i], ssd_dt_bias[i], ssd_a_log[i], ssd_d[i], ssd_norm_w[i])
        h = h + rms_norm(m, norm_mix_post[i])
        f = swiglu(rms_norm(h, norm_ffn_pre[i]), ffn_w_gate[i], ffn_w_up[i], ffn_w_down[i])
        h = h + rms_norm(f, norm_ffn_post[i])
    return h[:, N_META:]
```

```python
import numpy as np
from contextlib import ExitStack
import concourse.bass as bass
import concourse.mybir as mybir
from concourse.bass_utils import run_bass_kernel_spmd

F32 = mybir.dt.float32
BF16 = mybir.dt.bfloat16
AF = mybir.ActivationFunctionType
ALU = mybir.AluOpType

D = 2048
NMETA = 16
CH = 128
DEPTH = 4
EPS = 1e-6
RET_H, RET_D, RET_W = 8, 128, 1024
RW_H, RW_N, RW_W = 16, 64, 1024
RW_COLS = 3 * 1024 + 64 + 64 + 160
SSD_H, SSD_P, SSD_W, SSD_G, SSD_N = 32, 64, 2048, 4, 128
SSD_CONV = 3072
FFN = 5632
IN_COLS = 18752
NP_COLS = IN_COLS - 3 * D
O_RQ, O_RK, O_RV, O_RG = 0, 1024, 2048, 3072
O_RW = 4096
O_Z = O_RW + RW_COLS
O_XBC = O_Z + SSD_W
O_DT = O_XBC + SSD_CONV
assert O_DT + SSD_H == NP_COLS


_UID = [0]


def uq(name):
    _UID[0] += 1
    return f"{name}_{_UID[0]}"


class Rec:
    ENGS = ("pe", "dve", "act", "pool", "sp")

    def __init__(self):
        self.ops = {k: [] for k in self.ENGS}
        self.cnt = {}
        self.seen = {k: {} for k in self.ENGS}
        self.lastw = {}
        self.rds = {}
        self.nops = 0

    def _deps(self, eng, reads, writes):
        need = {}

        def want(ev):
            s, v = ev
            if self.seen[eng].get(s, 0) < v and need.get(s, 0) < v:
                need[s] = v
        for k in reads:
            if k in self.lastw:
                want(self.lastw[k])
        for k in writes:
            if k in self.lastw:
                want(self.lastw[k])
            for ev in self.rds.get(k, {}).items():
                want(ev)
        for s, v in need.items():
            self.seen[eng][s] = v
        return list(need.items())

    def _commit(self, ev, reads, writes):
        s, v = ev
        for k in reads:
            d = self.rds.setdefault(k, {})
            if d.get(s, 0) < v:
                d[s] = v
        for k in writes:
            self.lastw[k] = ev
            self.rds[k] = {}

    countdown = None

    def op(self, eng, insts, reads=(), writes=()):
        if self.countdown is not None:
            if self.countdown <= 0:
                return
            self.countdown -= 1
        waits = self._deps(eng, reads, writes)
        s = "c_" + eng
        self.cnt[s] = self.cnt.get(s, 0) + 1
        self.ops[eng].append((waits, insts, s, 1))
        self._commit((s, self.cnt[s]), reads, writes)
        self.nops += len(insts)

    def dma(self, q, semkey, kwargs, reads=(), writes=()):
        if self.countdown is not None and self.countdown <= 0:
            return
        waits = self._deps(q, reads, writes)
        s = "d_" + semkey
        self.cnt[s] = self.cnt.get(s, 0) + 16
        self.ops[q].append((waits, [("dma_start", kwargs)], s, 16))
        self._commit((s, self.cnt[s]), reads, writes)
        self.nops += 1

    def barrier(self):
        for eng in self.ENGS:
            waits = []
            for s, v in self.cnt.items():
                if self.seen[eng].get(s, 0) < v:
                    waits.append((s, v))
                    self.seen[eng][s] = v
            if waits:
                self.ops[eng].append((waits, [], None, 0))
        self.lastw = {}
        self.rds = {}

    def emit(self, nc):
        blockname = {"pe": "tensor", "dve": "vector", "act": "scalar", "pool": "gpsimd", "sp": "sync"}
        with ExitStack() as st:
            sems = {name: st.enter_context(nc.semaphore(name)) for name in self.cnt}
            block = st.enter_context(nc.Block())
            for eng in self.ENGS:
                def body(e, eng=eng):
                    for waits, insts, s, inc in self.ops[eng]:
                        for ws, wv in waits:
                            e.wait_ge(sems[ws], wv)
                        last = None
                        for name, kw in insts:
                            last = getattr(e, name)(**kw)
                        if last is not None:
                            last.then_inc(sems[s], inc)
                    if eng == "sp":
                        for name, v in self.cnt.items():
                            e.wait_ge(sems[name], v)
                getattr(block, blockname[eng])(body)


class Cfg:
    def __init__(self, nchunks=64, layers=(0, 1, 2, 3), debug=False, depth=DEPTH):
        self.nchunks = nchunks
        self.layers = tuple(layers)
        self.debug = debug
        self.depth = depth
        import os
        self.stop = os.environ.get("KSTOP", "")
        self.phases = os.environ.get("KPHASES", "ARSWCD")
        self.yt_input = bool(os.environ.get("KYTIN", ""))
        self.S = nchunks * CH
        self.L = NMETA + self.S
        self.tiles = [(0, NMETA)] + [(NMETA + CH * i, CH) for i in range(nchunks)]


def mm_group(out, pairs):
    n = len(pairs)
    return [("matmul", dict(out=out, lhsT=l, rhs=r, start=(i == 0), stop=(i == n - 1))) for i, (l, r) in enumerate(pairs)]


def passes_of(tiles, per_pass):
    out = []
    i = 0
    while i < len(tiles):
        k = per_pass + 1 if i == 0 else per_pass
        out.append(list(range(i, min(i + k, len(tiles)))))
        i += k
    return out


def norm_transpose_stage(R, nc, st, cfg, tl, Hsrc, wn_bc, xT, ident, pfx, pst):
    offs = []
    off = 0
    for j, t in enumerate(tl):
        r0, n = cfg.tiles[t]
        b = j % 2
        hb = st["h"][b]
        R.dma("sp", f"{pfx}h{b}", dict(out=hb[:n], in_=Hsrc[r0:r0 + n, :]), reads=[("H", t)], writes=[(pfx, "h", b)])
        R.op("act", [("activation", dict(out=st["junk"][:n], in_=hb[:n], func=AF.Square, accum_out=st["ss"][:n]))],
             reads=[(pfx, "h", b)], writes=[(pfx, "junk"), (pfx, "ss")])
        R.op("dve", [("tensor_scalar", dict(out=st["ms"][:n], in0=st["ss"][:n], scalar1=1.0 / D, scalar2=EPS, op0=ALU.mult, op1=ALU.add))],
             reads=[(pfx, "ss")], writes=[(pfx, "ms")])
        R.op("act", [("activation", dict(out=st["rt"][:n], in_=st["ms"][:n], func=AF.Sqrt))], reads=[(pfx, "ms")], writes=[(pfx, "rt")])
        R.op("dve", [("reciprocal", dict(out=st["rstd"][:n], in_=st["rt"][:n]))], reads=[(pfx, "rt")], writes=[(pfx, "rstd")])
        xn = st["xn"][b]
        R.op("dve", [("scalar_tensor_tensor", dict(out=xn[:n], in0=hb[:n], scalar=st["rstd"][:n], in1=wn_bc[:n], op0=ALU.mult, op1=ALU.mult))],
             reads=[(pfx, "h", b), (pfx, "rstd"), (pfx, "wn")], writes=[(pfx, "xn", b)])
        pt = pst[b]
        R.op("pe", [("transpose", dict(out=pt[:, k, :n], in_=xn[:n, k * 128:(k + 1) * 128], identity=ident[:n, :n])) for k in range(D // 128)],
             reads=[(pfx, "xn", b), "ident"], writes=[("psT", b)])
        R.op("act" if j % 2 == 0 else "dve",
             [("activation", dict(out=xT[:, :, off:off + n], in_=pt[:, :, :n], func=AF.Copy))] if j % 2 == 0 else
             [("tensor_copy", dict(out=xT[:, :, off:off + n], in_=pt[:, :, :n]))],
             reads=[("psT", b)], writes=[(pfx, "xT", j)])
        offs.append((t, off, n))
        off += n
    return offs


def phase_A(R, nc, cfg, li, dr, ident):
    TP = 8
    H, SGT, w_in = dr["H"], dr["SGT"], dr["w_in"]
    with ExitStack() as es:
        sb = lambda name, shape, dt: es.enter_context(nc.sbuf_tensor(uq(name), shape, dt))
        ps = lambda name, shape, dt: es.enter_context(nc.psum_tensor(uq(name), shape, dt))
        st = dict(h=[sb(f"A_h{i}", [128, D], F32) for i in range(2)], junk=sb("A_junk", [128, D], BF16),
                  ss=sb("A_ss", [128, 1], F32), ms=sb("A_ms", [128, 1], F32), rt=sb("A_rt", [128, 1], F32),
                  rstd=sb("A_rstd", [128, 1], F32), xn=[sb(f"A_xn{i}", [128, D], BF16) for i in range(2)])
        wn_bc = sb("A_wn", [128, D], F32)
        TMAX = NMETA + TP * CH
        xT = sb("A_xT", [128, D // 128, TMAX], BF16)
        NWB = 3
        wb = [sb(f"A_wb{i}", [128, D // 128, 512], BF16) for i in range(NWB)]
        NSTG = 4
        stg = [sb(f"A_stg{i}", [128, 512], F32) for i in range(NSTG)]
        pst = [ps(f"A_pst{i}", [128, D // 128, 128], BF16) for i in range(2)]
        NPS = 4
        psm = [ps(f"A_psm{i}", [128, 512], F32) for i in range(NPS)]

        R.dma("sp", "A_wn", dict(out=wn_bc[:], in_=dr["norm_mix_pre"][li:li + 1, :].partition_broadcast(128)), writes=[("A", "wn")])
        slabs = []
        for (sec0, secw, tname) in ((0, 4096, "P_ret"), (O_RW, RW_COLS, "P_rw"), (O_Z, NP_COLS - O_Z, "P_ssd")):
            for c in range(0, secw, 512):
                slabs.append((sec0 + c, min(512, secw - c), False, tname, c))
        slabs += [(c, 512, True, None, 0) for c in range(NP_COLS, IN_COLS, 512)]
        wi = 0
        si = 0
        pi = 0
        for tl in passes_of(cfg.tiles, TP):
            offs = norm_transpose_stage(R, nc, st, cfg, tl, H, wn_bc, xT, ident, "A", pst)
            ntok = sum(n for _, _, n in offs)
            if cfg.stop == "A1":
                continue
            row_base = cfg.tiles[tl[0]][0]
            xkeys = [("A", "xT", j) for j in range(len(tl))]
            for (c0, w, is_gate, tname, lc0) in slabs:
                wbuf = wb[wi % NWB]
                wkey = ("A", "wb", wi % NWB)
                wi += 1
                R.dma("pool", f"A_wb{(wi - 1) % NWB}",
                      dict(out=wbuf[:, :, :w], in_=w_in[li, :, c0:c0 + w].rearrange("(k p) n -> p k n", p=128)),
                      writes=[wkey])
                if not is_gate:
                    for j, (t, off, n) in enumerate(offs):
                        pb = psm[pi % NPS]
                        pkey = ("psm", pi % NPS)
                        pi += 1
                        R.op("pe", mm_group(pb[:n, :w], [(xT[:, k, off:off + n], wbuf[:, k, :w]) for k in range(D // 128)]),
                             reads=[wkey, xkeys[j]], writes=[pkey])
                        sg = stg[si % NSTG]
                        skey = ("A", "stg", si % NSTG)
                        if si % 2 == 0:
                            R.op("act", [("activation", dict(out=sg[:n, :w], in_=pb[:n, :w], func=AF.Copy))], reads=[pkey], writes=[skey])
                        else:
                            R.op("dve", [("tensor_copy", dict(out=sg[:n, :w], in_=pb[:n, :w]))], reads=[pkey], writes=[skey])
                        r0 = cfg.tiles[t][0]
                        R.dma("sp", f"A_stg{si % NSTG}", dict(out=dr[tname][r0:r0 + n, lc0:lc0 + w], in_=sg[:n, :w]), reads=[skey], writes=[("P", t)])
                        si += 1
                else:
                    g0 = c0 - NP_COLS
                    for q in range(4):
                        for tg in range(0, ntok, 512):
                            tw = min(512, ntok - tg)
                            pb = psm[pi % NPS]
                            pkey = ("psm", pi % NPS)
                            pi += 1
                            R.op("pe", mm_group(pb[:, :tw], [(wbuf[:, k, q * 128:(q + 1) * 128], xT[:, k, tg:tg + tw]) for k in range(D // 128)]),
                                 reads=[wkey] + xkeys, writes=[pkey])
                            sg = stg[si % NSTG]
                            skey = ("A", "stg", si % NSTG)
                            R.op("act", [("activation", dict(out=sg[:, :tw], in_=pb[:, :tw], func=AF.Sigmoid))], reads=[pkey], writes=[skey])
                            R.dma("sp", f"A_stg{si % NSTG}",
                                  dict(out=SGT[g0 + q * 128:g0 + (q + 1) * 128, row_base + tg:row_base + tg + tw], in_=sg[:, :tw]),
                                  reads=[skey], writes=[("SGT", tl[0])])
                            si += 1
    R.barrier()


def rstd_ops(R, st, pfx, src, n, reads):
    R.op("act", [("activation", dict(out=st["junk"][:n], in_=src, func=AF.Square, accum_out=st["ss"][:n]))],
         reads=reads, writes=[(pfx, "junk"), (pfx, "ss")])
    R.op("dve", [("tensor_scalar", dict(out=st["ms"][:n], in0=st["ss"][:n], scalar1=1.0 / D, scalar2=EPS, op0=ALU.mult, op1=ALU.add))],
         reads=[(pfx, "ss")], writes=[(pfx, "ms")])
    R.op("act", [("activation", dict(out=st["rt"][:n], in_=st["ms"][:n], func=AF.Sqrt))], reads=[(pfx, "ms")], writes=[(pfx, "rt")])
    R.op("dve", [("reciprocal", dict(out=st["rstd"][:n], in_=st["rt"][:n]))], reads=[(pfx, "rt")], writes=[(pfx, "rstd")])


def phase_C(R, nc, cfg, li, dr, ident):
    TP = 4
    H, SGT, YT = dr["H"], dr["SGT"], dr["YT"]
    TMAX = NMETA + TP * CH
    KY = 32
    for tl in passes_of(cfg.tiles, TP):
        offs = []
        off = 0
        for t in tl:
            offs.append((t, off, cfg.tiles[t][1]))
            off += cfg.tiles[t][1]
        ntok = off
        row_base = cfg.tiles[tl[0]][0]
        with ExitStack() as es0:
            mT = es0.enter_context(nc.sbuf_tensor(uq("C_mT"), [128, D // 128, TMAX], BF16))
            with ExitStack() as es:
                sb = lambda name, shape, dt: es.enter_context(nc.sbuf_tensor(uq(name), shape, dt))
                ps = lambda name, shape, dt: es.enter_context(nc.psum_tensor(uq(name), shape, dt))
                yT = sb("C_yT", [128, KY, TMAX], BF16)
                wbc = [sb(f"C_wb{i}", [128, KY, 512], BF16) for i in range(2)]
                sg = [sb(f"C_sg{i}", [128, 3, 512], F32) for i in range(2)]
                m1 = [sb(f"C_m1_{i}", [128, 512], F32) for i in range(2)]
                m2 = [sb(f"C_m2_{i}", [128, 512], F32) for i in range(2)]
                m3 = [sb(f"C_m3_{i}", [128, 512], F32) for i in range(2)]
                psb = [[ps(f"C_ps{i}_{j}", [128, 512], F32) for j in range(3)] for i in range(2)]
                R.dma("sp", "C_yT", dict(out=yT[:, :, :ntok], in_=YT[:, row_base:row_base + ntok].rearrange("(k p) t -> p k t", p=128)),
                      reads=[("YT", t) for t in tl], writes=[("C", "yT")])
                it = 0
                for sl in range(D // 512):
                    wb = wbc[sl % 2]
                    wkey = ("C", "wb", sl % 2)
                    c0 = sl * 512
                    R.dma("pool", f"C_wb{sl % 2}", dict(out=wb[:, 0:8, :], in_=dr["w_branch_ret"][li, :, c0:c0 + 512].rearrange("(k p) n -> p k n", p=128)), writes=[wkey])
                    R.dma("pool", f"C_wb{sl % 2}", dict(out=wb[:, 8:16, :], in_=dr["w_branch_rwkv"][li, :, c0:c0 + 512].rearrange("(k p) n -> p k n", p=128)), writes=[wkey])
                    R.dma("pool", f"C_wb{sl % 2}", dict(out=wb[:, 16:32, :], in_=dr["w_branch_ssd"][li, :, c0:c0 + 512].rearrange("(k p) n -> p k n", p=128)), writes=[wkey])
                    for q in range(4):
                        f0 = c0 + q * 128
                        for tg in range(0, ntok, 512):
                            tw = min(512, ntok - tg)
                            b = it % 2
                            it += 1
                            R.dma("sp", f"C_sg{b}", dict(out=sg[b][:, :, :tw],
                                  in_=SGT[:, row_base + tg:row_base + tg + tw].rearrange("(b f) t -> f b t", b=3)[f0:f0 + 128]),
                                  reads=[("SGT", 0)], writes=[("C", "sg", b)])
                            for br, (k0, k1) in enumerate(((0, 8), (8, 16), (16, 32))):
                                R.op("pe", mm_group(psb[b][br][:, :tw], [(wb[:, k, q * 128:(q + 1) * 128], yT[:, k, tg:tg + tw]) for k in range(k0, k1)]),
                                     reads=[wkey, ("C", "yT")], writes=[("C", "ps", b, br)])
                            for br, mm_ in enumerate((m1, m2, m3)):
                                R.op("dve", [("tensor_tensor", dict(out=mm_[b][:, :tw], in0=psb[b][br][:, :tw], in1=sg[b][:, br, :tw], op=ALU.mult))],
                                     reads=[("C", "ps", b, br), ("C", "sg", b)], writes=[("C", "m", br, b)])
                            R.op("pool", [("tensor_tensor", dict(out=m1[b][:, :tw], in0=m1[b][:, :tw], in1=m2[b][:, :tw], op=ALU.add))],
                                 reads=[("C", "m", 1, b)], writes=[("C", "m", 0, b)])
                            R.op("pool", [("tensor_tensor", dict(out=mT[:, sl * 4 + q, tg:tg + tw], in0=m1[b][:, :tw], in1=m3[b][:, :tw], op=ALU.add))],
                                 reads=[("C", "m", 0, b), ("C", "m", 2, b)], writes=[("C", "mT")])
            R.barrier()
            with ExitStack() as es:
                sb = lambda name, shape, dt: es.enter_context(nc.sbuf_tensor(uq(name), shape, dt))
                ps = lambda name, shape, dt: es.enter_context(nc.psum_tensor(uq(name), shape, dt))
                wo = sb("C_wo", [128, D // 128, D], BF16)
                wn_bc = sb("C_wn", [128, D], F32)
                st = dict(junk=sb("C_junk", [128, D], BF16), ss=sb("C_ss", [128, 1], F32), ms=sb("C_ms", [128, 1], F32),
                          rt=sb("C_rt", [128, 1], F32), rstd=sb("C_rstd", [128, 1], F32))
                hb = [sb(f"C_h{i}", [128, D], F32) for i in range(2)]
                tmp = [sb(f"C_tmp{i}", [128, D], F32) for i in range(2)]
                psm = [ps(f"C_psm{i}", [128, D], F32) for i in range(2)]
                for hh in range(2):
                    R.dma("pool", "C_wo", dict(out=wo[:, :, hh * 1024:(hh + 1) * 1024], in_=dr["w_out"][li, :, hh * 1024:(hh + 1) * 1024].rearrange("(k p) n -> p k n", p=128)),
                          writes=[("C", "wo")])
                R.dma("sp", "C_wn", dict(out=wn_bc[:], in_=dr["norm_mix_post"][li:li + 1, :].partition_broadcast(128)), writes=[("C", "wn")])
                for j, (t, off, n) in enumerate(offs):
                    b = j % 2
                    r0 = cfg.tiles[t][0]
                    R.dma("sp", f"C_h{b}", dict(out=hb[b][:n], in_=H[r0:r0 + n, :]), reads=[("H", t)], writes=[("C", "h", b)])
                    for s4 in range(4):
                        R.op("pe", mm_group(psm[b][:n, s4 * 512:(s4 + 1) * 512], [(mT[:, k, off:off + n], wo[:, k, s4 * 512:(s4 + 1) * 512]) for k in range(D // 128)]),
                             reads=[("C", "mT"), ("C", "wo")], writes=[("C", "psm", b, s4)])
                    pk = [("C", "psm", b, s4) for s4 in range(4)]
                    rstd_ops(R, st, "C", psm[b][:n, :], n, pk)
                    R.op("dve", [("scalar_tensor_tensor", dict(out=tmp[b][:n], in0=psm[b][:n, :], scalar=st["rstd"][:n], in1=wn_bc[:n], op0=ALU.mult, op1=ALU.mult))],
                         reads=pk + [("C", "rstd"), ("C", "wn")], writes=[("C", "tmp", b)])
                    R.op("pool", [("tensor_tensor", dict(out=hb[b][:n], in0=hb[b][:n], in1=tmp[b][:n], op=ALU.add))],
                         reads=[("C", "tmp", b)], writes=[("C", "h", b)])
                    R.dma("sp", f"C_hs{b}", dict(out=H[r0:r0 + n, :], in_=hb[b][:n]), reads=[("C", "h", b)], writes=[("H", t)])
            R.barrier()


def phase_D(R, nc, cfg, li, dr, ident):
    TP = 4
    H = dr["H"]
    TMAX = NMETA + TP * CH
    KF = FFN // 128
    for tl in passes_of(cfg.tiles, TP):
        with ExitStack() as es0:
            aT = es0.enter_context(nc.sbuf_tensor(uq("D_aT"), [128, KF, TMAX], BF16))
            with ExitStack() as es:
                sb = lambda name, shape, dt: es.enter_context(nc.sbuf_tensor(uq(name), shape, dt))
                ps = lambda name, shape, dt: es.enter_context(nc.psum_tensor(uq(name), shape, dt))
                st = dict(h=[sb(f"D_h{i}", [128, D], F32) for i in range(2)], junk=sb("D_junk", [128, D], BF16),
                          ss=sb("D_ss", [128, 1], F32), ms=sb("D_ms", [128, 1], F32), rt=sb("D_rt", [128, 1], F32),
                          rstd=sb("D_rstd", [128, 1], F32), xn=[sb(f"D_xn{i}", [128, D], BF16) for i in range(2)])
                wn_bc = sb("D_wn", [128, D], F32)
                xT = sb("D_xT", [128, D // 128, TMAX], BF16)
                wg = [sb(f"D_wg{i}", [128, D // 128, 512], BF16) for i in range(2)]
                wu = [sb(f"D_wu{i}", [128, D // 128, 512], BF16) for i in range(2)]
                sl_t = [sb(f"D_sl{i}", [128, 512], F32) for i in range(2)]
                pst = [ps(f"D_pst{i}", [128, D // 128, 128], BF16) for i in range(2)]
                psg = [ps(f"D_psg{i}", [128, 512], F32) for i in range(2)]
                psu = [ps(f"D_psu{i}", [128, 512], F32) for i in range(2)]
                R.dma("sp", "D_wn", dict(out=wn_bc[:], in_=dr["norm_ffn_pre"][li:li + 1, :].partition_broadcast(128)), writes=[("D", "wn")])
                offs = norm_transpose_stage(R, nc, st, cfg, tl, H, wn_bc, xT, ident, "D", pst)
                ntok = sum(n for _, _, n in offs)
                xkeys = [("D", "xT", j) for j in range(len(tl))]
                it = 0
                for sl in range(FFN // 512):
                    b2 = sl % 2
                    c0 = sl * 512
                    R.dma("pool", f"D_wg{b2}", dict(out=wg[b2][:], in_=dr["ffn_w_gate"][li, :, c0:c0 + 512].rearrange("(k p) n -> p k n", p=128)), writes=[("D", "wg", b2)])
                    R.dma("pool", f"D_wu{b2}", dict(out=wu[b2][:], in_=dr["ffn_w_up"][li, :, c0:c0 + 512].rearrange("(k p) n -> p k n", p=128)), writes=[("D", "wu", b2)])
                    for q in range(4):
                        for tg in range(0, ntok, 512):
                            tw = min(512, ntok - tg)
                            b = it % 2
                            it += 1
                            R.op("pe", mm_group(psg[b][:, :tw], [(wg[b2][:, k, q * 128:(q + 1) * 128], xT[:, k, tg:tg + tw]) for k in range(D // 128)]),
                                 reads=[("D", "wg", b2)] + xkeys, writes=[("D", "psg", b)])
                            R.op("pe", mm_group(psu[b][:, :tw], [(wu[b2][:, k, q * 128:(q + 1) * 128], xT[:, k, tg:tg + tw]) for k in range(D // 128)]),
                                 reads=[("D", "wu", b2)] + xkeys, writes=[("D", "psu", b)])
                            R.op("act", [("activation", dict(out=sl_t[b][:, :tw], in_=psg[b][:, :tw], func=AF.Silu))], reads=[("D", "psg", b)], writes=[("D", "sl", b)])
                            R.op("dve", [("tensor_tensor", dict(out=aT[:, sl * 4 + q, tg:tg + tw], in0=psu[b][:, :tw], in1=sl_t[b][:, :tw], op=ALU.mult))],
                                 reads=[("D", "psu", b), ("D", "sl", b)], writes=[("D", "aT")])
            R.barrier()
            with ExitStack() as es:
                sb = lambda name, shape, dt: es.enter_context(nc.sbuf_tensor(uq(name), shape, dt))
                ps = lambda name, shape, dt: es.enter_context(nc.psum_tensor(uq(name), shape, dt))
                CW = 256
                wd = [sb(f"D_wd{i}", [128, KF, CW], BF16) for i in range(2)]
                fst = sb("D_f", [128, TP + 1, D], F32)
                wn_bc = sb("D_wn2", [128, D], F32)
                st = dict(junk=sb("D_junk2", [128, D], BF16), ss=sb("D_ss2", [128, 1], F32), ms=sb("D_ms2", [128, 1], F32),
                          rt=sb("D_rt2", [128, 1], F32), rstd=sb("D_rstd2", [128, 1], F32))
                hb = [sb(f"D_hb{i}", [128, D], F32) for i in range(2)]
                psf = [ps(f"D_psf{i}", [128, 512], F32) for i in range(4)]
                R.dma("sp", "D_wn2", dict(out=wn_bc[:], in_=dr["norm_ffn_post"][li:li + 1, :].partition_broadcast(128)), writes=[("D", "wn2")])
                it = 0
                for sl in range(D // CW):
                    b2 = sl % 2
                    c0 = sl * CW
                    for hh in range(2):
                        R.dma("pool", f"D_wd{b2}", dict(out=wd[b2][:, hh * 22:(hh + 1) * 22, :],
                              in_=dr["ffn_w_down"][li, hh * 22 * 128:(hh + 1) * 22 * 128, c0:c0 + CW].rearrange("(k p) n -> p k n", p=128)), writes=[("D", "wd", b2)])
                    for j, (t, off, n) in enumerate(offs):
                        b = it % 4
                        it += 1
                        R.op("pe", mm_group(psf[b][:n, :CW], [(aT[:, k, off:off + n], wd[b2][:, k, :]) for k in range(KF)]),
                             reads=[("D", "aT"), ("D", "wd", b2)], writes=[("D", "psf", b)])
                        if it % 2 == 0:
                            R.op("act", [("activation", dict(out=fst[:n, j, c0:c0 + CW], in_=psf[b][:n, :CW], func=AF.Copy))], reads=[("D", "psf", b)], writes=[("D", "f", j, sl)])
                        else:
                            R.op("dve", [("tensor_copy", dict(out=fst[:n, j, c0:c0 + CW], in_=psf[b][:n, :CW]))], reads=[("D", "psf", b)], writes=[("D", "f", j, sl)])
                for j, (t, off, n) in enumerate(offs):
                    b = j % 2
                    r0 = cfg.tiles[t][0]
                    fk = [("D", "f", j, sl) for sl in range(D // CW)]
                    R.dma("sp", f"D_hb{b}", dict(out=hb[b][:n], in_=H[r0:r0 + n, :]), reads=[("H", t)], writes=[("D", "hb", b)])
                    rstd_ops(R, st, "D3", fst[:n, j, :], n, fk)
                    R.op("dve", [("scalar_tensor_tensor", dict(out=fst[:n, j, :], in0=fst[:n, j, :], scalar=st["rstd"][:n], in1=wn_bc[:n], op0=ALU.mult, op1=ALU.mult))],
                         reads=fk + [("D3", "rstd"), ("D", "wn2")], writes=[("D", "f2", j)])
                    R.op("pool", [("tensor_tensor", dict(out=hb[b][:n], in0=hb[b][:n], in1=fst[:n, j, :], op=ALU.add))],
                         reads=[("D", "f2", j), ("D", "hb", b)], writes=[("D", "hb", b)])
                    R.dma("sp", f"D_hs{b}", dict(out=H[r0:r0 + n, :], in_=hb[b][:n]), reads=[("D", "hb", b)], writes=[("H", t)])
            R.barrier()


AX = mybir.AxisListType


def bc(ap, shape):
    return ap.to_broadcast(list(shape))


def phase_B_ret(R, nc, cfg, li, dr, ident):
    P, YT = dr["P_ret"], dr["YT"]
    Hh, Dh = RET_H, RET_D
    with ExitStack() as es:
        sb = lambda name, shape, dt: es.enter_context(nc.sbuf_tensor(uq(name), shape, dt))
        ps = lambda name, shape, dt: es.enter_context(nc.psum_tensor(uq(name), shape, dt))
        maskT = sb("R_maskT", [128, Hh, 128], F32)
        vec = sb("R_vec", [128, 3, Hh], F32)
        cdec = sb("R_cdec", [128, Hh, Dh], F32)
        S = sb("R_S", [128, Hh, Dh], F32)
        S_bf = sb("R_Sbf", [128, Hh, Dh], BF16)
        pr = [sb(f"R_pr{i}", [128, 4 * RET_W], F32) for i in range(2)]
        rope = [sb(f"R_rope{i}", [128, 4, 64], F32) for i in range(2)]
        tt = [sb(f"R_t{i}", [128, Hh, 64], F32) for i in range(8)]
        qr = sb("R_qr", [128, Hh, Dh], BF16)
        kr = sb("R_kr", [128, Hh, Dh], BF16)
        kd = sb("R_kd", [128, Hh, Dh], BF16)
        v_bf = sb("R_vbf", [128, Hh, Dh], BF16)
        qkT = sb("R_qkT", [128, 2 * Hh, 128], BF16)
        sT = sb("R_sT", [128, Hh, 128], BF16)
        yo = sb("R_yo", [128, Hh, Dh], F32)
        y = sb("R_y", [128, Hh, Dh], F32)
        ysq = sb("R_ysq", [128, Hh, Dh], F32)
        sgl = sb("R_sg", [128, Hh, Dh], F32)
        sm = [sb(f"R_sm{i}", [128, Hh], F32) for i in range(4)]
        yb = sb("R_yb", [128, Hh * Dh], BF16)
        yTo = sb("R_yTo", [128, Hh, 128], BF16)
        psT = ps("R_psT", [128, 2 * Hh, 128], BF16)
        ps_sc = ps("R_pssc", [128, Hh, 128], F32)
        ps_yi = ps("R_psyi", [128, Hh, Dh], F32)
        ps_yo = ps("R_psyo", [128, Hh, Dh], F32)

        R.dma("sp", "R_maskT", dict(out=maskT[:], in_=dr["ret_maskT"][:, :, :]), writes=["R_maskT"])
        R.dma("sp", "R_vec", dict(out=vec[:], in_=dr["ret_vec"][:, :, :]), writes=["R_vec"])
        for h in range(Hh):
            g = 1.0 - 2.0 ** (-5.0 - h)
            R.op("pool", [("memset", dict(ap=cdec[:, h, :], constant=float(g ** CH)))], writes=["R_cdec"])
        R.op("pool", [("memset", dict(ap=S[:], constant=0.0))], writes=["R_S"])
        R.op("pool", [("memset", dict(ap=S_bf[:], constant=0.0))], writes=["R_Sbf"])

        for t, (r0, n) in enumerate(cfg.tiles):
            b = t % 2
            R.dma("sp", f"R_pr{b}", dict(out=pr[b][:n], in_=P[r0:r0 + n, O_RQ:O_RQ + 4 * RET_W]), reads=[("P", t)], writes=[("R_pr", b)])
            R.dma("sp", f"R_rope{b}", dict(out=rope[b][:n], in_=dr["rope"][r0:r0 + n, :, :]), writes=[("R_rope", b)])
            prv = pr[b]
            for which, (src0, dst, ci, si, e1, e2) in enumerate(((0, qr, 0, 1, "dve", "pool"), (RET_W, kr, 2, 3, "pool", "dve"))):
                x = prv[:n, src0:src0 + RET_W].rearrange("p (h t d) -> p h t d", h=Hh, t=2)
                x1, x2 = x[:, :, 0, :], x[:, :, 1, :]
                cs_ = bc(rope[b][:n, ci:ci + 1, :], (n, Hh, 64))
                sn_ = bc(rope[b][:n, si:si + 1, :], (n, Hh, 64))
                o = dst[:n].rearrange("p h (t d) -> p h t d", t=2)
                T = tt[4 * which:4 * which + 4]
                tk = [("R_t", 4 * which + i) for i in range(4)]
                rd = [("R_pr", b), ("R_rope", b)]
                R.op(e1, [("tensor_tensor", dict(out=T[0][:n], in0=x1, in1=cs_, op=ALU.mult))], reads=rd, writes=[tk[0]])
                R.op(e2, [("tensor_tensor", dict(out=T[1][:n], in0=x2, in1=sn_, op=ALU.mult))], reads=rd, writes=[tk[1]])
                R.op(e1, [("tensor_tensor", dict(out=T[2][:n], in0=x1, in1=sn_, op=ALU.mult))], reads=rd, writes=[tk[2]])
                R.op(e2, [("tensor_tensor", dict(out=T[3][:n], in0=x2, in1=cs_, op=ALU.mult))], reads=rd, writes=[tk[3]])
                R.op(e1, [("tensor_tensor", dict(out=o[:, :, 0, :], in0=T[0][:n], in1=T[1][:n], op=ALU.subtract))], reads=[tk[0], tk[1]], writes=[("R_rot", which, 0)])
                R.op(e2, [("tensor_tensor", dict(out=o[:, :, 1, :], in0=T[2][:n], in1=T[3][:n], op=ALU.add))], reads=[tk[2], tk[3]], writes=[("R_rot", which, 1)])
            qk_keys = [("R_rot", w_, i_) for w_ in range(2) for i_ in range(2)]
            kcol = 0 if n == CH else 1
            R.op("dve", [("tensor_tensor", dict(out=kd[:n], in0=kr[:n], in1=bc(vec[:n, kcol, :].unsqueeze(2), (n, Hh, Dh)), op=ALU.mult))],
                 reads=qk_keys + ["R_vec"], writes=["R_kd"])
            R.op("act", [("activation", dict(out=v_bf[:n], in_=prv[:n, O_RV:O_RV + RET_W].rearrange("p (h d) -> p h d", h=Hh), func=AF.Copy))],
                 reads=[("R_pr", b)], writes=["R_vbf"])
            R.op("pe", [("transpose", dict(out=psT[:, h, :n], in_=qr[:n, h, :], identity=ident[:n, :n])) for h in range(Hh)] +
                       [("transpose", dict(out=psT[:, Hh + h, :n], in_=kr[:n, h, :], identity=ident[:n, :n])) for h in range(Hh)],
                 reads=qk_keys + ["ident"], writes=["R_psT"])
            R.op("dve", [("tensor_copy", dict(out=qkT[:, :, :n], in_=psT[:, :, :n]))], reads=["R_psT"], writes=["R_qkT"])
            R.op("pe", [("matmul", dict(out=ps_sc[:n, h, :n], lhsT=qkT[:, Hh + h, :n], rhs=qkT[:, h, :n], start=True, stop=True)) for h in range(Hh)],
                 reads=["R_qkT"], writes=["R_pssc"])
            R.op("dve", [("tensor_tensor", dict(out=sT[:n, :, :n], in0=ps_sc[:n, :, :n], in1=maskT[:n, :, :n], op=ALU.mult))],
                 reads=["R_pssc", "R_maskT"], writes=["R_sT"])
            R.op("pe", [("matmul", dict(out=ps_yi[:n, h, :], lhsT=sT[:n, h, :n], rhs=v_bf[:n, h, :], start=True, stop=True)) for h in range(Hh)],
                 reads=["R_sT", "R_vbf"], writes=["R_psyi"])
            R.op("pe", [("matmul", dict(out=ps_yo[:n, h, :], lhsT=qkT[:, h, :n], rhs=S_bf[:, h, :], start=True, stop=True)) for h in range(Hh)],
                 reads=["R_qkT", "R_Sbf"], writes=["R_psyo"])
            R.op("dve", [("tensor_tensor", dict(out=yo[:n], in0=ps_yo[:n], in1=bc(vec[:n, 2, :].unsqueeze(2), (n, Hh, Dh)), op=ALU.mult))],
                 reads=["R_psyo", "R_vec"], writes=["R_yo"])
            R.op("dve", [("tensor_tensor", dict(out=y[:n], in0=ps_yi[:n], in1=yo[:n], op=ALU.add))], reads=["R_psyi", "R_yo"], writes=["R_y"])
            R.op("pe", [("matmul", dict(out=ps_sc[:, h, :], lhsT=kd[:n, h, :], rhs=v_bf[:n, h, :], start=True, stop=True)) for h in range(Hh)],
                 reads=["R_kd", "R_vbf"], writes=["R_pssc"])
            R.op("dve", [("tensor_tensor", dict(out=S[:], in0=S[:], in1=cdec[:], op=ALU.mult))], reads=["R_cdec"], writes=["R_S"])
            R.op("dve", [("tensor_tensor", dict(out=S[:], in0=S[:], in1=ps_sc[:], op=ALU.add))], reads=["R_pssc"], writes=["R_S"])
            R.op("act", [("activation", dict(out=S_bf[:], in_=S[:], func=AF.Copy))], reads=["R_S"], writes=["R_Sbf"])
            R.op("dve", [("tensor_tensor", dict(out=ysq[:n], in0=y[:n], in1=y[:n], op=ALU.mult))], reads=["R_y"], writes=["R_ysq"])
            R.op("dve", [("tensor_reduce", dict(out=sm[0][:n], in_=ysq[:n], axis=AX.X, op=ALU.add))], reads=["R_ysq"], writes=[("R_sm", 0)])
            R.op("dve", [("tensor_scalar", dict(out=sm[1][:n], in0=sm[0][:n], scalar1=1.0 / Dh, scalar2=EPS, op0=ALU.mult, op1=ALU.add))],
                 reads=[("R_sm", 0)], writes=[("R_sm", 1)])
            R.op("act", [("activation", dict(out=sm[2][:n], in_=sm[1][:n], func=AF.Sqrt))], reads=[("R_sm", 1)], writes=[("R_sm", 2)])
            R.op("dve", [("reciprocal", dict(out=sm[3][:n], in_=sm[2][:n]))], reads=[("R_sm", 2)], writes=[("R_sm", 3)])
            R.op("act", [("activation", dict(out=sgl[:n], in_=prv[:n, O_RG:O_RG + RET_W].rearrange("p (h d) -> p h d", h=Hh), func=AF.Silu))],
                 reads=[("R_pr", b)], writes=["R_sg"])
            R.op("dve", [("tensor_tensor", dict(out=y[:n], in0=y[:n], in1=bc(sm[3][:n].unsqueeze(2), (n, Hh, Dh)), op=ALU.mult))],
                 reads=[("R_sm", 3)], writes=["R_y"])
            R.op("dve", [("tensor_tensor", dict(out=yb[:n].rearrange("p (h d) -> p h d", h=Hh), in0=y[:n], in1=sgl[:n], op=ALU.mult))],
                 reads=["R_y", "R_sg"], writes=["R_yb"])
            R.op("pe", [("transpose", dict(out=psT[:, h, :n], in_=yb[:n, h * 128:(h + 1) * 128], identity=ident[:n, :n])) for h in range(Hh)],
                 reads=["R_yb", "ident"], writes=["R_psT"])
            R.op("act", [("activation", dict(out=yTo[:, :, :n], in_=psT[:, 0:Hh, :n], func=AF.Copy))], reads=["R_psT"], writes=["R_yTo"])
            R.dma("sp", "R_yTo", dict(out=YT[0:RET_W, r0:r0 + n].rearrange("(k p) t -> p k t", p=128), in_=yTo[:, :, :n]),
                  reads=["R_yTo"], writes=[("YT", t)])
    R.barrier()


def phase_B_ssd(R, nc, cfg, li, dr, ident):
    P, YT = dr["P_ssd"], dr["YT"]
    O_Z, O_XBC, O_DT = 0, SSD_W, SSD_W + SSD_CONV
    NH, HP, G = SSD_H, SSD_P, SSD_G
    with ExitStack() as es:
        sb = lambda name, shape, dt: es.enter_context(nc.sbuf_tensor(uq(name), shape, dt))
        ps = lambda name, shape, dt: es.enter_context(nc.psum_tensor(uq(name), shape, dt))
        wconv = sb("S_wconv", [128, 4, SSD_CONV], F32)
        convb = sb("S_convb", [128, SSD_CONV], F32)
        normw = sb("S_normw", [128, SSD_W], F32)
        hv = sb("S_hv", [128, 3, NH], F32)
        tri = sb("S_tri", [128, 3, 128], F32)
        SLb = sb("S_SLb", [128, 128], BF16)
        ST = sb("S_ST", [128, NH, HP], F32)
        ST_bf = sb("S_STbf", [128, NH, HP], BF16)
        xs = [sb(f"S_xs{i}", [128, 4, 1024], F32) for i in range(2)]
        z = sb("S_z", [128, SSD_W], F32)
        dtr = sb("S_dtr", [128, NH], F32)
        xa = sb("S_xa", [128, SSD_CONV], F32)
        Bb = sb("S_Bb", [128, 512], BF16)
        Cb = sb("S_Cb", [128, 512], BF16)
        sm = [sb(f"S_sm{i}", [128, NH], F32) for i in range(8)]
        cst = sb("S_cst", [128, 2 * NH], F32)
        g4 = [sb(f"S_g4{i}", [128, G], F32) for i in range(4)]
        xdt = sb("S_xdt", [128, SSD_W], BF16)
        xdd = sb("S_xdd", [128, SSD_W], BF16)
        bcT = sb("S_bcT", [128, 8, 128], BF16)
        cbm = sb("S_cbm", [128, G, 128], F32)
        rseg = sb("S_rseg", [128, 16, 128], BF16)
        ed = sb("S_ed", [128, 16, 128], BF16)
        scT = sb("S_scT", [128, NH, 128], BF16)
        t1 = sb("S_t1", [128, SSD_W], F32)
        y = sb("S_y", [128, SSD_W], F32)
        yb = sb("S_yb", [128, SSD_W], BF16)
        yTo = sb("S_yTo", [128, 16, 128], BF16)
        psT = ps("S_psT", [128, 8, 128], BF16)
        ps_cb = ps("S_pscb", [128, G, 128], F32)
        psA = ps("S_psA", [128, SSD_W], F32)
        psB = ps("S_psB", [128, 1024], F32)

        R.dma("sp", "S_wconv", dict(out=wconv[:].rearrange("p k c -> p (k c)"), in_=dr["ssd_conv_w"][li:li + 1].rearrange("o k c -> o (k c)").partition_broadcast(128)), writes=["S_wconv"])
        R.dma("sp", "S_convb", dict(out=convb[:], in_=dr["ssd_conv_b"][li:li + 1, :].partition_broadcast(128)), writes=["S_convb"])
        R.dma("sp", "S_normw", dict(out=normw[:], in_=dr["ssd_norm_w"][li:li + 1, :].partition_broadcast(128)), writes=["S_normw"])
        for i, nm in enumerate(("ssd_dt_bias", "ssd_a_log", "ssd_d")):
            R.dma("sp", "S_hv", dict(out=hv[:, i, :], in_=dr[nm][li:li + 1, :].partition_broadcast(128)), writes=["S_hv"])
        R.dma("sp", "S_tri", dict(out=tri[:], in_=dr["tri"][:, :, :]), writes=["S_tri"])
        R.op("act", [("activation", dict(out=hv[:, 1, :], in_=hv[:, 1, :], func=AF.Exp))], reads=["S_hv"], writes=["S_hv"])
        R.op("dve", [("tensor_scalar", dict(out=hv[:, 1, :], in0=hv[:, 1, :], scalar1=-1.0, scalar2=None, op0=ALU.mult))], reads=["S_hv"], writes=["S_hv"])
        R.op("dve", [("tensor_copy", dict(out=SLb[:], in_=tri[:, 1, :]))], reads=["S_tri"], writes=["S_SLb"])
        R.op("pool", [("memset", dict(ap=ST[:], constant=0.0))], writes=["S_ST"])
        R.op("pool", [("memset", dict(ap=ST_bf[:], constant=0.0))], writes=["S_STbf"])
        U = tri[:, 0, :]
        ONES = tri[:, 2, :]
        xi = 0
        for t, (r0, n) in enumerate(cfg.tiles):
            pk = [("P", t)] + ([("P", t - 1)] if t > 0 else [])
            R.dma("sp", "S_z", dict(out=z[:n], in_=P[r0:r0 + n, O_Z:O_Z + SSD_W]), reads=pk, writes=["S_z"])
            R.dma("sp", "S_dtr", dict(out=dtr[:n], in_=P[r0:r0 + n, O_DT:O_DT + NH]), reads=pk, writes=["S_dtr"])
            for blk in range(3):
                xb = xs[xi % 2]
                xk = ("S_xs", xi % 2)
                xi += 1
                c0 = O_XBC + blk * 1024
                if t == 0:
                    R.op("pool", [("memset", dict(ap=xb[:n].rearrange("p k c -> p (k c)"), constant=0.0))], writes=[xk])
                for k in range(4):
                    sh = 3 - k
                    lo = max(0, sh - r0)
                    R.dma("sp", f"S_xs{(xi - 1) % 2}", dict(out=xb[lo:n, k, :], in_=P[r0 - sh + lo:r0 - sh + n, c0:c0 + 1024]), reads=pk, writes=[xk])
                e = ["pool", "dve"]
                for k in range(4):
                    R.op(e[k % 2], [("tensor_tensor", dict(out=xb[:n, k, :], in0=xb[:n, k, :], in1=wconv[:n, k, blk * 1024:(blk + 1) * 1024], op=ALU.mult))],
                         reads=["S_wconv"], writes=[xk])
                R.op("dve", [("tensor_tensor", dict(out=xb[:n, 0, :], in0=xb[:n, 0, :], in1=xb[:n, 1, :], op=ALU.add))], writes=[xk])
                R.op("dve", [("tensor_tensor", dict(out=xb[:n, 2, :], in0=xb[:n, 2, :], in1=xb[:n, 3, :], op=ALU.add))], writes=[xk])
                R.op("dve", [("tensor_tensor", dict(out=xb[:n, 0, :], in0=xb[:n, 0, :], in1=convb[:n, blk * 1024:(blk + 1) * 1024], op=ALU.add))], reads=["S_convb"], writes=[xk])
                R.op("dve", [("tensor_tensor", dict(out=xb[:n, 0, :], in0=xb[:n, 0, :], in1=xb[:n, 2, :], op=ALU.add))], writes=[xk])
                R.op("act", [("activation", dict(out=xa[:n, blk * 1024:(blk + 1) * 1024], in_=xb[:n, 0, :], func=AF.Silu))], reads=[xk], writes=[("S_xa", blk)])
            xak = [("S_xa", i) for i in range(3)]
            R.op("act", [("activation", dict(out=Bb[:n], in_=xa[:n, 2048:2560], func=AF.Copy))], reads=xak, writes=["S_Bb"])
            R.op("dve", [("tensor_copy", dict(out=Cb[:n], in_=xa[:n, 2560:3072]))], reads=xak, writes=["S_Cb"])
            R.op("dve", [("tensor_tensor", dict(out=sm[0][:n], in0=dtr[:n], in1=hv[:n, 0, :], op=ALU.add))], reads=["S_dtr", "S_hv"], writes=[("S_sm", 0)])
            R.op("act", [("activation", dict(out=sm[1][:n], in_=sm[0][:n], func=AF.Exp))], reads=[("S_sm", 0)], writes=[("S_sm", 1)])
            R.op("act", [("activation", dict(out=sm[2][:n], in_=sm[1][:n], func=AF.Ln, bias=1.0))], reads=[("S_sm", 1)], writes=[("S_sm", 2)])
            R.op("dve", [("tensor_tensor", dict(out=sm[3][:n], in0=sm[2][:n], in1=hv[:n, 1, :], op=ALU.mult))], reads=[("S_sm", 2), "S_hv"], writes=[("S_sm", 3)])
            R.op("pe", [("matmul", dict(out=psB[:n, 0:NH], lhsT=U[:n, :n], rhs=sm[3][:n], start=True, stop=True)),
                        ("matmul", dict(out=psB[:n, NH:2 * NH], lhsT=ONES[:n, :n], rhs=sm[3][:n], start=True, stop=True))],
                 reads=["S_tri", ("S_sm", 3)], writes=["S_psB"])
            R.op("act", [("activation", dict(out=cst[:n], in_=psB[:n, 0:2 * NH], func=AF.Copy))], reads=["S_psB"], writes=["S_cst"])
            R.op("act", [("activation", dict(out=sm[4][:n], in_=cst[:n, 0:NH], func=AF.Exp))], reads=["S_cst"], writes=[("S_sm", 4)])
            R.op("dve", [("tensor_tensor", dict(out=sm[5][:n], in0=cst[:n, NH:2 * NH], in1=cst[:n, 0:NH], op=ALU.subtract))], reads=["S_cst"], writes=[("S_sm", 5)])
            R.op("act", [("activation", dict(out=sm[5][:n], in_=sm[5][:n], func=AF.Exp))], reads=[("S_sm", 5)], writes=[("S_sm", 5)])
            R.op("act", [("activation", dict(out=sm[6][:n], in_=cst[:n, NH:2 * NH], func=AF.Exp))], reads=["S_cst"], writes=[("S_sm", 6)])
            R.op("dve", [("tensor_tensor", dict(out=sm[7][:n], in0=sm[2][:n], in1=sm[5][:n], op=ALU.mult))], reads=[("S_sm", 2), ("S_sm", 5)], writes=[("S_sm", 7)])
            x3 = xa[:n, 0:SSD_W].rearrange("p (h d) -> p h d", h=NH)
            R.op("dve", [("tensor_tensor", dict(out=xdt[:n].rearrange("p (h d) -> p h d", h=NH), in0=x3, in1=bc(sm[2][:n].unsqueeze(2), (n, NH, HP)), op=ALU.mult))],
                 reads=xak + [("S_sm", 2)], writes=["S_xdt"])
            R.op("dve", [("tensor_tensor", dict(out=xdd[:n].rearrange("p (h d) -> p h d", h=NH), in0=x3, in1=bc(sm[7][:n].unsqueeze(2), (n, NH, HP)), op=ALU.mult))],
                 reads=xak + [("S_sm", 7)], writes=["S_xdd"])
            R.op("pe", [("transpose", dict(out=psT[:, g, :n], in_=Bb[:n, g * 128:(g + 1) * 128], identity=ident[:n, :n])) for g in range(G)] +
                       [("transpose", dict(out=psT[:, G + g, :n], in_=Cb[:n, g * 128:(g + 1) * 128], identity=ident[:n, :n])) for g in range(G)],
                 reads=["S_Bb", "S_Cb", "ident"], writes=["S_psT"])
            R.op("dve", [("tensor_copy", dict(out=bcT[:, :, :n], in_=psT[:, :, :n]))], reads=["S_psT"], writes=["S_bcT"])
            R.op("pe", [("matmul", dict(out=ps_cb[:n, g, :n], lhsT=bcT[:, g, :n], rhs=bcT[:, G + g, :n], start=True, stop=True)) for g in range(G)],
                 reads=["S_bcT"], writes=["S_pscb"])
            R.op("dve", [("tensor_tensor", dict(out=cbm[:n, :, :n], in0=ps_cb[:n, :, :n], in1=bc(U[:n, :n].unsqueeze(1), (n, G, n)), op=ALU.mult))],
                 reads=["S_pscb", "S_tri"], writes=["S_cbm"])
            for half in range(2):
                h0 = half * 16
                R.op("dve", [("tensor_tensor", dict(out=rseg[:n, :, :n], in0=bc(sm[3][:n, h0:h0 + 16].unsqueeze(2), (n, 16, n)), in1=bc(U[:n, :n].unsqueeze(1), (n, 16, n)), op=ALU.mult))],
                     reads=[("S_sm", 3), "S_tri"], writes=["S_rseg"])
                R.op("pe", [("matmul", dict(out=psA[:n, q * 512:q * 512 + 4 * n].rearrange("p (h i) -> p h i", h=4), lhsT=SLb[:n, :n], rhs=rseg[:n, 4 * q:4 * q + 4, :n], start=True, stop=True)) for q in range(4)],
                     reads=["S_rseg", "S_SLb"], writes=["S_psA"])
                for q in range(4):
                    R.op("act", [("activation", dict(out=ed[:n, 4 * q:4 * q + 4, :n], in_=psA[:n, q * 512:q * 512 + 4 * n].rearrange("p (h i) -> p h i", h=4), func=AF.Exp))],
                         reads=["S_psA"], writes=[("S_ed", q)])
                R.op("dve", [("tensor_tensor", dict(out=scT[:n, h0:h0 + 16, :n].rearrange("p (g r) i -> p g r i", g=2),
                                                    in0=ed[:n, :, :n].rearrange("p (g r) i -> p g r i", g=2),
                                                    in1=bc(cbm[:n, 2 * half:2 * half + 2, :n].unsqueeze(2), (n, 2, 8, n)), op=ALU.mult))],
                     reads=[("S_ed", q) for q in range(4)] + ["S_cbm"], writes=[("S_scT", half)])
            R.op("pe", [("matmul", dict(out=psA[:n, h * HP:(h + 1) * HP], lhsT=scT[:n, h, :n], rhs=xdt[:n, h * HP:(h + 1) * HP], start=True, stop=True)) for h in range(NH)],
                 reads=[("S_scT", 0), ("S_scT", 1), "S_xdt"], writes=["S_psA"])
            for half in range(2):
                R.op("pe", [("matmul", dict(out=psB[:n, gg * 512:(gg + 1) * 512], lhsT=bcT[:, G + 2 * half + gg, :n],
                                            rhs=ST_bf[:, (2 * half + gg) * 8:(2 * half + gg + 1) * 8, :].rearrange("p h d -> p (h d)"), start=True, stop=True)) for gg in range(2)],
                     reads=["S_bcT", "S_STbf"], writes=["S_psB"])
                R.op("dve", [("tensor_tensor", dict(out=t1[:n, half * 1024:(half + 1) * 1024].rearrange("p (h d) -> p h d", h=16),
                                                    in0=psB[:n, :].rearrange("p (h d) -> p h d", h=16),
                                                    in1=bc(sm[4][:n, half * 16:(half + 1) * 16].unsqueeze(2), (n, 16, HP)), op=ALU.mult))],
                     reads=["S_psB", ("S_sm", 4)], writes=[("S_t1", half)])
            R.op("dve", [("tensor_tensor", dict(out=y[:n], in0=psA[:n, :], in1=t1[:n], op=ALU.add))], reads=["S_psA", ("S_t1", 0), ("S_t1", 1)], writes=["S_y"])
            R.op("pe", [("matmul", dict(out=psA[:, g * 512:(g + 1) * 512], lhsT=Bb[:n, g * 128:(g + 1) * 128], rhs=xdd[:n, g * 512:(g + 1) * 512], start=True, stop=True)) for g in range(G)],
                 reads=["S_Bb", "S_xdd"], writes=["S_psA"])
            if n == CH:
                R.op("dve", [("tensor_tensor", dict(out=ST[:], in0=ST[:], in1=bc(sm[6][:, :].unsqueeze(2), (128, NH, HP)), op=ALU.mult))],
                     reads=[("S_sm", 6)], writes=["S_ST"])
            R.op("dve", [("tensor_tensor", dict(out=ST[:].rearrange("p h d -> p (h d)"), in0=ST[:].rearrange("p h d -> p (h d)"), in1=psA[:, :], op=ALU.add))],
                 reads=["S_psA"], writes=["S_ST"])
            R.op("act", [("activation", dict(out=ST_bf[:], in_=ST[:], func=AF.Copy))], reads=["S_ST"], writes=["S_STbf"])
            R.op("dve", [("tensor_tensor", dict(out=x3, in0=x3, in1=bc(hv[:n, 2, :].unsqueeze(2), (n, NH, HP)), op=ALU.mult))], reads=["S_hv", "S_xdt", "S_xdd"], writes=xak)
            R.op("dve", [("tensor_tensor", dict(out=y[:n], in0=y[:n], in1=xa[:n, 0:SSD_W], op=ALU.add))], reads=xak, writes=["S_y"])
            R.op("act", [("activation", dict(out=z[:n], in_=z[:n], func=AF.Silu))], writes=["S_z"])
            R.op("dve", [("tensor_tensor", dict(out=y[:n], in0=y[:n], in1=z[:n], op=ALU.mult))], reads=["S_z"], writes=["S_y"])
            R.op("dve", [("tensor_tensor", dict(out=t1[:n], in0=y[:n], in1=y[:n], op=ALU.mult))], reads=["S_y"], writes=[("S_t1", 0), ("S_t1", 1)])
            R.op("dve", [("tensor_reduce", dict(out=g4[0][:n], in_=t1[:n].rearrange("p (g d) -> p g d", g=G), axis=AX.X, op=ALU.add))], reads=[("S_t1", 0), ("S_t1", 1)], writes=[("S_g4", 0)])
            R.op("dve", [("tensor_scalar", dict(out=g4[1][:n], in0=g4[0][:n], scalar1=1.0 / (SSD_W // G), scalar2=EPS, op0=ALU.mult, op1=ALU.add))], reads=[("S_g4", 0)], writes=[("S_g4", 1)])
            R.op("act", [("activation", dict(out=g4[2][:n], in_=g4[1][:n], func=AF.Sqrt))], reads=[("S_g4", 1)], writes=[("S_g4", 2)])
            R.op("dve", [("reciprocal", dict(out=g4[3][:n], in_=g4[2][:n]))], reads=[("S_g4", 2)], writes=[("S_g4", 3)])
            R.op("dve", [("tensor_tensor", dict(out=y[:n].rearrange("p (g d) -> p g d", g=G), in0=y[:n].rearrange("p (g d) -> p g d", g=G), in1=bc(g4[3][:n].unsqueeze(2), (n, G, SSD_W // G)), op=ALU.mult))],
                 reads=[("S_g4", 3)], writes=["S_y"])
            R.op("dve", [("tensor_tensor", dict(out=yb[:n], in0=y[:n], in1=normw[:n], op=ALU.mult))], reads=["S_y", "S_normw"], writes=["S_yb"])
            for half in range(2):
                R.op("pe", [("transpose", dict(out=psT[:, k, :n], in_=yb[:n, (half * 8 + k) * 128:(half * 8 + k + 1) * 128], identity=ident[:n, :n])) for k in range(8)],
                     reads=["S_yb", "ident"], writes=["S_psT"])
                R.op("act", [("activation", dict(out=yTo[:, half * 8:(half + 1) * 8, :n], in_=psT[:, :, :n], func=AF.Copy))], reads=["S_psT"], writes=[("S_yTo", half)])
            R.dma("sp", "S_yTo", dict(out=YT[2048:4096, r0:r0 + n].rearrange("(k p) t -> p k t", p=128), in_=yTo[:, :, :n]),
                  reads=[("S_yTo", 0), ("S_yTo", 1)], writes=[("YT", t)])
    R.barrier()


def phase_B_rwkv(R, nc, cfg, li, dr, ident, ident_f):
    P, YT = dr["P_rw"], dr["YT"]
    O_RW = 0
    NH, HN, W = RW_H, RW_N, RW_W
    C0 = float(np.exp(-0.5))
    with ExitStack() as es:
        sb = lambda name, shape, dt: es.enter_context(nc.sbuf_tensor(uq(name), shape, dt))
        ps = lambda name, shape, dt: es.enter_context(nc.psum_tensor(uq(name), shape, dt))
        mu = sb("W_mu", [128, RW_COLS], F32)
        vecs = sb("W_vecs", [128, 5, W], F32)
        w2a = sb("W_w2a", [128, W], F32)
        a2a = sb("W_a2a", [128, W], F32)
        g2 = sb("W_g2", [128, 2, W], F32)
        msk = sb("W_msk", [128, 5, 128], F32)
        ones = sb("W_ones", [128, 128], F32)
        Tst = sb("W_T", [128, 8, HN], F32)
        T_bf = sb("W_Tbf", [128, 8, HN], BF16)
        cols = sb("W_cols", [128, RW_COLS], F32)
        prev = sb("W_prev", [128, RW_COLS], F32)
        lin = sb("W_lin", [128, 288], F32)
        loT = sb("W_loT", [128, 4, 128], F32)
        F = [sb(f"W_F{i}", [128, W], F32) for i in range(10)] + [prev[:, 0:W], prev[:, W:2 * W]]
        Bq = [sb(f"W_B{i}", [128, W], BF16) for i in range(7)]
        fT2 = sb("W_fT2", [128, 2, 8, 128], BF16)
        rP = sb("W_rP", [128, NH, 128], BF16)
        kP = sb("W_kP", [128, NH, 128], BF16)
        Am = [sb(f"W_A{i}", [128, NH, 128], BF16) for i in range(3)]
        Pm = [sb(f"W_P{i}", [128, NH, 128], BF16) for i in range(2)]
        PTm = [sb(f"W_PT{i}", [128, NH, 128], BF16) for i in range(2)]
        MT = sb("W_MT", [128, NH, 128], BF16)
        s16 = [sb(f"W_s16_{i}", [128, NH], F32) for i in range(6)]
        WC = sb("W_WC", [128, 8], F32)
        yTo = sb("W_yTo", [128, 8, 128], BF16)
        X0 = ps("W_X0", [128, 512], F32)
        Y2 = ps("W_Y2", [128, W], F32)
        Z2 = ps("W_Z2", [128, W], F32)
        QQ = ps("W_QQ", [128, 8, 128], F32)

        def bcv(i, n):
            return vecs[:n, i, :]

        R.dma("sp", "W_mu", dict(out=mu[:], in_=dr["rwkv_mu"][li:li + 1, :].partition_broadcast(128)), writes=["W_mu"])
        for i, nm in enumerate(("rwkv_k_k", "rwkv_k_a", "rwkv_ln_w", "rwkv_ln_b")):
            R.dma("sp", "W_vecs", dict(out=vecs[:, i, :], in_=dr[nm][li:li + 1, :].partition_broadcast(128)), writes=["W_vecs"])
        R.dma("sp", "W_vecs", dict(out=vecs[:, 4, :], in_=dr["rwkv_r_k"][li:li + 1].rearrange("o h d -> o (h d)").partition_broadcast(128)), writes=["W_vecs"])
        R.dma("sp", "W_w2a", dict(out=w2a[0:64, :], in_=dr["rwkv_w2"][li, :, :]), writes=["W_w2a"])
        R.dma("sp", "W_w2a", dict(out=w2a[64:65, :], in_=dr["rwkv_w0"][li:li + 1, :]), writes=["W_w2a"])
        R.dma("sp", "W_a2a", dict(out=a2a[0:64, :], in_=dr["rwkv_a2"][li, :, :]), writes=["W_a2a"])
        R.dma("sp", "W_a2a", dict(out=a2a[64:65, :], in_=dr["rwkv_a0"][li:li + 1, :]), writes=["W_a2a"])
        R.dma("sp", "W_g2", dict(out=g2[:, 0, :], in_=dr["rwkv_g2"][li, 0:128, :]), writes=["W_g2"])
        R.dma("sp", "W_g2", dict(out=g2[0:32, 1, :], in_=dr["rwkv_g2"][li, 128:160, :]), writes=["W_g2"])
        R.dma("sp", "W_msk", dict(out=msk[:], in_=dr["rw_msk"][:, :, :]), writes=["W_msk"])
        R.op("pool", [("memset", dict(ap=ones[:], constant=1.0))], writes=["W_ones"])
        R.op("pool", [("memset", dict(ap=rP[:].rearrange("p a b -> p (a b)"), constant=0.0))], writes=[("W_fT", 0)])
        R.op("pool", [("memset", dict(ap=kP[:].rearrange("p a b -> p (a b)"), constant=0.0))], writes=[("W_fT", 0)])
        R.op("pool", [("memset", dict(ap=loT[:].rearrange("p a b -> p (a b)"), constant=1.0))], writes=["W_loT"])
        R.op("pool", [("memset", dict(ap=Tst[:].rearrange("p a b -> p (a b)"), constant=0.0))], writes=["W_T"])
        R.op("pool", [("memset", dict(ap=T_bf[:].rearrange("p a b -> p (a b)"), constant=0.0))], writes=["W_Tbf"])
        MU, MS, MSn, MLn, MI = (msk[:, i, :] for i in range(5))
        qi = 0
        for t, (r0, n) in enumerate(cfg.tiles):
            pk = [("P", t)] + ([("P", t - 1)] if t > 0 else [])
            R.countdown = None
            FK = [("W_F", i) for i in range(12)]
            R.dma("sp", "W_cols", dict(out=cols[:n], in_=P[r0:r0 + n, O_RW:O_RW + RW_COLS]), reads=pk, writes=["W_cols"])
            if t == 0:
                R.op("pool", [("memset", dict(ap=prev[:n], constant=0.0))], writes=["W_prev", FK[10], FK[11]])
                R.dma("sp", "W_prev", dict(out=prev[1:n], in_=P[0:n - 1, O_RW:O_RW + RW_COLS]), reads=pk, writes=["W_prev", FK[10], FK[11]])
            else:
                R.dma("sp", "W_prev", dict(out=prev[:n], in_=P[r0 - 1:r0 + n - 1, O_RW:O_RW + RW_COLS]), reads=pk, writes=["W_prev", FK[10], FK[11]])
            R.op("pool", [("tensor_tensor", dict(out=prev[:n], in0=prev[:n], in1=cols[:n], op=ALU.subtract))], reads=["W_cols"], writes=["W_prev"])
            R.op("pool", [("tensor_tensor", dict(out=prev[:n], in0=prev[:n], in1=mu[:n], op=ALU.mult))], reads=["W_mu"], writes=["W_prev"])
            R.op("dve", [("tensor_tensor", dict(out=cols[:n], in0=cols[:n], in1=prev[:n], op=ALU.add))], reads=["W_prev"], writes=["W_cols"])
            if cfg.stop == "W1":
                continue
            r_, k_, v_ = cols[:n, 0:W], cols[:n, W:2 * W], cols[:n, 2 * W:3 * W]
            R.op("act", [("activation", dict(out=lin[:n, 0:64], in_=cols[:n, 3072:3136], func=AF.Tanh))], reads=["W_cols"], writes=[("W_lin", 0)])
            R.op("act", [("activation", dict(out=lin[:n, 64:128], in_=cols[:n, 3136:3200], func=AF.Copy))], reads=["W_cols"], writes=[("W_lin", 1)])
            R.op("act", [("activation", dict(out=lin[:n, 128:288], in_=cols[:n, 3200:3360], func=AF.Sigmoid))], reads=["W_cols"], writes=[("W_lin", 2)])
            X0t = X0[:].rearrange("p (a b) -> p a b", a=4)
            R.op("pe", [("transpose", dict(out=X0t[0:64, 0, :n], in_=lin[:n, 0:64], identity=ident_f[:n, :n])),
                        ("transpose", dict(out=X0t[0:64, 1, :n], in_=lin[:n, 64:128], identity=ident_f[:n, :n])),
                        ("transpose", dict(out=X0t[:, 2, :n], in_=lin[:n, 128:256], identity=ident_f[:n, :n])),
                        ("transpose", dict(out=X0t[0:32, 3, :n], in_=lin[:n, 256:288], identity=ident_f[:n, :n]))],
                 reads=[("W_lin", 0), ("W_lin", 1), ("W_lin", 2), "ident_f"], writes=["W_X0"])
            R.op("dve", [("tensor_copy", dict(out=loT[0:64, 0:2, :n], in_=X0t[0:64, 0:2, :n]))], reads=["W_X0"], writes=["W_loT"])
            R.op("dve", [("tensor_copy", dict(out=loT[:, 2, :n], in_=X0t[:, 2, :n]))], reads=["W_X0"], writes=["W_loT"])
            R.op("dve", [("tensor_copy", dict(out=loT[0:32, 3, :n], in_=X0t[0:32, 3, :n]))], reads=["W_X0"], writes=["W_loT"])
            if cfg.stop == "W2":
                continue
            e_, a_, g_ = F[0], F[1], F[2]
            R.op("pe", [("matmul", dict(out=Y2[:n, hh * 512:(hh + 1) * 512], lhsT=loT[0:65, 0, :n], rhs=w2a[0:65, hh * 512:(hh + 1) * 512], start=True, stop=True)) for hh in range(2)],
                 reads=["W_loT", "W_w2a"], writes=["W_Y2"])
            R.op("act", [("activation", dict(out=e_[:n], in_=Y2[:n, :], func=AF.Sigmoid))], reads=["W_Y2"], writes=[FK[0]])
            R.op("pe", [("matmul", dict(out=Z2[:n, hh * 512:(hh + 1) * 512], lhsT=loT[0:65, 1, :n], rhs=a2a[0:65, hh * 512:(hh + 1) * 512], start=True, stop=True)) for hh in range(2)],
                 reads=["W_loT", "W_a2a"], writes=["W_Z2"])
            R.op("act", [("activation", dict(out=a_[:n], in_=Z2[:n, :], func=AF.Sigmoid))], reads=["W_Z2"], writes=[FK[1]])
            R.op("pe", [m for hh in range(2) for m in (
                        ("matmul", dict(out=Y2[:n, hh * 512:(hh + 1) * 512], lhsT=loT[:, 2, :n], rhs=g2[:, 0, hh * 512:(hh + 1) * 512], start=True, stop=False)),
                        ("matmul", dict(out=Y2[:n, hh * 512:(hh + 1) * 512], lhsT=loT[0:32, 3, :n], rhs=g2[0:32, 1, hh * 512:(hh + 1) * 512], start=False, stop=True)))],
                 reads=["W_loT", "W_g2"], writes=["W_Y2"])
            R.op("act", [("activation", dict(out=g_[:n], in_=Y2[:n, :], func=AF.Copy))], reads=["W_Y2"], writes=[FK[2]])
            if cfg.stop == "W3":
                continue
            if cfg.stop.startswith("CUT"):
                R.countdown = int(cfg.stop[3:])
            kk_, k2_, b_, tmp_ = F[7], F[8], F[9], F[10]
            R.op("pool", [("tensor_tensor", dict(out=kk_[:n], in0=k_, in1=bcv(0, n), op=ALU.mult))], reads=["W_cols", "W_vecs"], writes=[FK[7]])
            R.op("pool", [("tensor_tensor", dict(out=tmp_[:n], in0=kk_[:n], in1=kk_[:n], op=ALU.mult))], reads=[FK[7]], writes=[FK[10]])
            R.op("dve", [("tensor_reduce", dict(out=s16[0][:n], in_=tmp_[:n].rearrange("p (h d) -> p h d", h=NH), axis=AX.X, op=ALU.add))], reads=[FK[10]], writes=[("W_s16", 0)])
            R.op("act", [("activation", dict(out=s16[0][:n], in_=s16[0][:n], func=AF.Sqrt))], reads=[("W_s16", 0)], writes=[("W_s16", 0)])
            R.op("dve", [("tensor_scalar", dict(out=s16[0][:n], in0=s16[0][:n], scalar1=1e-12, scalar2=None, op0=ALU.max))], reads=[("W_s16", 0)], writes=[("W_s16", 0)])
            R.op("dve", [("reciprocal", dict(out=s16[1][:n], in_=s16[0][:n]))], reads=[("W_s16", 0)], writes=[("W_s16", 1)])
            R.op("dve", [("tensor_tensor", dict(out=kk_[:n].rearrange("p (h d) -> p h d", h=NH), in0=kk_[:n].rearrange("p (h d) -> p h d", h=NH), in1=bc(s16[1][:n].unsqueeze(2), (n, NH, HN)), op=ALU.mult))],
                 reads=[("W_s16", 1)], writes=[FK[7]])
            R.op("dve", [("scalar_tensor_tensor", dict(out=tmp_[:n], in0=a_[:n], scalar=-1.0, in1=bcv(1, n), op0=ALU.add, op1=ALU.mult))], reads=[FK[1], "W_vecs"], writes=[FK[10]])
            R.op("dve", [("tensor_scalar", dict(out=tmp_[:n], in0=tmp_[:n], scalar1=1.0, scalar2=None, op0=ALU.add))], reads=[FK[10]], writes=[FK[10]])
            R.op("pool", [("tensor_tensor", dict(out=k2_[:n], in0=k_, in1=tmp_[:n], op=ALU.mult))], reads=["W_cols", FK[10]], writes=[FK[8]])
            R.op("pool", [("tensor_tensor", dict(out=b_[:n], in0=kk_[:n], in1=a_[:n], op=ALU.mult))], reads=[FK[7], FK[1]], writes=[FK[9]])
            bo_ = F[11]
            R.op("pool", [("tensor_tensor", dict(out=tmp_[:n], in0=r_, in1=k2_[:n], op=ALU.mult))], reads=["W_cols", FK[8]], writes=[FK[10]])
            R.op("pool", [("tensor_tensor", dict(out=tmp_[:n], in0=tmp_[:n], in1=bcv(4, n), op=ALU.mult))], reads=["W_vecs"], writes=[FK[10]])
            R.op("dve", [("tensor_reduce", dict(out=s16[2][:n], in_=tmp_[:n].rearrange("p (h d) -> p h d", h=NH), axis=AX.X, op=ALU.add))], reads=[FK[10]], writes=[("W_s16", 2)])
            R.op("dve", [("tensor_tensor", dict(out=bo_[:n].rearrange("p (h d) -> p h d", h=NH), in0=v_.rearrange("p (h d) -> p h d", h=NH), in1=bc(s16[2][:n].unsqueeze(2), (n, NH, HN)), op=ALU.mult))],
                 reads=["W_cols", ("W_s16", 2)], writes=[FK[11]])
            if cfg.stop == "W4":
                continue
            R.op("pe", [("matmul", dict(out=Y2[:n, hh * 512:(hh + 1) * 512], lhsT=MU[:n, :n], rhs=e_[:n, hh * 512:(hh + 1) * 512], start=True, stop=True)) for hh in range(2)],
                 reads=["W_msk", FK[0]], writes=["W_Y2"])
            R.op("pe", [("matmul", dict(out=Z2[:n, hh * 512:(hh + 1) * 512], lhsT=ones[:n, :n], rhs=e_[:n, hh * 512:(hh + 1) * 512], start=True, stop=True)) for hh in range(2)],
                 reads=["W_ones", FK[0]], writes=["W_Z2"])
            R.op("pe", [("matmul", dict(out=X0[:, hp:hp + 1], lhsT=e_[:n, hp * 128:(hp + 1) * 128], rhs=ones[:n, 0:1], start=True, stop=True)) for hp in range(8)],
                 reads=["W_ones", FK[0]], writes=["W_X0"])
            R.op("act", [("activation", dict(out=WC[:], in_=X0[:, 0:8], func=AF.Exp, scale=-C0))], reads=["W_X0"], writes=["W_WC"])
            Wt, Winv, Wprev, Wrel = F[3], F[4], F[5], F[6]
            R.op("act", [("activation", dict(out=Wt[:n], in_=Y2[:n, :], func=AF.Exp, scale=-C0))], reads=["W_Y2"], writes=[FK[3]])
            R.op("act", [("activation", dict(out=Winv[:n], in_=Y2[:n, :], func=AF.Exp, scale=C0))], reads=["W_Y2"], writes=[FK[4]])
            R.op("dve", [("tensor_tensor", dict(out=Wprev[:n], in0=e_[:n], in1=Y2[:n, :], op=ALU.subtract))], reads=["W_Y2", FK[0]], writes=[FK[5]])
            R.op("act", [("activation", dict(out=Wprev[:n], in_=Wprev[:n], func=AF.Exp, scale=C0))], reads=[FK[5]], writes=[FK[5]])
            R.op("act", [("activation", dict(out=Wrel[:n], in_=Z2[:n, :], func=AF.Exp, scale=-C0))], reads=["W_Z2"], writes=[FK[6]])
            R.op("pool", [("tensor_tensor", dict(out=Wrel[:n], in0=Wrel[:n], in1=Winv[:n], op=ALU.mult))], reads=[FK[4]], writes=[FK[6]])
            if cfg.stop == "W5":
                continue
            BK = [("W_B", i) for i in range(7)]
            rt, kkt, kh, bh, kbar, bbar, v_bf = Bq
            specs = ((rt, r_, Wt, "W_cols", FK[3]), (kkt, kk_[:n], Wprev, FK[7], FK[5]), (kh, k2_[:n], Winv, FK[8], FK[4]), (bh, b_[:n], Winv, FK[9], FK[4]),
                     (kbar, k2_[:n], Wrel, FK[8], FK[6]), (bbar, b_[:n], Wrel, FK[9], FK[6]))
            for i, (dst, src, wgt, k1, k2k) in enumerate(specs):
                R.op("pool" if i % 2 else "dve", [("tensor_tensor", dict(out=dst[:n], in0=src, in1=wgt[:n], op=ALU.mult))], reads=[k1, k2k], writes=[BK[i]])
            R.op("act", [("activation", dict(out=v_bf[:n], in_=v_, func=AF.Copy))], reads=["W_cols"], writes=[BK[6]])
            if cfg.stop == "W6":
                continue
            Z2b = Z2[:].bitcast(BF16).rearrange("p (k t) -> p k t", t=128)
            for half in range(2):
                srcs = (rt, kkt) if half == 0 else (kh, bh)
                R.op("pe", [("transpose", dict(out=Z2b[:, qq * 8 + hp, :n], in_=srcs[qq][:n, hp * 128:(hp + 1) * 128], identity=ident[:n, :n])) for qq in range(2) for hp in range(8)],
                     reads=[BK[2 * half], BK[2 * half + 1], "ident"], writes=["W_Z2"])
                if half == 0:
                    for qq, dstp in enumerate((rP, kP)):
                        d4 = dstp[:].rearrange("p (a b) t -> p a b t", b=2)
                        R.op("dve", [("tensor_copy", dict(out=d4[0:64, :, 0, :n], in_=Z2b[0:64, qq * 8:(qq + 1) * 8, :n]))], reads=["W_Z2"], writes=[("W_fT", 0)])
                        R.op("act", [("activation", dict(out=d4[64:128, :, 1, :n], in_=Z2b[64:128, qq * 8:(qq + 1) * 8, :n], func=AF.Copy))], reads=["W_Z2"], writes=[("W_fT", 0)])
                else:
                    R.op("dve", [("tensor_copy", dict(out=fT2[:, :, :, :n], in_=Z2b[:, :, :n].rearrange("p (a b) t -> p a b t", a=2)))], reads=["W_Z2"], writes=[("W_fT", 1)])
            fk = [("W_fT", 0), ("W_fT", 1)]

            def opnd(q, h):
                if q == 0:
                    return rP[:, h, :n]
                if q == 1:
                    return kP[:, h, :n]
                return fT2[:, q - 2, h // 2, :n]
            ArkT, ArbT, AkkT = Am
            jobs = ((ArkT, 2, 0, MU, ("W_A", 0)), (ArbT, 3, 0, MU, ("W_A", 1)), (AkkT, 2, 1, MS, ("W_A", 2)),
                    (PTm[0], 3, 1, MSn, ("W_PT", 0)), (Pm[0], 1, 3, MLn, ("W_P", 0)))
            units = [(QQ[:, :, :], "W_QQ"), (Y2[:].rearrange("p (h i) -> p h i", h=8), "W_Y2"), (Z2[:].rearrange("p (h i) -> p h i", h=8), "W_Z2")]
            HB, HW = 2, 8
            for dst, ql, qr_, mk, key in jobs:
                for hb in range(HB):
                    qb, qk = units[qi % 3]
                    qi += 1
                    R.op("pe", [("matmul", dict(out=qb[:n, i, :n], lhsT=opnd(ql, HW * hb + i), rhs=opnd(qr_, HW * hb + i), start=True, stop=True)) for i in range(HW)],
                         reads=fk, writes=[qk])
                    R.op("dve", [("tensor_tensor", dict(out=dst[:n, HW * hb:HW * hb + HW, :n], in0=qb[:n, :, :n], in1=bc(mk[:n, :n].unsqueeze(1), (n, HW, n)), op=ALU.mult))],
                         reads=[qk, "W_msk"], writes=[key + (hb,)])
            if cfg.stop == "W8":
                continue
            R.op("pool", [("tensor_tensor", dict(out=MT[:n, :, :n], in0=PTm[0][:n, :, :n], in1=bc(MI[:n, :n].unsqueeze(1), (n, NH, n)), op=ALU.add))],
                 reads=[("W_PT", 0, hb) for hb in range(HB)] + ["W_msk"], writes=[("W_MT", hb) for hb in range(HB)])
            cur = 0
            NSQ = 6
            for kq in range(NSQ):
                nxt = 1 - cur
                for hb in range(HB):
                    qb, qk = units[qi % 3]
                    qi += 1
                    R.op("pe", [("matmul", dict(out=qb[:n, i, :n], lhsT=PTm[cur][:n, HW * hb + i, :n], rhs=Pm[cur][:n, HW * hb + i, :n], start=True, stop=True)) for i in range(HW)],
                         reads=[("W_PT", cur, hb), ("W_P", cur, hb)], writes=[qk])
                    R.op("act", [("activation", dict(out=Pm[nxt][:n, HW * hb:HW * hb + HW, :n], in_=qb[:n, :, :n], func=AF.Copy))], reads=[qk], writes=[("W_P", nxt, hb)])
                    if kq < NSQ - 1:
                        qb, qk = units[qi % 3]
                        qi += 1
                        R.op("pe", [("matmul", dict(out=qb[:n, i, :n], lhsT=Pm[cur][:n, HW * hb + i, :n], rhs=PTm[cur][:n, HW * hb + i, :n], start=True, stop=True)) for i in range(HW)],
                             reads=[("W_PT", cur, hb), ("W_P", cur, hb)], writes=[qk])
                        R.op("dve", [("tensor_copy", dict(out=PTm[nxt][:n, HW * hb:HW * hb + HW, :n], in_=qb[:n, :, :n]))], reads=[qk], writes=[("W_PT", nxt, hb)])
                for hb in range(HB):
                    qb, qk = units[qi % 3]
                    qi += 1
                    R.op("pe", [("matmul", dict(out=qb[:n, i, :n], lhsT=Pm[nxt][:n, HW * hb + i, :n], rhs=MT[:n, HW * hb + i, :n], start=True, stop=True)) for i in range(HW)],
                         reads=[("W_P", nxt, hb), ("W_MT", hb)], writes=[qk])
                    R.op("dve", [("tensor_tensor", dict(out=MT[:n, HW * hb:HW * hb + HW, :n], in0=qb[:n, :, :n], in1=MT[:n, HW * hb:HW * hb + HW, :n], op=ALU.add))],
                         reads=[qk], writes=[("W_MT", hb)])
                cur = nxt
            if cfg.stop == "W9":
                continue
            akeys = lambda i: [("W_A", i, hb) for hb in range(HB)]
            mtk = [("W_MT", hb) for hb in range(HB)]
            RHS0, Uneg, yo = rt, kkt, kh
            hs = lambda h: slice(h * HN, (h + 1) * HN)
            R.op("pe", [m for h in range(NH) for m in (
                        ("matmul", dict(out=Y2[:n, hs(h)], lhsT=opnd(1, h), rhs=T_bf[:, h // 2, :], start=True, stop=False)),
                        ("matmul", dict(out=Y2[:n, hs(h)], lhsT=AkkT[:n, h, :n], rhs=v_bf[:n, hs(h)], start=False, stop=True)))],
                 reads=fk + ["W_Tbf", BK[6]] + akeys(2), writes=["W_Y2"])
            R.op("act", [("activation", dict(out=RHS0[:n], in_=Y2[:n, :], func=AF.Copy))], reads=["W_Y2"] + fk, writes=[BK[0]])
            if cfg.stop == "W10":
                continue
            R.op("pe", [("matmul", dict(out=Z2[:n, hs(h)], lhsT=MT[:n, h, :n], rhs=RHS0[:n, hs(h)], start=True, stop=True)) for h in range(NH)],
                 reads=mtk + [BK[0]], writes=["W_Z2"])
            R.op("act", [("activation", dict(out=Uneg[:n], in_=Z2[:n, :], func=AF.Copy, scale=-1.0))], reads=["W_Z2"] + fk, writes=[BK[1]])
            if cfg.stop == "W11":
                continue
            R.op("pe", [m for h in range(NH) for m in (
                        ("matmul", dict(out=Y2[:n, hs(h)], lhsT=opnd(0, h), rhs=T_bf[:, h // 2, :], start=True, stop=False)),
                        ("matmul", dict(out=Y2[:n, hs(h)], lhsT=ArkT[:n, h, :n], rhs=v_bf[:n, hs(h)], start=False, stop=False)),
                        ("matmul", dict(out=Y2[:n, hs(h)], lhsT=ArbT[:n, h, :n], rhs=Uneg[:n, hs(h)], start=False, stop=True)))],
                 reads=fk + ["W_Tbf", BK[6], BK[1]] + akeys(0) + akeys(1), writes=["W_Y2"])
            if cfg.stop == "W12":
                continue
            Y2s = Z2[:].rearrange("p (a b c) -> p a b c", a=8, b=2)
            R.op("pe", [m for h in range(NH) for m in (
                        ("matmul", dict(out=Y2s[:, h // 2, h % 2, :], lhsT=kbar[:n, (h // 2) * 128:(h // 2 + 1) * 128], rhs=v_bf[:n, hs(h)], start=True, stop=False)),
                        ("matmul", dict(out=Y2s[:, h // 2, h % 2, :], lhsT=bbar[:n, (h // 2) * 128:(h // 2 + 1) * 128], rhs=Uneg[:n, hs(h)], start=False, stop=True)))],
                 reads=[BK[4], BK[5], BK[6], BK[1]], writes=["W_Z2"])
            R.op("dve", [("tensor_tensor", dict(out=Tst[:], in0=Tst[:], in1=bc(WC[:, :].unsqueeze(2), (128, 8, HN)), op=ALU.mult))], reads=["W_WC"], writes=["W_T"])
            R.op("dve", [("tensor_tensor", dict(out=Tst[0:64], in0=Tst[0:64], in1=Y2s[0:64, :, 0, :], op=ALU.add))], reads=["W_Z2"], writes=["W_T"])
            R.op("dve", [("tensor_tensor", dict(out=Tst[64:128], in0=Tst[64:128], in1=Y2s[64:128, :, 1, :], op=ALU.add))], reads=["W_Z2"], writes=["W_T"])
            R.op("act", [("activation", dict(out=T_bf[:], in_=Tst[:], func=AF.Copy))], reads=["W_T"], writes=["W_Tbf"])
            if cfg.stop == "W13":
                continue
            yc, ysq = F[0], F[1]
            h3 = lambda ap: ap.rearrange("p (h d) -> p h d", h=NH)
            R.op("dve", [("tensor_reduce", dict(out=s16[3][:n], in_=h3(Y2[:n, :]), axis=AX.X, op=ALU.add))], reads=["W_Y2"], writes=[("W_s16", 3)])
            R.op("dve", [("tensor_scalar", dict(out=s16[3][:n], in0=s16[3][:n], scalar1=1.0 / HN, scalar2=None, op0=ALU.mult))], reads=[("W_s16", 3)], writes=[("W_s16", 3)])
            R.op("dve", [("tensor_tensor", dict(out=h3(yc[:n]), in0=h3(Y2[:n, :]), in1=bc(s16[3][:n].unsqueeze(2), (n, NH, HN)), op=ALU.subtract))],
                 reads=["W_Y2", ("W_s16", 3)], writes=[FK[0]])
            R.op("pool", [("tensor_tensor", dict(out=ysq[:n], in0=yc[:n], in1=yc[:n], op=ALU.mult))], reads=[FK[0]], writes=[FK[1]])
            R.op("dve", [("tensor_reduce", dict(out=s16[4][:n], in_=h3(ysq[:n]), axis=AX.X, op=ALU.add))], reads=[FK[1]], writes=[("W_s16", 4)])
            R.op("dve", [("tensor_scalar", dict(out=s16[4][:n], in0=s16[4][:n], scalar1=1.0 / HN, scalar2=64e-5, op0=ALU.mult, op1=ALU.add))], reads=[("W_s16", 4)], writes=[("W_s16", 4)])
            R.op("act", [("activation", dict(out=s16[4][:n], in_=s16[4][:n], func=AF.Sqrt))], reads=[("W_s16", 4)], writes=[("W_s16", 4)])
            R.op("dve", [("reciprocal", dict(out=s16[5][:n], in_=s16[4][:n]))], reads=[("W_s16", 4)], writes=[("W_s16", 5)])
            R.op("dve", [("tensor_tensor", dict(out=h3(yc[:n]), in0=h3(yc[:n]), in1=bc(s16[5][:n].unsqueeze(2), (n, NH, HN)), op=ALU.mult))], reads=[("W_s16", 5)], writes=[FK[0]])
            R.op("pool", [("tensor_tensor", dict(out=yc[:n], in0=yc[:n], in1=bcv(2, n), op=ALU.mult))], reads=["W_vecs"], writes=[FK[0]])
            R.op("pool", [("tensor_tensor", dict(out=yc[:n], in0=yc[:n], in1=bcv(3, n), op=ALU.add))], reads=["W_vecs"], writes=[FK[0]])
            R.op("pool", [("tensor_tensor", dict(out=yc[:n], in0=yc[:n], in1=bo_[:n], op=ALU.add))], reads=[FK[11]], writes=[FK[0]])
            R.op("pool", [("tensor_tensor", dict(out=yo[:n], in0=yc[:n], in1=g_[:n], op=ALU.mult))], reads=[FK[2]] + fk, writes=[BK[2]])
            Z2b8 = Z2b[:, 0:8, :]
            R.op("pe", [("transpose", dict(out=Z2b8[:, k, :n], in_=yo[:n, k * 128:(k + 1) * 128], identity=ident[:n, :n])) for k in range(8)],
                 reads=[BK[2], "ident"], writes=["W_Z2"])
            R.op("act", [("activation", dict(out=yTo[:, :, :n], in_=Z2b8[:, :, :n], func=AF.Copy))], reads=["W_Z2"], writes=["W_yTo"])
            R.dma("sp", "W_yTo", dict(out=YT[RET_W:RET_W + W, r0:r0 + n].rearrange("(k p) t -> p k t", p=128), in_=yTo[:, :, :n]),
                  reads=["W_yTo"], writes=[("YT", t)])
    R.countdown = None
    R.barrier()


def host_consts(cfg):
    c = {}
    L = cfg.L
    half = 64
    inv = (np.float32(10000.0) ** (-np.arange(half, dtype=np.float32) / np.float32(half))).astype(np.float32)
    ang = (np.arange(L, dtype=np.float32)[:, None] * inv[None, :]).astype(np.float32)
    cs, sn = np.cos(ang).astype(np.float32), np.sin(ang).astype(np.float32)
    sc = np.float32(RET_D ** -0.5)
    c["rope"] = np.ascontiguousarray(np.stack([cs, sn, cs * sc, sn * sc], axis=1)).astype(np.float32)
    log_g = np.log(1.0 - 2.0 ** (-5.0 - np.arange(RET_H, dtype=np.float64)))
    idx = np.arange(CH, dtype=np.float64)
    diff = idx[None, :] - idx[:, None]
    m = np.where((diff >= 0)[:, None, :], np.exp(np.maximum(diff, 0)[:, None, :] * log_g[None, :, None]), 0.0)
    c["ret_maskT"] = np.ascontiguousarray(m).astype(np.float32)
    vec = np.zeros((128, 3, RET_H), np.float32)
    vec[:, 0, :] = np.exp((CH - 1 - idx)[:, None] * log_g[None, :])
    vec[:NMETA, 1, :] = np.exp((NMETA - 1 - idx[:NMETA])[:, None] * log_g[None, :])
    vec[:, 2, :] = np.exp((idx + 1.0)[:, None] * log_g[None, :])
    c["ret_vec"] = vec
    tri = np.zeros((128, 3, 128), np.float32)
    ii = np.arange(128)
    tri[:, 0, :] = (ii[:, None] <= ii[None, :])
    tri[:, 1, :] = (ii[:, None] > ii[None, :])
    tri[:, 2, :] = 1.0
    c["tri"] = tri
    mk = np.zeros((128, 5, 128), np.float32)
    mk[:, 0, :] = (ii[:, None] <= ii[None, :])
    mk[:, 1, :] = (ii[:, None] < ii[None, :])
    mk[:, 2, :] = -1.0 * (ii[:, None] < ii[None, :])
    mk[:, 3, :] = -1.0 * (ii[:, None] > ii[None, :])
    mk[:, 4, :] = (ii[:, None] == ii[None, :])
    c["rw_msk"] = mk
    return c


def build_program(cfg):
    nc = bass.Bass("TRN2", target_bir_lowering=False)
    dr = {}

    def din(name, shape, dt=F32):
        dr[name] = nc.dram_tensor(name, list(shape), dt, kind="ExternalInput").ap()

    def dscratch(name, shape, dt=F32):
        kind = "ExternalOutput" if cfg.debug else "Internal"
        dr[name] = nc.dram_tensor(name, list(shape), dt, kind=kind).ap()

    din("x", (cfg.S, D))
    din("meta_tokens", (NMETA, D))
    din("ident_in", (128, 128))
    din("rope", (cfg.L, 4, 64))
    din("ret_maskT", (128, RET_H, 128))
    din("ret_vec", (128, 3, RET_H))
    din("tri", (128, 3, 128))
    din("rw_msk", (128, 5, 128))
    din("rwkv_mu", (cfg.depth, RW_COLS))
    for nm in ("rwkv_w0", "rwkv_a0", "rwkv_k_k", "rwkv_k_a", "rwkv_ln_w", "rwkv_ln_b"):
        din(nm, (cfg.depth, RW_W))
    din("rwkv_w2", (cfg.depth, 64, RW_W))
    din("rwkv_a2", (cfg.depth, 64, RW_W))
    din("rwkv_g2", (cfg.depth, 160, RW_W))
    din("rwkv_r_k", (cfg.depth, RW_H, RW_N))
    din("ssd_conv_w", (cfg.depth, 4, SSD_CONV))
    din("ssd_conv_b", (cfg.depth, SSD_CONV))
    din("ssd_dt_bias", (cfg.depth, SSD_H))
    din("ssd_a_log", (cfg.depth, SSD_H))
    din("ssd_d", (cfg.depth, SSD_H))
    din("ssd_norm_w", (cfg.depth, SSD_W))
    for nm in ("norm_mix_pre", "norm_mix_post", "norm_ffn_pre", "norm_ffn_post"):
        din(nm, (cfg.depth, D))
    din("w_in", (cfg.depth, D, IN_COLS))
    din("w_branch_ret", (cfg.depth, RET_W, D))
    din("w_branch_rwkv", (cfg.depth, RW_W, D))
    din("w_branch_ssd", (cfg.depth, SSD_W, D))
    din("w_out", (cfg.depth, D, D))
    din("ffn_w_gate", (cfg.depth, D, FFN))
    din("ffn_w_up", (cfg.depth, D, FFN))
    din("ffn_w_down", (cfg.depth, FFN, D))
    if cfg.yt_input:
        din("YT", (2 * D, cfg.L), BF16)
    else:
        dscratch("YT", (2 * D, cfg.L), BF16)
    dr["H"] = nc.dram_tensor("H", [cfg.L, D], F32, kind="ExternalOutput").ap()
    dscratch("P_ret", (cfg.L, 4096))
    dscratch("P_rw", (cfg.L, RW_COLS))
    dscratch("P_ssd", (cfg.L, NP_COLS - O_Z))
    dscratch("SGT", (3 * D, cfg.L))
    R = Rec()
    with ExitStack() as es:
        ident_f = es.enter_context(nc.sbuf_tensor("ident_f", [128, 128], F32))
        ident = es.enter_context(nc.sbuf_tensor("ident", [128, 128], BF16))
        R.dma("sp", "ident", dict(out=ident_f[:], in_=dr["ident_in"][:, :]), writes=["ident_f"])
        R.op("dve", [("tensor_copy", dict(out=ident[:], in_=ident_f[:]))], reads=["ident_f"], writes=["ident"])
        R.dma("sp", "Hinit", dict(out=dr["H"][0:NMETA, :], in_=dr["meta_tokens"][:, :]), writes=[("H", 0)])
        for i in range(0, cfg.nchunks, 8):
            r0 = NMETA + i * CH
            nn = min(8, cfg.nchunks - i) * CH
            R.dma("sp", "Hinit", dict(out=dr["H"][r0:r0 + nn, :], in_=dr["x"][i * CH:i * CH + nn, :]), writes=[("Hx", i)])
        for t in range(len(cfg.tiles)):
            R.lastw[("H", t)] = ("d_Hinit", R.cnt["d_Hinit"])
        for li in cfg.layers:
            if cfg.stop == "init":
                break
            if "A" in cfg.phases:
                phase_A(R, nc, cfg, li, dr, ident)
            if "R" in cfg.phases:
                phase_B_ret(R, nc, cfg, li, dr, ident)
            if "W" in cfg.phases:
                phase_B_rwkv(R, nc, cfg, li, dr, ident, ident_f)
            if "S" in cfg.phases:
                phase_B_ssd(R, nc, cfg, li, dr, ident)
            if "C" in cfg.phases:
                phase_C(R, nc, cfg, li, dr, ident)
            if "D" in cfg.phases:
                phase_D(R, nc, cfg, li, dr, ident)
        R.emit(nc)
    return nc, R


def host_inputs(inputs, b, cfg):
    m = {"x": np.ascontiguousarray(inputs["x"][b, :cfg.S]), "meta_tokens": np.ascontiguousarray(inputs["meta_tokens"]),
         "ident_in": np.eye(128, dtype=np.float32)}
    m.update(host_consts(cfg))
    for k in ("norm_mix_pre", "norm_mix_post", "norm_ffn_pre", "norm_ffn_post", "w_in", "w_branch_ret", "w_branch_rwkv",
              "w_branch_ssd", "w_out", "ffn_w_gate", "ffn_w_up", "ffn_w_down",
              "ssd_conv_w", "ssd_conv_b", "ssd_dt_bias", "ssd_a_log", "ssd_d", "ssd_norm_w",
              "rwkv_mu", "rwkv_w0", "rwkv_w2", "rwkv_a0", "rwkv_a2", "rwkv_g2", "rwkv_k_k", "rwkv_k_a", "rwkv_r_k", "rwkv_ln_w", "rwkv_ln_b"):
        m[k] = np.ascontiguousarray(inputs[k])
    return m


_CACHE = {}


def kernel(**inputs):
    cfg = Cfg(nchunks=64, layers=(0, 1, 2, 3))
    if "prog" not in _CACHE:
        _CACHE["prog"] = build_program(cfg)
    nc, _ = _CACHE["prog"]
    nb = inputs["x"].shape[0]
    in_maps = [host_inputs(inputs, b, cfg) for b in range(nb)]
    res = run_bass_kernel_spmd(nc, in_maps, core_ids=list(range(nb)))
    out = np.stack([np.asarray(res.results[b]["H"])[NMETA:] for b in range(nb)]).astype(np.float32)
    return out
```

```python
import numpy as np
from contextlib import ExitStack
import concourse.bass as bass
import concourse.mybir as mybir
from concourse.bass_utils import run_bass_kernel_spmd

F32 = mybir.dt.float32
BF16 = mybir.dt.bfloat16
AF = mybir.ActivationFunctionType
ALU = mybir.AluOpType

D = 2048
NMETA = 16
CH = 128
DEPTH = 4
EPS = 1e-6
RET_H, RET_D, RET_W = 8, 128, 1024
RW_H, RW_N, RW_W = 16, 64, 1024
RW_COLS = 3 * 1024 + 64 + 64 + 160
SSD_H, SSD_P, SSD_W, SSD_G, SSD_N = 32, 64, 2048, 4, 128
SSD_CONV = 3072
FFN = 5632
IN_COLS = 18752
NP_COLS = IN_COLS - 3 * D
O_RQ, O_RK, O_RV, O_RG = 0, 1024, 2048, 3072
O_RW = 4096
O_Z = O_RW + RW_COLS
O_XBC = O_Z + SSD_W
O_DT = O_XBC + SSD_CONV
assert O_DT + SSD_H == NP_COLS


_UID = [0]


def uq(name):
    _UID[0] += 1
    return f"{name}_{_UID[0]}"


class Rec:
    ENGS = ("pe", "dve", "act", "pool", "sp")

    def __init__(self):
        self.ops = {k: [] for k in self.ENGS}
        self.cnt = {}
        self.seen = {k: {} for k in self.ENGS}
        self.lastw = {}
        self.rds = {}
        self.nops = 0

    def _deps(self, eng, reads, writes):
        need = {}

        def want(ev):
            s, v = ev
            if self.seen[eng].get(s, 0) < v and need.get(s, 0) < v:
                need[s] = v
        for k in reads:
            if k in self.lastw:
                want(self.lastw[k])
        for k in writes:
            if k in self.lastw:
                want(self.lastw[k])
            for ev in self.rds.get(k, {}).items():
                want(ev)
        for s, v in need.items():
            self.seen[eng][s] = v
        return list(need.items())

    def _commit(self, ev, reads, writes):
        s, v = ev
        for k in reads:
            d = self.rds.setdefault(k, {})
            if d.get(s, 0) < v:
                d[s] = v
        for k in writes:
            self.lastw[k] = ev
            self.rds[k] = {}

    countdown = None

    def op(self, eng, insts, reads=(), writes=()):
        if self.countdown is not None:
            if self.countdown <= 0:
                return
            self.countdown -= 1
        waits = self._deps(eng, reads, writes)
        s = "c_" + eng
        self.cnt[s] = self.cnt.get(s, 0) + 1
        self.ops[eng].append((waits, insts, s, 1))
        self._commit((s, self.cnt[s]), reads, writes)
        self.nops += len(insts)

    def dma(self, q, semkey, kwargs, reads=(), writes=()):
        if self.countdown is not None and self.countdown <= 0:
            return
        waits = self._deps(q, reads, writes)
        s = "d_" + semkey
        self.cnt[s] = self.cnt.get(s, 0) + 16
        self.ops[q].append((waits, [("dma_start", kwargs)], s, 16))
        self._commit((s, self.cnt[s]), reads, writes)
        self.nops += 1

    def barrier(self):
        for eng in self.ENGS:
            waits = []
            for s, v in self.cnt.items():
                if self.seen[eng].get(s, 0) < v:
                    waits.append((s, v))
                    self.seen[eng][s] = v
            if waits:
                self.ops[eng].append((waits, [], None, 0))
        self.lastw = {}
        self.rds = {}

    def emit(self, nc):
        blockname = {"pe": "tensor", "dve": "vector", "act": "scalar", "pool": "gpsimd", "sp": "sync"}
        with ExitStack() as st:
            sems = {name: st.enter_context(nc.semaphore(name)) for name in self.cnt}
            block = st.enter_context(nc.Block())
            for eng in self.ENGS:
                def body(e, eng=eng):
                    for waits, insts, s, inc in self.ops[eng]:
                        for ws, wv in waits:
                            e.wait_ge(sems[ws], wv)
                        last = None
                        for name, kw in insts:
                            last = getattr(e, name)(**kw)
                        if last is not None:
                            last.then_inc(sems[s], inc)
                    if eng == "sp":
                        for name, v in self.cnt.items():
                            e.wait_ge(sems[name], v)
                getattr(block, blockname[eng])(body)


class Cfg:
    def __init__(self, nchunks=64, layers=(0, 1, 2, 3), debug=False, depth=DEPTH):
        self.nchunks = nchunks
        self.layers = tuple(layers)
        self.debug = debug
        self.depth = depth
        import os
        self.stop = os.environ.get("KSTOP", "")
        self.phases = os.environ.get("KPHASES", "ARSWCD")
        self.yt_input = bool(os.environ.get("KYTIN", ""))
        self.S = nchunks * CH
        self.L = NMETA + self.S
        self.tiles = [(0, NMETA)] + [(NMETA + CH * i, CH) for i in range(nchunks)]


def mm_group(out, pairs):
    n = len(pairs)
    return [("matmul", dict(out=out, lhsT=l, rhs=r, start=(i == 0), stop=(i == n - 1))) for i, (l, r) in enumerate(pairs)]


def passes_of(tiles, per_pass):
    out = []
    i = 0
    while i < len(tiles):
        k = per_pass + 1 if i == 0 else per_pass
        out.append(list(range(i, min(i + k, len(tiles)))))
        i += k
    return out


def norm_transpose_stage(R, nc, st, cfg, tl, Hsrc, wn_bc, xT, ident, pfx, pst):
    offs = []
    off = 0
    for j, t in enumerate(tl):
        r0, n = cfg.tiles[t]
        b = j % 2
        hb = st["h"][b]
        R.dma("sp", f"{pfx}h{b}", dict(out=hb[:n], in_=Hsrc[r0:r0 + n, :]), reads=[("H", t)], writes=[(pfx, "h", b)])
        R.op("act", [("activation", dict(out=st["junk"][:n], in_=hb[:n], func=AF.Square, accum_out=st["ss"][:n]))],
             reads=[(pfx, "h", b)], writes=[(pfx, "junk"), (pfx, "ss")])
        R.op("dve", [("tensor_scalar", dict(out=st["ms"][:n], in0=st["ss"][:n], scalar1=1.0 / D, scalar2=EPS, op0=ALU.mult, op1=ALU.add))],
             reads=[(pfx, "ss")], writes=[(pfx, "ms")])
        R.op("act", [("activation", dict(out=st["rt"][:n], in_=st["ms"][:n], func=AF.Sqrt))], reads=[(pfx, "ms")], writes=[(pfx, "rt")])
        R.op("dve", [("reciprocal", dict(out=st["rstd"][:n], in_=st["rt"][:n]))], reads=[(pfx, "rt")], writes=[(pfx, "rstd")])
        xn = st["xn"][b]
        R.op("dve", [("scalar_tensor_tensor", dict(out=xn[:n], in0=hb[:n], scalar=st["rstd"][:n], in1=wn_bc[:n], op0=ALU.mult, op1=ALU.mult))],
             reads=[(pfx, "h", b), (pfx, "rstd"), (pfx, "wn")], writes=[(pfx, "xn", b)])
        pt = pst[b]
        R.op("pe", [("transpose", dict(out=pt[:, k, :n], in_=xn[:n, k * 128:(k + 1) * 128], identity=ident[:n, :n])) for k in range(D // 128)],
             reads=[(pfx, "xn", b), "ident"], writes=[("psT", b)])
        R.op("act" if j % 2 == 0 else "dve",
             [("activation", dict(out=xT[:, :, off:off + n], in_=pt[:, :, :n], func=AF.Copy))] if j % 2 == 0 else
             [("tensor_copy", dict(out=xT[:, :, off:off + n], in_=pt[:, :, :n]))],
             reads=[("psT", b)], writes=[(pfx, "xT", j)])
        offs.append((t, off, n))
        off += n
    return offs


def phase_A(R, nc, cfg, li, dr, ident):
    TP = 8
    H, SGT, w_in = dr["H"], dr["SGT"], dr["w_in"]
    with ExitStack() as es:
        sb = lambda name, shape, dt: es.enter_context(nc.sbuf_tensor(uq(name), shape, dt))
        ps = lambda name, shape, dt: es.enter_context(nc.psum_tensor(uq(name), shape, dt))
        st = dict(h=[sb(f"A_h{i}", [128, D], F32) for i in range(2)], junk=sb("A_junk", [128, D], BF16),
                  ss=sb("A_ss", [128, 1], F32), ms=sb("A_ms", [128, 1], F32), rt=sb("A_rt", [128, 1], F32),
                  rstd=sb("A_rstd", [128, 1], F32), xn=[sb(f"A_xn{i}", [128, D], BF16) for i in range(2)])
        wn_bc = sb("A_wn", [128, D], F32)
        TMAX = NMETA + TP * CH
        xT = sb("A_xT", [128, D // 128, TMAX], BF16)
        NWB = 3
        wb = [sb(f"A_wb{i}", [128, D // 128, 512], BF16) for i in range(NWB)]
        NSTG = 4
        stg = [sb(f"A_stg{i}", [128, 512], F32) for i in range(NSTG)]
        pst = [ps(f"A_pst{i}", [128, D // 128, 128], BF16) for i in range(2)]
        NPS = 4
        psm = [ps(f"A_psm{i}", [128, 512], F32) for i in range(NPS)]

        R.dma("sp", "A_wn", dict(out=wn_bc[:], in_=dr["norm_mix_pre"][li:li + 1, :].partition_broadcast(128)), writes=[("A", "wn")])
        slabs = []
        for (sec0, secw, tname) in ((0, 4096, "P_ret"), (O_RW, RW_COLS, "P_rw"), (O_Z, NP_COLS - O_Z, "P_ssd")):
            for c in range(0, secw, 512):
                slabs.append((sec0 + c, min(512, secw - c), False, tname, c))
        slabs += [(c, 512, True, None, 0) for c in range(NP_COLS, IN_COLS, 512)]
        wi = 0
        si = 0
        pi = 0
        for tl in passes_of(cfg.tiles, TP):
            offs = norm_transpose_stage(R, nc, st, cfg, tl, H, wn_bc, xT, ident, "A", pst)
            ntok = sum(n for _, _, n in offs)
            if cfg.stop == "A1":
                continue
            row_base = cfg.tiles[tl[0]][0]
            xkeys = [("A", "xT", j) for j in range(len(tl))]
            for (c0, w, is_gate, tname, lc0) in slabs:
                wbuf = wb[wi % NWB]
                wkey = ("A", "wb", wi % NWB)
                wi += 1
                R.dma("pool", f"A_wb{(wi - 1) % NWB}",
                      dict(out=wbuf[:, :, :w], in_=w_in[li, :, c0:c0 + w].rearrange("(k p) n -> p k n", p=128)),
                      writes=[wkey])
                if not is_gate:
                    for j, (t, off, n) in enumerate(offs):
                        pb = psm[pi % NPS]
                        pkey = ("psm", pi % NPS)
                        pi += 1
                        R.op("pe", mm_group(pb[:n, :w], [(xT[:, k, off:off + n], wbuf[:, k, :w]) for k in range(D // 128)]),
                             reads=[wkey, xkeys[j]], writes=[pkey])
                        sg = stg[si % NSTG]
                        skey = ("A", "stg", si % NSTG)
                        if si % 2 == 0:
                            R.op("act", [("activation", dict(out=sg[:n, :w], in_=pb[:n, :w], func=AF.Copy))], reads=[pkey], writes=[skey])
                        else:
                            R.op("dve", [("tensor_copy", dict(out=sg[:n, :w], in_=pb[:n, :w]))], reads=[pkey], writes=[skey])
                        r0 = cfg.tiles[t][0]
                        R.dma("sp", f"A_stg{si % NSTG}", dict(out=dr[tname][r0:r0 + n, lc0:lc0 + w], in_=sg[:n, :w]), reads=[skey], writes=[("P", t)])
                        si += 1
                else:
                    g0 = c0 - NP_COLS
                    for q in range(4):
                        for tg in range(0, ntok, 512):
                            tw = min(512, ntok - tg)
                            pb = psm[pi % NPS]
                            pkey = ("psm", pi % NPS)
                            pi += 1
                            R.op("pe", mm_group(pb[:, :tw], [(wbuf[:, k, q * 128:(q + 1) * 128], xT[:, k, tg:tg + tw]) for k in range(D // 128)]),
                                 reads=[wkey] + xkeys, writes=[pkey])
                            sg = stg[si % NSTG]
                            skey = ("A", "stg", si % NSTG)
                            R.op("act", [("activation", dict(out=sg[:, :tw], in_=pb[:, :tw], func=AF.Sigmoid))], reads=[pkey], writes=[skey])
                            R.dma("sp", f"A_stg{si % NSTG}",
                                  dict(out=SGT[g0 + q * 128:g0 + (q + 1) * 128, row_base + tg:row_base + tg + tw], in_=sg[:, :tw]),
                                  reads=[skey], writes=[("SGT", tl[0])])
                            si += 1
    R.barrier()


def rstd_ops(R, st, pfx, src, n, reads):
    R.op("act", [("activation", dict(out=st["junk"][:n], in_=src, func=AF.Square, accum_out=st["ss"][:n]))],
         reads=reads, writes=[(pfx, "junk"), (pfx, "ss")])
    R.op("dve", [("tensor_scalar", dict(out=st["ms"][:n], in0=st["ss"][:n], scalar1=1.0 / D, scalar2=EPS, op0=ALU.mult, op1=ALU.add))],
         reads=[(pfx, "ss")], writes=[(pfx, "ms")])
    R.op("act", [("activation", dict(out=st["rt"][:n], in_=st["ms"][:n], func=AF.Sqrt))], reads=[(pfx, "ms")], writes=[(pfx, "rt")])
    R.op("dve", [("reciprocal", dict(out=st["rstd"][:n], in_=st["rt"][:n]))], reads=[(pfx, "rt")], writes=[(pfx, "rstd")])


def phase_C(R, nc, cfg, li, dr, ident):
    TP = 4
    H, SGT, YT = dr["H"], dr["SGT"], dr["YT"]
    TMAX = NMETA + TP * CH
    KY = 32
    for tl in passes_of(cfg.tiles, TP):
        offs = []
        off = 0
        for t in tl:
            offs.append((t, off, cfg.tiles[t][1]))
            off += cfg.tiles[t][1]
        ntok = off
        row_base = cfg.tiles[tl[0]][0]
        with ExitStack() as es0:
            mT = es0.enter_context(nc.sbuf_tensor(uq("C_mT"), [128, D // 128, TMAX], BF16))
            with ExitStack() as es:
                sb = lambda name, shape, dt: es.enter_context(nc.sbuf_tensor(uq(name), shape, dt))
                ps = lambda name, shape, dt: es.enter_context(nc.psum_tensor(uq(name), shape, dt))
                yT = sb("C_yT", [128, KY, TMAX], BF16)
                wbc = [sb(f"C_wb{i}", [128, KY, 512], BF16) for i in range(2)]
                sg = [sb(f"C_sg{i}", [128, 3, 512], F32) for i in range(2)]
                m1 = [sb(f"C_m1_{i}", [128, 512], F32) for i in range(2)]
                m2 = [sb(f"C_m2_{i}", [128, 512], F32) for i in range(2)]
                m3 = [sb(f"C_m3_{i}", [128, 512], F32) for i in range(2)]
                psb = [[ps(f"C_ps{i}_{j}", [128, 512], F32) for j in range(3)] for i in range(2)]
                R.dma("sp", "C_yT", dict(out=yT[:, :, :ntok], in_=YT[:, row_base:row_base + ntok].rearrange("(k p) t -> p k t", p=128)),
                      reads=[("YT", t) for t in tl], writes=[("C", "yT")])
                it = 0
                for sl in range(D // 512):
                    wb = wbc[sl % 2]
                    wkey = ("C", "wb", sl % 2)
                    c0 = sl * 512
                    R.dma("pool", f"C_wb{sl % 2}", dict(out=wb[:, 0:8, :], in_=dr["w_branch_ret"][li, :, c0:c0 + 512].rearrange("(k p) n -> p k n", p=128)), writes=[wkey])
                    R.dma("pool", f"C_wb{sl % 2}", dict(out=wb[:, 8:16, :], in_=dr["w_branch_rwkv"][li, :, c0:c0 + 512].rearrange("(k p) n -> p k n", p=128)), writes=[wkey])
                    R.dma("pool", f"C_wb{sl % 2}", dict(out=wb[:, 16:32, :], in_=dr["w_branch_ssd"][li, :, c0:c0 + 512].rearrange("(k p) n -> p k n", p=128)), writes=[wkey])
                    for q in range(4):
                        f0 = c0 + q * 128
                        for tg in range(0, ntok, 512):
                            tw = min(512, ntok - tg)
                            b = it % 2
                            it += 1
                            R.dma("sp", f"C_sg{b}", dict(out=sg[b][:, :, :tw],
                                  in_=SGT[:, row_base + tg:row_base + tg + tw].rearrange("(b f) t -> f b t", b=3)[f0:f0 + 128]),
                                  reads=[("SGT", 0)], writes=[("C", "sg", b)])
                            for br, (k0, k1) in enumerate(((0, 8), (8, 16), (16, 32))):
                                R.op("pe", mm_group(psb[b][br][:, :tw], [(wb[:, k, q * 128:(q + 1) * 128], yT[:, k, tg:tg + tw]) for k in range(k0, k1)]),
                                     reads=[wkey, ("C", "yT")], writes=[("C", "ps", b, br)])
                            for br, mm_ in enumerate((m1, m2, m3)):
                                R.op("dve", [("tensor_tensor", dict(out=mm_[b][:, :tw], in0=psb[b][br][:, :tw], in1=sg[b][:, br, :tw], op=ALU.mult))],
                                     reads=[("C", "ps", b, br), ("C", "sg", b)], writes=[("C", "m", br, b)])
                            R.op("pool", [("tensor_tensor", dict(out=m1[b][:, :tw], in0=m1[b][:, :tw], in1=m2[b][:, :tw], op=ALU.add))],
                                 reads=[("C", "m", 1, b)], writes=[("C", "m", 0, b)])
                            R.op("pool", [("tensor_tensor", dict(out=mT[:, sl * 4 + q, tg:tg + tw], in0=m1[b][:, :tw], in1=m3[b][:, :tw], op=ALU.add))],
                                 reads=[("C", "m", 0, b), ("C", "m", 2, b)], writes=[("C", "mT")])
            R.barrier()
            with ExitStack() as es:
                sb = lambda name, shape, dt: es.enter_context(nc.sbuf_tensor(uq(name), shape, dt))
                ps = lambda name, shape, dt: es.enter_context(nc.psum_tensor(uq(name), shape, dt))
                wo = sb("C_wo", [128, D // 128, D], BF16)
                wn_bc = sb("C_wn", [128, D], F32)
                st = dict(junk=sb("C_junk", [128, D], BF16), ss=sb("C_ss", [128, 1], F32), ms=sb("C_ms", [128, 1], F32),
                          rt=sb("C_rt", [128, 1], F32), rstd=sb("C_rstd", [128, 1], F32))
                hb = [sb(f"C_h{i}", [128, D], F32) for i in range(2)]
                tmp = [sb(f"C_tmp{i}", [128, D], F32) for i in range(2)]
                psm = [ps(f"C_psm{i}", [128, D], F32) for i in range(2)]
                for hh in range(2):
                    R.dma("pool", "C_wo", dict(out=wo[:, :, hh * 1024:(hh + 1) * 1024], in_=dr["w_out"][li, :, hh * 1024:(hh + 1) * 1024].rearrange("(k p) n -> p k n", p=128)),
                          writes=[("C", "wo")])
                R.dma("sp", "C_wn", dict(out=wn_bc[:], in_=dr["norm_mix_post"][li:li + 1, :].partition_broadcast(128)), writes=[("C", "wn")])
                for j, (t, off, n) in enumerate(offs):
                    b = j % 2
                    r0 = cfg.tiles[t][0]
                    R.dma("sp", f"C_h{b}", dict(out=hb[b][:n], in_=H[r0:r0 + n, :]), reads=[("H", t)], writes=[("C", "h", b)])
                    for s4 in range(4):
                        R.op("pe", mm_group(psm[b][:n, s4 * 512:(s4 + 1) * 512], [(mT[:, k, off:off + n], wo[:, k, s4 * 512:(s4 + 1) * 512]) for k in range(D // 128)]),
                             reads=[("C", "mT"), ("C", "wo")], writes=[("C", "psm", b, s4)])
                    pk = [("C", "psm", b, s4) for s4 in range(4)]
                    rstd_ops(R, st, "C", psm[b][:n, :], n, pk)
                    R.op("dve", [("scalar_tensor_tensor", dict(out=tmp[b][:n], in0=psm[b][:n, :], scalar=st["rstd"][:n], in1=wn_bc[:n], op0=ALU.mult, op1=ALU.mult))],
                         reads=pk + [("C", "rstd"), ("C", "wn")], writes=[("C", "tmp", b)])
                    R.op("pool", [("tensor_tensor", dict(out=hb[b][:n], in0=hb[b][:n], in1=tmp[b][:n], op=ALU.add))],
                         reads=[("C", "tmp", b)], writes=[("C", "h", b)])
                    R.dma("sp", f"C_hs{b}", dict(out=H[r0:r0 + n, :], in_=hb[b][:n]), reads=[("C", "h", b)], writes=[("H", t)])
            R.barrier()


def phase_D(R, nc, cfg, li, dr, ident):
    TP = 4
    H = dr["H"]
    TMAX = NMETA + TP * CH
    KF = FFN // 128
    for tl in passes_of(cfg.tiles, TP):
        with ExitStack() as es0:
            aT = es0.enter_context(nc.sbuf_tensor(uq("D_aT"), [128, KF, TMAX], BF16))
            with ExitStack() as es:
                sb = lambda name, shape, dt: es.enter_context(nc.sbuf_tensor(uq(name), shape, dt))
                ps = lambda name, shape, dt: es.enter_context(nc.psum_tensor(uq(name), shape, dt))
                st = dict(h=[sb(f"D_h{i}", [128, D], F32) for i in range(2)], junk=sb("D_junk", [128, D], BF16),
                          ss=sb("D_ss", [128, 1], F32), ms=sb("D_ms", [128, 1], F32), rt=sb("D_rt", [128, 1], F32),
                          rstd=sb("D_rstd", [128, 1], F32), xn=[sb(f"D_xn{i}", [128, D], BF16) for i in range(2)])
                wn_bc = sb("D_wn", [128, D], F32)
                xT = sb("D_xT", [128, D // 128, TMAX], BF16)
                wg = [sb(f"D_wg{i}", [128, D // 128, 512], BF16) for i in range(2)]
                wu = [sb(f"D_wu{i}", [128, D // 128, 512], BF16) for i in range(2)]
                sl_t = [sb(f"D_sl{i}", [128, 512], F32) for i in range(2)]
                pst = [ps(f"D_pst{i}", [128, D // 128, 128], BF16) for i in range(2)]
                psg = [ps(f"D_psg{i}", [128, 512], F32) for i in range(2)]
                psu = [ps(f"D_psu{i}", [128, 512], F32) for i in range(2)]
                R.dma("sp", "D_wn", dict(out=wn_bc[:], in_=dr["norm_ffn_pre"][li:li + 1, :].partition_broadcast(128)), writes=[("D", "wn")])
                offs = norm_transpose_stage(R, nc, st, cfg, tl, H, wn_bc, xT, ident, "D", pst)
                ntok = sum(n for _, _, n in offs)
                xkeys = [("D", "xT", j) for j in range(len(tl))]
                it = 0
                for sl in range(FFN // 512):
                    b2 = sl % 2
                    c0 = sl * 512
                    R.dma("pool", f"D_wg{b2}", dict(out=wg[b2][:], in_=dr["ffn_w_gate"][li, :, c0:c0 + 512].rearrange("(k p) n -> p k n", p=128)), writes=[("D", "wg", b2)])
                    R.dma("pool", f"D_wu{b2}", dict(out=wu[b2][:], in_=dr["ffn_w_up"][li, :, c0:c0 + 512].rearrange("(k p) n -> p k n", p=128)), writes=[("D", "wu", b2)])
                    for q in range(4):
                        for tg in range(0, ntok, 512):
                            tw = min(512, ntok - tg)
                            b = it % 2
                            it += 1
                            R.op("pe", mm_group(psg[b][:, :tw], [(wg[b2][:, k, q * 128:(q + 1) * 128], xT[:, k, tg:tg + tw]) for k in range(D // 128)]),
                                 reads=[("D", "wg", b2)] + xkeys, writes=[("D", "psg", b)])
                            R.op("pe", mm_group(psu[b][:, :tw], [(wu[b2][:, k, q * 128:(q + 1) * 128], xT[:, k, tg:tg + tw]) for k in range(D // 128)]),
                                 reads=[("D", "wu", b2)] + xkeys, writes=[("D", "psu", b)])
                            R.op("act", [("activation", dict(out=sl_t[b][:, :tw], in_=psg[b][:, :tw], func=AF.Silu))], reads=[("D", "psg", b)], writes=[("D", "sl", b)])
                            R.op("dve", [("tensor_tensor", dict(out=aT[:, sl * 4 + q, tg:tg + tw], in0=psu[b][:, :tw], in1=sl_t[b][:, :tw], op=ALU.mult))],
                                 reads=[("D", "psu", b), ("D", "sl", b)], writes=[("D", "aT")])
            R.barrier()
            with ExitStack() as es:
                sb = lambda name, shape, dt: es.enter_context(nc.sbuf_tensor(uq(name), shape, dt))
                ps = lambda name, shape, dt: es.enter_context(nc.psum_tensor(uq(name), shape, dt))
                CW = 256
                wd = [sb(f"D_wd{i}", [128, KF, CW], BF16) for i in range(2)]
                fst = sb("D_f", [128, TP + 1, D], F32)
                wn_bc = sb("D_wn2", [128, D], F32)
                st = dict(junk=sb("D_junk2", [128, D], BF16), ss=sb("D_ss2", [128, 1], F32), ms=sb("D_ms2", [128, 1], F32),
                          rt=sb("D_rt2", [128, 1], F32), rstd=sb("D_rstd2", [128, 1], F32))
                hb = [sb(f"D_hb{i}", [128, D], F32) for i in range(2)]
                psf = [ps(f"D_psf{i}", [128, 512], F32) for i in range(4)]
                R.dma("sp", "D_wn2", dict(out=wn_bc[:], in_=dr["norm_ffn_post"][li:li + 1, :].partition_broadcast(128)), writes=[("D", "wn2")])
                it = 0
                for sl in range(D // CW):
                    b2 = sl % 2
                    c0 = sl * CW
                    for hh in range(2):
                        R.dma("pool", f"D_wd{b2}", dict(out=wd[b2][:, hh * 22:(hh + 1) * 22, :],
                              in_=dr["ffn_w_down"][li, hh * 22 * 128:(hh + 1) * 22 * 128, c0:c0 + CW].rearrange("(k p) n -> p k n", p=128)), writes=[("D", "wd", b2)])
                    for j, (t, off, n) in enumerate(offs):
                        b = it % 4
                        it += 1
                        R.op("pe", mm_group(psf[b][:n, :CW], [(aT[:, k, off:off + n], wd[b2][:, k, :]) for k in range(KF)]),
                             reads=[("D", "aT"), ("D", "wd", b2)], writes=[("D", "psf", b)])
                        if it % 2 == 0:
                            R.op("act", [("activation", dict(out=fst[:n, j, c0:c0 + CW], in_=psf[b][:n, :CW], func=AF.Copy))], reads=[("D", "psf", b)], writes=[("D", "f", j, sl)])
                        else:
                            R.op("dve", [("tensor_copy", dict(out=fst[:n, j, c0:c0 + CW], in_=psf[b][:n, :CW]))], reads=[("D", "psf", b)], writes=[("D", "f", j, sl)])
                for j, (t, off, n) in enumerate(offs):
                    b = j % 2
                    r0 = cfg.tiles[t][0]
                    fk = [("D", "f", j, sl) for sl in range(D // CW)]
                    R.dma("sp", f"D_hb{b}", dict(out=hb[b][:n], in_=H[r0:r0 + n, :]), reads=[("H", t)], writes=[("D", "hb", b)])
                    rstd_ops(R, st, "D3", fst[:n, j, :], n, fk)
                    R.op("dve", [("scalar_tensor_tensor", dict(out=fst[:n, j, :], in0=fst[:n, j, :], scalar=st["rstd"][:n], in1=wn_bc[:n], op0=ALU.mult, op1=ALU.mult))],
                         reads=fk + [("D3", "rstd"), ("D", "wn2")], writes=[("D", "f2", j)])
                    R.op("pool", [("tensor_tensor", dict(out=hb[b][:n], in0=hb[b][:n], in1=fst[:n, j, :], op=ALU.add))],
                         reads=[("D", "f2", j), ("D", "hb", b)], writes=[("D", "hb", b)])
                    R.dma("sp", f"D_hs{b}", dict(out=H[r0:r0 + n, :], in_=hb[b][:n]), reads=[("D", "hb", b)], writes=[("H", t)])
            R.barrier()


AX = mybir.AxisListType


def bc(ap, shape):
    return ap.to_broadcast(list(shape))


def phase_B_ret(R, nc, cfg, li, dr, ident):
    P, YT = dr["P_ret"], dr["YT"]
    Hh, Dh = RET_H, RET_D
    with ExitStack() as es:
        sb = lambda name, shape, dt: es.enter_context(nc.sbuf_tensor(uq(name), shape, dt))
        ps = lambda name, shape, dt: es.enter_context(nc.psum_tensor(uq(name), shape, dt))
        maskT = sb("R_maskT", [128, Hh, 128], F32)
        vec = sb("R_vec", [128, 3, Hh], F32)
        cdec = sb("R_cdec", [128, Hh, Dh], F32)
        S = sb("R_S", [128, Hh, Dh], F32)
        S_bf = sb("R_Sbf", [128, Hh, Dh], BF16)
        pr = [sb(f"R_pr{i}", [128, 4 * RET_W], F32) for i in range(2)]
        rope = [sb(f"R_rope{i}", [128, 4, 64], F32) for i in range(2)]
        tt = [sb(f"R_t{i}", [128, Hh, 64], F32) for i in range(8)]
        qr = sb("R_qr", [128, Hh, Dh], BF16)
        kr = sb("R_kr", [128, Hh, Dh], BF16)
        kd = sb("R_kd", [128, Hh, Dh], BF16)
        v_bf = sb("R_vbf", [128, Hh, Dh], BF16)
        qkT = sb("R_qkT", [128, 2 * Hh, 128], BF16)
        sT = sb("R_sT", [128, Hh, 128], BF16)
        yo = sb("R_yo", [128, Hh, Dh], F32)
        y = sb("R_y", [128, Hh, Dh], F32)
        ysq = sb("R_ysq", [128, Hh, Dh], F32)
        sgl = sb("R_sg", [128, Hh, Dh], F32)
        sm = [sb(f"R_sm{i}", [128, Hh], F32) for i in range(4)]
        yb = sb("R_yb", [128, Hh * Dh], BF16)
        yTo = sb("R_yTo", [128, Hh, 128], BF16)
        psT = ps("R_psT", [128, 2 * Hh, 128], BF16)
        ps_sc = ps("R_pssc", [128, Hh, 128], F32)
        ps_yi = ps("R_psyi", [128, Hh, Dh], F32)
        ps_yo = ps("R_psyo", [128, Hh, Dh], F32)

        R.dma("sp", "R_maskT", dict(out=maskT[:], in_=dr["ret_maskT"][:, :, :]), writes=["R_maskT"])
        R.dma("sp", "R_vec", dict(out=vec[:], in_=dr["ret_vec"][:, :, :]), writes=["R_vec"])
        for h in range(Hh):
            g = 1.0 - 2.0 ** (-5.0 - h)
            R.op("pool", [("memset", dict(ap=cdec[:, h, :], constant=float(g ** CH)))], writes=["R_cdec"])
        R.op("pool", [("memset", dict(ap=S[:], constant=0.0))], writes=["R_S"])
        R.op("pool", [("memset", dict(ap=S_bf[:], constant=0.0))], writes=["R_Sbf"])

        def r_loads(t):
            r0, n = cfg.tiles[t]
            b = t % 2
            R.dma("sp", f"R_pr{b}", dict(out=pr[b][:n], in_=P[r0:r0 + n, O_RQ:O_RQ + 4 * RET_W]), reads=[("P", t)], writes=[("R_pr", b)])
            R.dma("sp", f"R_rope{b}", dict(out=rope[b][:n], in_=dr["rope"][r0:r0 + n, :, :]), writes=[("R_rope", b)])
        r_loads(0)
        for t, (r0, n) in enumerate(cfg.tiles):
            b = t % 2
            if t + 1 < len(cfg.tiles):
                r_loads(t + 1)
            prv = pr[b]
            for which, (src0, dst, ci, si, e1, e2) in enumerate(((0, qr, 0, 1, "dve", "pool"), (RET_W, kr, 2, 3, "pool", "dve"))):
                x = prv[:n, src0:src0 + RET_W].rearrange("p (h t d) -> p h t d", h=Hh, t=2)
                x1, x2 = x[:, :, 0, :], x[:, :, 1, :]
                cs_ = bc(rope[b][:n, ci:ci + 1, :], (n, Hh, 64))
                sn_ = bc(rope[b][:n, si:si + 1, :], (n, Hh, 64))
                o = dst[:n].rearrange("p h (t d) -> p h t d", t=2)
                T = tt[4 * which:4 * which + 4]
                tk = [("R_t", 4 * which + i) for i in range(4)]
                rd = [("R_pr", b), ("R_rope", b)]
                R.op(e1, [("tensor_tensor", dict(out=T[0][:n], in0=x1, in1=cs_, op=ALU.mult))], reads=rd, writes=[tk[0]])
                R.op(e2, [("tensor_tensor", dict(out=T[1][:n], in0=x2, in1=sn_, op=ALU.mult))], reads=rd, writes=[tk[1]])
                R.op(e1, [("tensor_tensor", dict(out=T[2][:n], in0=x1, in1=sn_, op=ALU.mult))], reads=rd, writes=[tk[2]])
                R.op(e2, [("tensor_tensor", dict(out=T[3][:n], in0=x2, in1=cs_, op=ALU.mult))], reads=rd, writes=[tk[3]])
                R.op(e1, [("tensor_tensor", dict(out=o[:, :, 0, :], in0=T[0][:n], in1=T[1][:n], op=ALU.subtract))], reads=[tk[0], tk[1]], writes=[("R_rot", which, 0)])
                R.op(e2, [("tensor_tensor", dict(out=o[:, :, 1, :], in0=T[2][:n], in1=T[3][:n], op=ALU.add))], reads=[tk[2], tk[3]], writes=[("R_rot", which, 1)])
            qk_keys = [("R_rot", w_, i_) for w_ in range(2) for i_ in range(2)]
            kcol = 0 if n == CH else 1
            R.op("dve", [("tensor_tensor", dict(out=kd[:n], in0=kr[:n], in1=bc(vec[:n, kcol, :].unsqueeze(2), (n, Hh, Dh)), op=ALU.mult))],
                 reads=qk_keys + ["R_vec"], writes=["R_kd"])
            R.op("act", [("activation", dict(out=v_bf[:n], in_=prv[:n, O_RV:O_RV + RET_W].rearrange("p (h d) -> p h d", h=Hh), func=AF.Copy))],
                 reads=[("R_pr", b)], writes=["R_vbf"])
            R.op("pe", [("transpose", dict(out=psT[:, h, :n], in_=qr[:n, h, :], identity=ident[:n, :n])) for h in range(Hh)] +
                       [("transpose", dict(out=psT[:, Hh + h, :n], in_=kr[:n, h, :], identity=ident[:n, :n])) for h in range(Hh)],
                 reads=qk_keys + ["ident"], writes=["R_psT"])
            R.op("dve", [("tensor_copy", dict(out=qkT[:, :, :n], in_=psT[:, :, :n]))], reads=["R_psT"], writes=["R_qkT"])
            R.op("pe", [("matmul", dict(out=ps_sc[:n, h, :n], lhsT=qkT[:, Hh + h, :n], rhs=qkT[:, h, :n], start=True, stop=True)) for h in range(Hh)],
                 reads=["R_qkT"], writes=["R_pssc"])
            R.op("dve", [("tensor_tensor", dict(out=sT[:n, :, :n], in0=ps_sc[:n, :, :n], in1=maskT[:n, :, :n], op=ALU.mult))],
                 reads=["R_pssc", "R_maskT"], writes=["R_sT"])
            R.op("pe", [("matmul", dict(out=ps_yi[:n, h, :], lhsT=sT[:n, h, :n], rhs=v_bf[:n, h, :], start=True, stop=True)) for h in range(Hh)],
                 reads=["R_sT", "R_vbf"], writes=["R_psyi"])
            R.op("pe", [("matmul", dict(out=ps_yo[:n, h, :], lhsT=qkT[:, h, :n], rhs=S_bf[:, h, :], start=True, stop=True)) for h in range(Hh)],
                 reads=["R_qkT", "R_Sbf"], writes=["R_psyo"])
            R.op("dve", [("tensor_tensor", dict(out=yo[:n], in0=ps_yo[:n], in1=bc(vec[:n, 2, :].unsqueeze(2), (n, Hh, Dh)), op=ALU.mult))],
                 reads=["R_psyo", "R_vec"], writes=["R_yo"])
            R.op("dve", [("tensor_tensor", dict(out=y[:n], in0=ps_yi[:n], in1=yo[:n], op=ALU.add))], reads=["R_psyi", "R_yo"], writes=["R_y"])
            R.op("pe", [("matmul", dict(out=ps_sc[:, h, :], lhsT=kd[:n, h, :], rhs=v_bf[:n, h, :], start=True, stop=True)) for h in range(Hh)],
                 reads=["R_kd", "R_vbf"], writes=["R_pssc"])
            R.op("dve", [("tensor_tensor", dict(out=S[:], in0=S[:], in1=cdec[:], op=ALU.mult))], reads=["R_cdec"], writes=["R_S"])
            R.op("dve", [("tensor_tensor", dict(out=S[:], in0=S[:], in1=ps_sc[:], op=ALU.add))], reads=["R_pssc"], writes=["R_S"])
            R.op("act", [("activation", dict(out=S_bf[:], in_=S[:], func=AF.Copy))], reads=["R_S"], writes=["R_Sbf"])
            R.op("dve", [("tensor_tensor", dict(out=ysq[:n], in0=y[:n], in1=y[:n], op=ALU.mult))], reads=["R_y"], writes=["R_ysq"])
            R.op("dve", [("tensor_reduce", dict(out=sm[0][:n], in_=ysq[:n], axis=AX.X, op=ALU.add))], reads=["R_ysq"], writes=[("R_sm", 0)])
            R.op("dve", [("tensor_scalar", dict(out=sm[1][:n], in0=sm[0][:n], scalar1=1.0 / Dh, scalar2=EPS, op0=ALU.mult, op1=ALU.add))],
                 reads=[("R_sm", 0)], writes=[("R_sm", 1)])
            R.op("act", [("activation", dict(out=sm[2][:n], in_=sm[1][:n], func=AF.Sqrt))], reads=[("R_sm", 1)], writes=[("R_sm", 2)])
            R.op("dve", [("reciprocal", dict(out=sm[3][:n], in_=sm[2][:n]))], reads=[("R_sm", 2)], writes=[("R_sm", 3)])
            R.op("act", [("activation", dict(out=sgl[:n], in_=prv[:n, O_RG:O_RG + RET_W].rearrange("p (h d) -> p h d", h=Hh), func=AF.Silu))],
                 reads=[("R_pr", b)], writes=["R_sg"])
            R.op("dve", [("tensor_tensor", dict(out=y[:n], in0=y[:n], in1=bc(sm[3][:n].unsqueeze(2), (n, Hh, Dh)), op=ALU.mult))],
                 reads=[("R_sm", 3)], writes=["R_y"])
            R.op("dve", [("tensor_tensor", dict(out=yb[:n].rearrange("p (h d) -> p h d", h=Hh), in0=y[:n], in1=sgl[:n], op=ALU.mult))],
                 reads=["R_y", "R_sg"], writes=["R_yb"])
            R.op("pe", [("transpose", dict(out=psT[:, h, :n], in_=yb[:n, h * 128:(h + 1) * 128], identity=ident[:n, :n])) for h in range(Hh)],
                 reads=["R_yb", "ident"], writes=["R_psT"])
            R.op("act", [("activation", dict(out=yTo[:, :, :n], in_=psT[:, 0:Hh, :n], func=AF.Copy))], reads=["R_psT"], writes=["R_yTo"])
            R.dma("sp", "R_yTo", dict(out=YT[0:RET_W, r0:r0 + n].rearrange("(k p) t -> p k t", p=128), in_=yTo[:, :, :n]),
                  reads=["R_yTo"], writes=[("YT", t)])
    R.barrier()


def phase_B_ssd(R, nc, cfg, li, dr, ident):
    P, YT = dr["P_ssd"], dr["YT"]
    O_Z, O_XBC, O_DT = 0, SSD_W, SSD_W + SSD_CONV
    NH, HP, G = SSD_H, SSD_P, SSD_G
    with ExitStack() as es:
        sb = lambda name, shape, dt: es.enter_context(nc.sbuf_tensor(uq(name), shape, dt))
        ps = lambda name, shape, dt: es.enter_context(nc.psum_tensor(uq(name), shape, dt))
        wconv = sb("S_wconv", [128, 4, SSD_CONV], F32)
        convb = sb("S_convb", [128, SSD_CONV], F32)
        normw = sb("S_normw", [128, SSD_W], F32)
        hv = sb("S_hv", [128, 3, NH], F32)
        tri = sb("S_tri", [128, 3, 128], F32)
        SLb = sb("S_SLb", [128, 128], BF16)
        ST = sb("S_ST", [128, NH, HP], F32)
        ST_bf = sb("S_STbf", [128, NH, HP], BF16)
        xs = [sb(f"S_xs{i}", [128, 4, 1024], F32) for i in range(2)]
        z = sb("S_z", [128, SSD_W], F32)
        dtr = sb("S_dtr", [128, NH], F32)
        xa = sb("S_xa", [128, SSD_CONV], F32)
        Bb = sb("S_Bb", [128, 512], BF16)
        Cb = sb("S_Cb", [128, 512], BF16)
        sm = [sb(f"S_sm{i}", [128, NH], F32) for i in range(8)]
        cst = sb("S_cst", [128, 2 * NH], F32)
        g4 = [sb(f"S_g4{i}", [128, G], F32) for i in range(4)]
        xdt = sb("S_xdt", [128, SSD_W], BF16)
        xdd = sb("S_xdd", [128, SSD_W], BF16)
        bcT = sb("S_bcT", [128, 8, 128], BF16)
        cbm = sb("S_cbm", [128, G, 128], F32)
        rseg = sb("S_rseg", [128, 16, 128], BF16)
        ed = sb("S_ed", [128, 16, 128], BF16)
        scT = sb("S_scT", [128, NH, 128], BF16)
        t1 = sb("S_t1", [128, SSD_W], F32)
        y = sb("S_y", [128, SSD_W], F32)
        yb = sb("S_yb", [128, SSD_W], BF16)
        yTo = sb("S_yTo", [128, 16, 128], BF16)
        psT = ps("S_psT", [128, 8, 128], BF16)
        ps_cb = ps("S_pscb", [128, G, 128], F32)
        psA = ps("S_psA", [128, SSD_W], F32)
        psB = ps("S_psB", [128, 1024], F32)

        R.dma("sp", "S_wconv", dict(out=wconv[:].rearrange("p k c -> p (k c)"), in_=dr["ssd_conv_w"][li:li + 1].rearrange("o k c -> o (k c)").partition_broadcast(128)), writes=["S_wconv"])
        R.dma("sp", "S_convb", dict(out=convb[:], in_=dr["ssd_conv_b"][li:li + 1, :].partition_broadcast(128)), writes=["S_convb"])
        R.dma("sp", "S_normw", dict(out=normw[:], in_=dr["ssd_norm_w"][li:li + 1, :].partition_broadcast(128)), writes=["S_normw"])
        for i, nm in enumerate(("ssd_dt_bias", "ssd_a_log", "ssd_d")):
            R.dma("sp", "S_hv", dict(out=hv[:, i, :], in_=dr[nm][li:li + 1, :].partition_broadcast(128)), writes=["S_hv"])
        R.dma("sp", "S_tri", dict(out=tri[:], in_=dr["tri"][:, :, :]), writes=["S_tri"])
        R.op("act", [("activation", dict(out=hv[:, 1, :], in_=hv[:, 1, :], func=AF.Exp))], reads=["S_hv"], writes=["S_hv"])
        R.op("dve", [("tensor_scalar", dict(out=hv[:, 1, :], in0=hv[:, 1, :], scalar1=-1.0, scalar2=None, op0=ALU.mult))], reads=["S_hv"], writes=["S_hv"])
        R.op("dve", [("tensor_copy", dict(out=SLb[:], in_=tri[:, 1, :]))], reads=["S_tri"], writes=["S_SLb"])
        R.op("pool", [("memset", dict(ap=ST[:], constant=0.0))], writes=["S_ST"])
        R.op("pool", [("memset", dict(ap=ST_bf[:], constant=0.0))], writes=["S_STbf"])
        U = tri[:, 0, :]
        ONES = tri[:, 2, :]
        xi = 0
        for t, (r0, n) in enumerate(cfg.tiles):
            pk = [("P", t)] + ([("P", t - 1)] if t > 0 else [])
            R.dma("pool", "S_z", dict(out=z[:n], in_=P[r0:r0 + n, O_Z:O_Z + SSD_W]), reads=pk, writes=["S_z"])
            R.dma("pool", "S_dtr", dict(out=dtr[:n], in_=P[r0:r0 + n, O_DT:O_DT + NH]), reads=pk, writes=["S_dtr"])

            def xs_loads(tt_, blk_):
                rr0, nn = cfg.tiles[tt_]
                gi = 3 * tt_ + blk_
                xb_ = xs[gi % 2]
                xk_ = ("S_xs", gi % 2)
                cc0 = O_XBC + blk_ * 1024
                pk_ = [("P", tt_)] + ([("P", tt_ - 1)] if tt_ > 0 else [])
                if tt_ == 0:
                    R.op("pool", [("memset", dict(ap=xb_[:nn].rearrange("p k c -> p (k c)"), constant=0.0))], writes=[xk_])
                for k in range(4):
                    sh = 3 - k
                    lo = max(0, sh - rr0)
                    R.dma("sp", f"S_xs{gi % 2}", dict(out=xb_[lo:nn, k, :], in_=P[rr0 - sh + lo:rr0 - sh + nn, cc0:cc0 + 1024]), reads=pk_, writes=[xk_])
            if t == 0:
                xs_loads(0, 0)
            for blk in range(3):
                xb = xs[xi % 2]
                xk = ("S_xs", xi % 2)
                xi += 1
                c0 = O_XBC + blk * 1024
                if blk < 2:
                    xs_loads(t, blk + 1)
                elif t + 1 < len(cfg.tiles):
                    xs_loads(t + 1, 0)
                e = ["pool", "dve"]
                for k in range(4):
                    R.op(e[k % 2], [("tensor_tensor", dict(out=xb[:n, k, :], in0=xb[:n, k, :], in1=wconv[:n, k, blk * 1024:(blk + 1) * 1024], op=ALU.mult))],
                         reads=["S_wconv"], writes=[xk])
                R.op("dve", [("tensor_tensor", dict(out=xb[:n, 0, :], in0=xb[:n, 0, :], in1=xb[:n, 1, :], op=ALU.add))], writes=[xk])
                R.op("dve", [("tensor_tensor", dict(out=xb[:n, 2, :], in0=xb[:n, 2, :], in1=xb[:n, 3, :], op=ALU.add))], writes=[xk])
                R.op("dve", [("tensor_tensor", dict(out=xb[:n, 0, :], in0=xb[:n, 0, :], in1=convb[:n, blk * 1024:(blk + 1) * 1024], op=ALU.add))], reads=["S_convb"], writes=[xk])
                R.op("dve", [("tensor_tensor", dict(out=xb[:n, 0, :], in0=xb[:n, 0, :], in1=xb[:n, 2, :], op=ALU.add))], writes=[xk])
                R.op("act", [("activation", dict(out=xa[:n, blk * 1024:(blk + 1) * 1024], in_=xb[:n, 0, :], func=AF.Silu))], reads=[xk], writes=[("S_xa", blk)])
            xak = [("S_xa", i) for i in range(3)]
            R.op("act", [("activation", dict(out=Bb[:n], in_=xa[:n, 2048:2560], func=AF.Copy))], reads=xak, writes=["S_Bb"])
            R.op("dve", [("tensor_copy", dict(out=Cb[:n], in_=xa[:n, 2560:3072]))], reads=xak, writes=["S_Cb"])
            R.op("dve", [("tensor_tensor", dict(out=sm[0][:n], in0=dtr[:n], in1=hv[:n, 0, :], op=ALU.add))], reads=["S_dtr", "S_hv"], writes=[("S_sm", 0)])
            R.op("act", [("activation", dict(out=sm[1][:n], in_=sm[0][:n], func=AF.Exp))], reads=[("S_sm", 0)], writes=[("S_sm", 1)])
            R.op("act", [("activation", dict(out=sm[2][:n], in_=sm[1][:n], func=AF.Ln, bias=1.0))], reads=[("S_sm", 1)], writes=[("S_sm", 2)])
            R.op("dve", [("tensor_tensor", dict(out=sm[3][:n], in0=sm[2][:n], in1=hv[:n, 1, :], op=ALU.mult))], reads=[("S_sm", 2), "S_hv"], writes=[("S_sm", 3)])
            R.op("pe", [("matmul", dict(out=psB[:n, 0:NH], lhsT=U[:n, :n], rhs=sm[3][:n], start=True, stop=True)),
                        ("matmul", dict(out=psB[:n, NH:2 * NH], lhsT=ONES[:n, :n], rhs=sm[3][:n], start=True, stop=True))],
                 reads=["S_tri", ("S_sm", 3)], writes=["S_psB"])
            R.op("act", [("activation", dict(out=cst[:n], in_=psB[:n, 0:2 * NH], func=AF.Copy))], reads=["S_psB"], writes=["S_cst"])
            R.op("act", [("activation", dict(out=sm[4][:n], in_=cst[:n, 0:NH], func=AF.Exp))], reads=["S_cst"], writes=[("S_sm", 4)])
            R.op("dve", [("tensor_tensor", dict(out=sm[5][:n], in0=cst[:n, NH:2 * NH], in1=cst[:n, 0:NH], op=ALU.subtract))], reads=["S_cst"], writes=[("S_sm", 5)])
            R.op("act", [("activation", dict(out=sm[5][:n], in_=sm[5][:n], func=AF.Exp))], reads=[("S_sm", 5)], writes=[("S_sm", 5)])
            R.op("act", [("activation", dict(out=sm[6][:n], in_=cst[:n, NH:2 * NH], func=AF.Exp))], reads=["S_cst"], writes=[("S_sm", 6)])
            R.op("dve", [("tensor_tensor", dict(out=sm[7][:n], in0=sm[2][:n], in1=sm[5][:n], op=ALU.mult))], reads=[("S_sm", 2), ("S_sm", 5)], writes=[("S_sm", 7)])
            x3 = xa[:n, 0:SSD_W].rearrange("p (h d) -> p h d", h=NH)
            R.op("dve", [("tensor_tensor", dict(out=xdt[:n].rearrange("p (h d) -> p h d", h=NH), in0=x3, in1=bc(sm[2][:n].unsqueeze(2), (n, NH, HP)), op=ALU.mult))],
                 reads=xak + [("S_sm", 2)], writes=["S_xdt"])
            R.op("dve", [("tensor_tensor", dict(out=xdd[:n].rearrange("p (h d) -> p h d", h=NH), in0=x3, in1=bc(sm[7][:n].unsqueeze(2), (n, NH, HP)), op=ALU.mult))],
                 reads=xak + [("S_sm", 7)], writes=["S_xdd"])
            R.op("pe", [("transpose", dict(out=psT[:, g, :n], in_=Bb[:n, g * 128:(g + 1) * 128], identity=ident[:n, :n])) for g in range(G)] +
                       [("transpose", dict(out=psT[:, G + g, :n], in_=Cb[:n, g * 128:(g + 1) * 128], identity=ident[:n, :n])) for g in range(G)],
                 reads=["S_Bb", "S_Cb", "ident"], writes=["S_psT"])
            R.op("dve", [("tensor_copy", dict(out=bcT[:, :, :n], in_=psT[:, :, :n]))], reads=["S_psT"], writes=["S_bcT"])
            R.op("pe", [("matmul", dict(out=ps_cb[:n, g, :n], lhsT=bcT[:, g, :n], rhs=bcT[:, G + g, :n], start=True, stop=True)) for g in range(G)],
                 reads=["S_bcT"], writes=["S_pscb"])
            R.op("dve", [("tensor_tensor", dict(out=cbm[:n, :, :n], in0=ps_cb[:n, :, :n], in1=bc(U[:n, :n].unsqueeze(1), (n, G, n)), op=ALU.mult))],
                 reads=["S_pscb", "S_tri"], writes=["S_cbm"])
            for half in range(2):
                h0 = half * 16
                R.op("dve", [("tensor_tensor", dict(out=rseg[:n, :, :n], in0=bc(sm[3][:n, h0:h0 + 16].unsqueeze(2), (n, 16, n)), in1=bc(U[:n, :n].unsqueeze(1), (n, 16, n)), op=ALU.mult))],
                     reads=[("S_sm", 3), "S_tri"], writes=["S_rseg"])
                R.op("pe", [("matmul", dict(out=psA[:n, q * 512:q * 512 + 4 * n].rearrange("p (h i) -> p h i", h=4), lhsT=SLb[:n, :n], rhs=rseg[:n, 4 * q:4 * q + 4, :n], start=True, stop=True)) for q in range(4)],
                     reads=["S_rseg", "S_SLb"], writes=["S_psA"])
                for q in range(4):
                    R.op("act", [("activation", dict(out=ed[:n, 4 * q:4 * q + 4, :n], in_=psA[:n, q * 512:q * 512 + 4 * n].rearrange("p (h i) -> p h i", h=4), func=AF.Exp))],
                         reads=["S_psA"], writes=[("S_ed", q)])
                R.op("dve", [("tensor_tensor", dict(out=scT[:n, h0:h0 + 16, :n].rearrange("p (g r) i -> p g r i", g=2),
                                                    in0=ed[:n, :, :n].rearrange("p (g r) i -> p g r i", g=2),
                                                    in1=bc(cbm[:n, 2 * half:2 * half + 2, :n].unsqueeze(2), (n, 2, 8, n)), op=ALU.mult))],
                     reads=[("S_ed", q) for q in range(4)] + ["S_cbm"], writes=[("S_scT", half)])
            R.op("pe", [("matmul", dict(out=psA[:n, h * HP:(h + 1) * HP], lhsT=scT[:n, h, :n], rhs=xdt[:n, h * HP:(h + 1) * HP], start=True, stop=True)) for h in range(NH)],
                 reads=[("S_scT", 0), ("S_scT", 1), "S_xdt"], writes=["S_psA"])
            for half in range(2):
                R.op("pe", [("matmul", dict(out=psB[:n, gg * 512:(gg + 1) * 512], lhsT=bcT[:, G + 2 * half + gg, :n],
                                            rhs=ST_bf[:, (2 * half + gg) * 8:(2 * half + gg + 1) * 8, :].rearrange("p h d -> p (h d)"), start=True, stop=True)) for gg in range(2)],
                     reads=["S_bcT", "S_STbf"], writes=["S_psB"])
                R.op("dve", [("tensor_tensor", dict(out=t1[:n, half * 1024:(half + 1) * 1024].rearrange("p (h d) -> p h d", h=16),
                                                    in0=psB[:n, :].rearrange("p (h d) -> p h d", h=16),
                                                    in1=bc(sm[4][:n, half * 16:(half + 1) * 16].unsqueeze(2), (n, 16, HP)), op=ALU.mult))],
                     reads=["S_psB", ("S_sm", 4)], writes=[("S_t1", half)])
            R.op("dve", [("tensor_tensor", dict(out=y[:n], in0=psA[:n, :], in1=t1[:n], op=ALU.add))], reads=["S_psA", ("S_t1", 0), ("S_t1", 1)], writes=["S_y"])
            R.op("pe", [("matmul", dict(out=psA[:, g * 512:(g + 1) * 512], lhsT=Bb[:n, g * 128:(g + 1) * 128], rhs=xdd[:n, g * 512:(g + 1) * 512], start=True, stop=True)) for g in range(G)],
                 reads=["S_Bb", "S_xdd"], writes=["S_psA"])
            if n == CH:
                R.op("dve", [("tensor_tensor", dict(out=ST[:], in0=ST[:], in1=bc(sm[6][:, :].unsqueeze(2), (128, NH, HP)), op=ALU.mult))],
                     reads=[("S_sm", 6)], writes=["S_ST"])
            R.op("dve", [("tensor_tensor", dict(out=ST[:].rearrange("p h d -> p (h d)"), in0=ST[:].rearrange("p h d -> p (h d)"), in1=psA[:, :], op=ALU.add))],
                 reads=["S_psA"], writes=["S_ST"])
            R.op("act", [("activation", dict(out=ST_bf[:], in_=ST[:], func=AF.Copy))], reads=["S_ST"], writes=["S_STbf"])
            R.op("dve", [("tensor_tensor", dict(out=x3, in0=x3, in1=bc(hv[:n, 2, :].unsqueeze(2), (n, NH, HP)), op=ALU.mult))], reads=["S_hv", "S_xdt", "S_xdd"], writes=xak)
            R.op("dve", [("tensor_tensor", dict(out=y[:n], in0=y[:n], in1=xa[:n, 0:SSD_W], op=ALU.add))], reads=xak, writes=["S_y"])
            R.op("act", [("activation", dict(out=z[:n], in_=z[:n], func=AF.Silu))], writes=["S_z"])
            R.op("dve", [("tensor_tensor", dict(out=y[:n], in0=y[:n], in1=z[:n], op=ALU.mult))], reads=["S_z"], writes=["S_y"])
            R.op("dve", [("tensor_tensor", dict(out=t1[:n], in0=y[:n], in1=y[:n], op=ALU.mult))], reads=["S_y"], writes=[("S_t1", 0), ("S_t1", 1)])
            R.op("dve", [("tensor_reduce", dict(out=g4[0][:n], in_=t1[:n].rearrange("p (g d) -> p g d", g=G), axis=AX.X, op=ALU.add))], reads=[("S_t1", 0), ("S_t1", 1)], writes=[("S_g4", 0)])
            R.op("dve", [("tensor_scalar", dict(out=g4[1][:n], in0=g4[0][:n], scalar1=1.0 / (SSD_W // G), scalar2=EPS, op0=ALU.mult, op1=ALU.add))], reads=[("S_g4", 0)], writes=[("S_g4", 1)])
            R.op("act", [("activation", dict(out=g4[2][:n], in_=g4[1][:n], func=AF.Sqrt))], reads=[("S_g4", 1)], writes=[("S_g4", 2)])
            R.op("dve", [("reciprocal", dict(out=g4[3][:n], in_=g4[2][:n]))], reads=[("S_g4", 2)], writes=[("S_g4", 3)])
            R.op("dve", [("tensor_tensor", dict(out=y[:n].rearrange("p (g d) -> p g d", g=G), in0=y[:n].rearrange("p (g d) -> p g d", g=G), in1=bc(g4[3][:n].unsqueeze(2), (n, G, SSD_W // G)), op=ALU.mult))],
                 reads=[("S_g4", 3)], writes=["S_y"])
            R.op("dve", [("tensor_tensor", dict(out=yb[:n], in0=y[:n], in1=normw[:n], op=ALU.mult))], reads=["S_y", "S_normw"], writes=["S_yb"])
            for half in range(2):
                R.op("pe", [("transpose", dict(out=psT[:, k, :n], in_=yb[:n, (half * 8 + k) * 128:(half * 8 + k + 1) * 128], identity=ident[:n, :n])) for k in range(8)],
                     reads=["S_yb", "ident"], writes=["S_psT"])
                R.op("act", [("activation", dict(out=yTo[:, half * 8:(half + 1) * 8, :n], in_=psT[:, :, :n], func=AF.Copy))], reads=["S_psT"], writes=[("S_yTo", half)])
            R.dma("sp", "S_yTo", dict(out=YT[2048:4096, r0:r0 + n].rearrange("(k p) t -> p k t", p=128), in_=yTo[:, :, :n]),
                  reads=[("S_yTo", 0), ("S_yTo", 1)], writes=[("YT", t)])
    R.barrier()


def phase_B_rwkv(R, nc, cfg, li, dr, ident, ident_f):
    P, YT = dr["P_rw"], dr["YT"]
    O_RW = 0
    NH, HN, W = RW_H, RW_N, RW_W
    C0 = float(np.exp(-0.5))
    with ExitStack() as es:
        sb = lambda name, shape, dt: es.enter_context(nc.sbuf_tensor(uq(name), shape, dt))
        ps = lambda name, shape, dt: es.enter_context(nc.psum_tensor(uq(name), shape, dt))
        mu = sb("W_mu", [128, RW_COLS], F32)
        vecs = sb("W_vecs", [128, 5, W], F32)
        w2a = sb("W_w2a", [128, W], F32)
        a2a = sb("W_a2a", [128, W], F32)
        g2 = sb("W_g2", [128, 2, W], F32)
        msk = sb("W_msk", [128, 5, 128], F32)
        ones = sb("W_ones", [128, 128], F32)
        Tst = sb("W_T", [128, 8, HN], F32)
        T_bf = sb("W_Tbf", [128, 8, HN], BF16)
        cols = sb("W_cols", [128, RW_COLS], F32)
        prev = sb("W_prev", [128, RW_COLS], F32)
        lin = sb("W_lin", [128, 288], F32)
        loT = sb("W_loT", [128, 4, 128], F32)
        F = [sb(f"W_F{i}", [128, W], F32) for i in range(10)] + [prev[:, 0:W], prev[:, W:2 * W]]
        Bq = [sb(f"W_B{i}", [128, W], BF16) for i in range(7)]
        fT2 = sb("W_fT2", [128, 2, 8, 128], BF16)
        rP = sb("W_rP", [128, NH, 128], BF16)
        kP = sb("W_kP", [128, NH, 128], BF16)
        Am = [sb(f"W_A{i}", [128, NH, 128], BF16) for i in range(3)]
        Pm = [sb(f"W_P{i}", [128, NH, 128], BF16) for i in range(2)]
        PTm = [sb(f"W_PT{i}", [128, NH, 128], BF16) for i in range(2)]
        MT = sb("W_MT", [128, NH, 128], BF16)
        s16 = [sb(f"W_s16_{i}", [128, NH], F32) for i in range(6)]
        WC = sb("W_WC", [128, 8], F32)
        yTo = sb("W_yTo", [128, 8, 128], BF16)
        X0 = ps("W_X0", [128, 512], F32)
        Y2 = ps("W_Y2", [128, W], F32)
        Z2 = ps("W_Z2", [128, W], F32)
        QQ = ps("W_QQ", [128, 8, 128], F32)

        def bcv(i, n):
            return vecs[:n, i, :]

        R.dma("sp", "W_mu", dict(out=mu[:], in_=dr["rwkv_mu"][li:li + 1, :].partition_broadcast(128)), writes=["W_mu"])
        for i, nm in enumerate(("rwkv_k_k", "rwkv_k_a", "rwkv_ln_w", "rwkv_ln_b")):
            R.dma("sp", "W_vecs", dict(out=vecs[:, i, :], in_=dr[nm][li:li + 1, :].partition_broadcast(128)), writes=["W_vecs"])
        R.dma("sp", "W_vecs", dict(out=vecs[:, 4, :], in_=dr["rwkv_r_k"][li:li + 1].rearrange("o h d -> o (h d)").partition_broadcast(128)), writes=["W_vecs"])
        R.dma("sp", "W_w2a", dict(out=w2a[0:64, :], in_=dr["rwkv_w2"][li, :, :]), writes=["W_w2a"])
        R.dma("sp", "W_w2a", dict(out=w2a[64:65, :], in_=dr["rwkv_w0"][li:li + 1, :]), writes=["W_w2a"])
        R.dma("sp", "W_a2a", dict(out=a2a[0:64, :], in_=dr["rwkv_a2"][li, :, :]), writes=["W_a2a"])
        R.dma("sp", "W_a2a", dict(out=a2a[64:65, :], in_=dr["rwkv_a0"][li:li + 1, :]), writes=["W_a2a"])
        R.dma("sp", "W_g2", dict(out=g2[:, 0, :], in_=dr["rwkv_g2"][li, 0:128, :]), writes=["W_g2"])
        R.dma("sp", "W_g2", dict(out=g2[0:32, 1, :], in_=dr["rwkv_g2"][li, 128:160, :]), writes=["W_g2"])
        R.dma("sp", "W_msk", dict(out=msk[:], in_=dr["rw_msk"][:, :, :]), writes=["W_msk"])
        R.op("pool", [("memset", dict(ap=ones[:], constant=1.0))], writes=["W_ones"])
        R.op("pool", [("memset", dict(ap=rP[:].rearrange("p a b -> p (a b)"), constant=0.0))], writes=[("W_fT", 0)])
        R.op("pool", [("memset", dict(ap=kP[:].rearrange("p a b -> p (a b)"), constant=0.0))], writes=[("W_fT", 0)])
        R.op("pool", [("memset", dict(ap=loT[:].rearrange("p a b -> p (a b)"), constant=1.0))], writes=["W_loT"])
        R.op("pool", [("memset", dict(ap=Tst[:].rearrange("p a b -> p (a b)"), constant=0.0))], writes=["W_T"])
        R.op("pool", [("memset", dict(ap=T_bf[:].rearrange("p a b -> p (a b)"), constant=0.0))], writes=["W_Tbf"])
        MU, MS, MSn, MLn, MI = (msk[:, i, :] for i in range(5))
        qi = 0
        for t, (r0, n) in enumerate(cfg.tiles):
            pk = [("P", t)] + ([("P", t - 1)] if t > 0 else [])
            R.countdown = None
            FK = [("W_F", i) for i in range(12)]
            def w_loads(tt_):
                rr0, nn = cfg.tiles[tt_]
                pk_ = [("P", tt_)] + ([("P", tt_ - 1)] if tt_ > 0 else [])
                R.dma("sp", "W_cols", dict(out=cols[:nn], in_=P[rr0:rr0 + nn, O_RW:O_RW + RW_COLS]), reads=pk_, writes=["W_cols"])
                if tt_ == 0:
                    R.op("pool", [("memset", dict(ap=prev[:nn], constant=0.0))], writes=["W_prev", FK[10], FK[11]])
                    R.dma("sp", "W_prev", dict(out=prev[1:nn], in_=P[0:nn - 1, O_RW:O_RW + RW_COLS]), reads=pk_, writes=["W_prev", FK[10], FK[11]])
                else:
                    R.dma("sp", "W_prev", dict(out=prev[:nn], in_=P[rr0 - 1:rr0 + nn - 1, O_RW:O_RW + RW_COLS]), reads=pk_, writes=["W_prev", FK[10], FK[11]])
            if t == 0:
                w_loads(0)
            R.op("pool", [("tensor_tensor", dict(out=prev[:n], in0=prev[:n], in1=cols[:n], op=ALU.subtract))], reads=["W_cols"], writes=["W_prev"])
            R.op("pool", [("tensor_tensor", dict(out=prev[:n], in0=prev[:n], in1=mu[:n], op=ALU.mult))], reads=["W_mu"], writes=["W_prev"])
            R.op("dve", [("tensor_tensor", dict(out=cols[:n], in0=cols[:n], in1=prev[:n], op=ALU.add))], reads=["W_prev"], writes=["W_cols"])
            if cfg.stop == "W1":
                continue
            r_, k_, v_ = cols[:n, 0:W], cols[:n, W:2 * W], cols[:n, 2 * W:3 * W]
            R.op("act", [("activation", dict(out=lin[:n, 0:64], in_=cols[:n, 3072:3136], func=AF.Tanh))], reads=["W_cols"], writes=[("W_lin", 0)])
            R.op("act", [("activation", dict(out=lin[:n, 64:128], in_=cols[:n, 3136:3200], func=AF.Copy))], reads=["W_cols"], writes=[("W_lin", 1)])
            R.op("act", [("activation", dict(out=lin[:n, 128:288], in_=cols[:n, 3200:3360], func=AF.Sigmoid))], reads=["W_cols"], writes=[("W_lin", 2)])
            X0t = X0[:].rearrange("p (a b) -> p a b", a=4)
            R.op("pe", [("transpose", dict(out=X0t[0:64, 0, :n], in_=lin[:n, 0:64], identity=ident_f[:n, :n])),
                        ("transpose", dict(out=X0t[0:64, 1, :n], in_=lin[:n, 64:128], identity=ident_f[:n, :n])),
                        ("transpose", dict(out=X0t[:, 2, :n], in_=lin[:n, 128:256], identity=ident_f[:n, :n])),
                        ("transpose", dict(out=X0t[0:32, 3, :n], in_=lin[:n, 256:288], identity=ident_f[:n, :n]))],
                 reads=[("W_lin", 0), ("W_lin", 1), ("W_lin", 2), "ident_f"], writes=["W_X0"])
            R.op("dve", [("tensor_copy", dict(out=loT[0:64, 0:2, :n], in_=X0t[0:64, 0:2, :n]))], reads=["W_X0"], writes=["W_loT"])
            R.op("dve", [("tensor_copy", dict(out=loT[:, 2, :n], in_=X0t[:, 2, :n]))], reads=["W_X0"], writes=["W_loT"])
            R.op("dve", [("tensor_copy", dict(out=loT[0:32, 3, :n], in_=X0t[0:32, 3, :n]))], reads=["W_X0"], writes=["W_loT"])
            if cfg.stop == "W2":
                continue
            e_, a_, g_ = F[0], F[1], F[2]
            R.op("pe", [("matmul", dict(out=Y2[:n, hh * 512:(hh + 1) * 512], lhsT=loT[0:65, 0, :n], rhs=w2a[0:65, hh * 512:(hh + 1) * 512], start=True, stop=True)) for hh in range(2)],
                 reads=["W_loT", "W_w2a"], writes=["W_Y2"])
            R.op("act", [("activation", dict(out=e_[:n], in_=Y2[:n, :], func=AF.Sigmoid))], reads=["W_Y2"], writes=[FK[0]])
            R.op("pe", [("matmul", dict(out=Z2[:n, hh * 512:(hh + 1) * 512], lhsT=loT[0:65, 1, :n], rhs=a2a[0:65, hh * 512:(hh + 1) * 512], start=True, stop=True)) for hh in range(2)],
                 reads=["W_loT", "W_a2a"], writes=["W_Z2"])
            R.op("act", [("activation", dict(out=a_[:n], in_=Z2[:n, :], func=AF.Sigmoid))], reads=["W_Z2"], writes=[FK[1]])
            R.op("pe", [m for hh in range(2) for m in (
                        ("matmul", dict(out=Y2[:n, hh * 512:(hh + 1) * 512], lhsT=loT[:, 2, :n], rhs=g2[:, 0, hh * 512:(hh + 1) * 512], start=True, stop=False)),
                        ("matmul", dict(out=Y2[:n, hh * 512:(hh + 1) * 512], lhsT=loT[0:32, 3, :n], rhs=g2[0:32, 1, hh * 512:(hh + 1) * 512], start=False, stop=True)))],
                 reads=["W_loT", "W_g2"], writes=["W_Y2"])
            R.op("act", [("activation", dict(out=g_[:n], in_=Y2[:n, :], func=AF.Copy))], reads=["W_Y2"], writes=[FK[2]])
            if cfg.stop == "W3":
                continue
            if cfg.stop.startswith("CUT"):
                R.countdown = int(cfg.stop[3:])
            kk_, k2_, b_, tmp_ = F[7], F[8], F[9], F[10]
            R.op("pool", [("tensor_tensor", dict(out=kk_[:n], in0=k_, in1=bcv(0, n), op=ALU.mult))], reads=["W_cols", "W_vecs"], writes=[FK[7]])
            R.op("pool", [("tensor_tensor", dict(out=tmp_[:n], in0=kk_[:n], in1=kk_[:n], op=ALU.mult))], reads=[FK[7]], writes=[FK[10]])
            R.op("dve", [("tensor_reduce", dict(out=s16[0][:n], in_=tmp_[:n].rearrange("p (h d) -> p h d", h=NH), axis=AX.X, op=ALU.add))], reads=[FK[10]], writes=[("W_s16", 0)])
            R.op("act", [("activation", dict(out=s16[0][:n], in_=s16[0][:n], func=AF.Sqrt))], reads=[("W_s16", 0)], writes=[("W_s16", 0)])
            R.op("dve", [("tensor_scalar", dict(out=s16[0][:n], in0=s16[0][:n], scalar1=1e-12, scalar2=None, op0=ALU.max))], reads=[("W_s16", 0)], writes=[("W_s16", 0)])
            R.op("dve", [("reciprocal", dict(out=s16[1][:n], in_=s16[0][:n]))], reads=[("W_s16", 0)], writes=[("W_s16", 1)])
            R.op("dve", [("tensor_tensor", dict(out=kk_[:n].rearrange("p (h d) -> p h d", h=NH), in0=kk_[:n].rearrange("p (h d) -> p h d", h=NH), in1=bc(s16[1][:n].unsqueeze(2), (n, NH, HN)), op=ALU.mult))],
                 reads=[("W_s16", 1)], writes=[FK[7]])
            R.op("dve", [("scalar_tensor_tensor", dict(out=tmp_[:n], in0=a_[:n], scalar=-1.0, in1=bcv(1, n), op0=ALU.add, op1=ALU.mult))], reads=[FK[1], "W_vecs"], writes=[FK[10]])
            R.op("dve", [("tensor_scalar", dict(out=tmp_[:n], in0=tmp_[:n], scalar1=1.0, scalar2=None, op0=ALU.add))], reads=[FK[10]], writes=[FK[10]])
            R.op("pool", [("tensor_tensor", dict(out=k2_[:n], in0=k_, in1=tmp_[:n], op=ALU.mult))], reads=["W_cols", FK[10]], writes=[FK[8]])
            R.op("pool", [("tensor_tensor", dict(out=b_[:n], in0=kk_[:n], in1=a_[:n], op=ALU.mult))], reads=[FK[7], FK[1]], writes=[FK[9]])
            bo_ = F[3]
            R.op("pool", [("tensor_tensor", dict(out=tmp_[:n], in0=r_, in1=k2_[:n], op=ALU.mult))], reads=["W_cols", FK[8]], writes=[FK[10]])
            R.op("pool", [("tensor_tensor", dict(out=tmp_[:n], in0=tmp_[:n], in1=bcv(4, n), op=ALU.mult))], reads=["W_vecs"], writes=[FK[10]])
            R.op("dve", [("tensor_reduce", dict(out=s16[2][:n], in_=tmp_[:n].rearrange("p (h d) -> p h d", h=NH), axis=AX.X, op=ALU.add))], reads=[FK[10]], writes=[("W_s16", 2)])
            if cfg.stop == "W4":
                continue
            R.op("pe", [("matmul", dict(out=Y2[:n, hh * 512:(hh + 1) * 512], lhsT=MU[:n, :n], rhs=e_[:n, hh * 512:(hh + 1) * 512], start=True, stop=True)) for hh in range(2)],
                 reads=["W_msk", FK[0]], writes=["W_Y2"])
            R.op("pe", [("matmul", dict(out=Z2[:n, hh * 512:(hh + 1) * 512], lhsT=ones[:n, :n], rhs=e_[:n, hh * 512:(hh + 1) * 512], start=True, stop=True)) for hh in range(2)],
                 reads=["W_ones", FK[0]], writes=["W_Z2"])
            R.op("pe", [("matmul", dict(out=X0[:, hp:hp + 1], lhsT=e_[:n, hp * 128:(hp + 1) * 128], rhs=ones[:n, 0:1], start=True, stop=True)) for hp in range(8)],
                 reads=["W_ones", FK[0]], writes=["W_X0"])
            R.op("act", [("activation", dict(out=WC[:], in_=X0[:, 0:8], func=AF.Exp, scale=-C0))], reads=["W_X0"], writes=["W_WC"])
            Wt, Winv, Wprev, Wrel = F[3], F[4], F[5], F[6]
            R.op("act", [("activation", dict(out=Wt[:n], in_=Y2[:n, :], func=AF.Exp, scale=-C0))], reads=["W_Y2"], writes=[FK[3]])
            R.op("act", [("activation", dict(out=Winv[:n], in_=Y2[:n, :], func=AF.Exp, scale=C0))], reads=["W_Y2"], writes=[FK[4]])
            R.op("dve", [("tensor_tensor", dict(out=Wprev[:n], in0=e_[:n], in1=Y2[:n, :], op=ALU.subtract))], reads=["W_Y2", FK[0]], writes=[FK[5]])
            R.op("act", [("activation", dict(out=Wprev[:n], in_=Wprev[:n], func=AF.Exp, scale=C0))], reads=[FK[5]], writes=[FK[5]])
            R.op("act", [("activation", dict(out=Wrel[:n], in_=Z2[:n, :], func=AF.Exp, scale=-C0))], reads=["W_Z2"], writes=[FK[6]])
            R.op("pool", [("tensor_tensor", dict(out=Wrel[:n], in0=Wrel[:n], in1=Winv[:n], op=ALU.mult))], reads=[FK[4]], writes=[FK[6]])
            if cfg.stop == "W5":
                continue
            BK = [("W_B", i) for i in range(7)]
            rt, kkt, kh, bh, kbar, bbar, v_bf = Bq
            specs = ((rt, r_, Wt, "W_cols", FK[3]), (kkt, kk_[:n], Wprev, FK[7], FK[5]), (kh, k2_[:n], Winv, FK[8], FK[4]), (bh, b_[:n], Winv, FK[9], FK[4]),
                     (kbar, k2_[:n], Wrel, FK[8], FK[6]), (bbar, b_[:n], Wrel, FK[9], FK[6]))
            for i, (dst, src, wgt, k1, k2k) in enumerate(specs):
                R.op("pool" if i % 2 else "dve", [("tensor_tensor", dict(out=dst[:n], in0=src, in1=wgt[:n], op=ALU.mult))], reads=[k1, k2k], writes=[BK[i]])
            R.op("act", [("activation", dict(out=v_bf[:n], in_=v_, func=AF.Copy))], reads=["W_cols"], writes=[BK[6]])
            R.op("dve", [("tensor_tensor", dict(out=bo_[:n].rearrange("p (h d) -> p h d", h=NH), in0=v_.rearrange("p (h d) -> p h d", h=NH), in1=bc(s16[2][:n].unsqueeze(2), (n, NH, HN)), op=ALU.mult))],
                 reads=["W_cols", ("W_s16", 2)], writes=[FK[3]])
            if t + 1 < len(cfg.tiles):
                w_loads(t + 1)
            if cfg.stop == "W6":
                continue
            Z2b = Z2[:].bitcast(BF16).rearrange("p (k t) -> p k t", t=128)
            for half in range(2):
                srcs = (rt, kkt) if half == 0 else (kh, bh)
                R.op("pe", [("transpose", dict(out=Z2b[:, qq * 8 + hp, :n], in_=srcs[qq][:n, hp * 128:(hp + 1) * 128], identity=ident[:n, :n])) for qq in range(2) for hp in range(8)],
                     reads=[BK[2 * half], BK[2 * half + 1], "ident"], writes=["W_Z2"])
                if half == 0:
                    for qq, dstp in enumerate((rP, kP)):
                        d4 = dstp[:].rearrange("p (a b) t -> p a b t", b=2)
                        R.op("dve", [("tensor_copy", dict(out=d4[0:64, :, 0, :n], in_=Z2b[0:64, qq * 8:(qq + 1) * 8, :n]))], reads=["W_Z2"], writes=[("W_fT", 0)])
                        R.op("act", [("activation", dict(out=d4[64:128, :, 1, :n], in_=Z2b[64:128, qq * 8:(qq + 1) * 8, :n], func=AF.Copy))], reads=["W_Z2"], writes=[("W_fT", 0)])
                else:
                    R.op("dve", [("tensor_copy", dict(out=fT2[:, :, :, :n], in_=Z2b[:, :, :n].rearrange("p (a b) t -> p a b t", a=2)))], reads=["W_Z2"], writes=[("W_fT", 1)])
            fk = [("W_fT", 0), ("W_fT", 1)]

            def opnd(q, h):
                if q == 0:
                    return rP[:, h, :n]
                if q == 1:
                    return kP[:, h, :n]
                return fT2[:, q - 2, h // 2, :n]
            ArkT, ArbT, AkkT = Am
            jobs = ((ArkT, 2, 0, MU, ("W_A", 0)), (ArbT, 3, 0, MU, ("W_A", 1)), (AkkT, 2, 1, MS, ("W_A", 2)),
                    (PTm[0], 3, 1, MSn, ("W_PT", 0)), (Pm[0], 1, 3, MLn, ("W_P", 0)))
            units = [(QQ[:, :, :], "W_QQ"), (Y2[:].rearrange("p (h i) -> p h i", h=8), "W_Y2"), (Z2[:].rearrange("p (h i) -> p h i", h=8), "W_Z2")]
            HB, HW = 2, 8
            for dst, ql, qr_, mk, key in jobs:
                for hb in range(HB):
                    qb, qk = units[qi % 3]
                    qi += 1
                    R.op("pe", [("matmul", dict(out=qb[:n, i, :n], lhsT=opnd(ql, HW * hb + i), rhs=opnd(qr_, HW * hb + i), start=True, stop=True)) for i in range(HW)],
                         reads=fk, writes=[qk])
                    R.op("dve", [("tensor_tensor", dict(out=dst[:n, HW * hb:HW * hb + HW, :n], in0=qb[:n, :, :n], in1=bc(mk[:n, :n].unsqueeze(1), (n, HW, n)), op=ALU.mult))],
                         reads=[qk, "W_msk"], writes=[key + (hb,)])
            if cfg.stop == "W8":
                continue
            R.op("pool", [("tensor_tensor", dict(out=MT[:n, :, :n], in0=PTm[0][:n, :, :n], in1=bc(MI[:n, :n].unsqueeze(1), (n, NH, n)), op=ALU.add))],
                 reads=[("W_PT", 0, hb) for hb in range(HB)] + ["W_msk"], writes=[("W_MT", hb) for hb in range(HB)])
            cur = 0
            NSQ = 6
            for kq in range(NSQ):
                nxt = 1 - cur
                for hb in range(HB):
                    qb, qk = units[qi % 3]
                    qi += 1
                    R.op("pe", [("matmul", dict(out=qb[:n, i, :n], lhsT=PTm[cur][:n, HW * hb + i, :n], rhs=Pm[cur][:n, HW * hb + i, :n], start=True, stop=True)) for i in range(HW)],
                         reads=[("W_PT", cur, hb), ("W_P", cur, hb)], writes=[qk])
                    R.op("act", [("activation", dict(out=Pm[nxt][:n, HW * hb:HW * hb + HW, :n], in_=qb[:n, :, :n], func=AF.Copy))], reads=[qk], writes=[("W_P", nxt, hb)])
                    if kq < NSQ - 1:
                        qb, qk = units[qi % 3]
                        qi += 1
                        R.op("pe", [("matmul", dict(out=qb[:n, i, :n], lhsT=Pm[cur][:n, HW * hb + i, :n], rhs=PTm[cur][:n, HW * hb + i, :n], start=True, stop=True)) for i in range(HW)],
                             reads=[("W_PT", cur, hb), ("W_P", cur, hb)], writes=[qk])
                        R.op("dve", [("tensor_copy", dict(out=PTm[nxt][:n, HW * hb:HW * hb + HW, :n], in_=qb[:n, :, :n]))], reads=[qk], writes=[("W_PT", nxt, hb)])
                for hb in range(HB):
                    qb, qk = units[qi % 3]
                    qi += 1
                    R.op("pe", [("matmul", dict(out=qb[:n, i, :n], lhsT=Pm[nxt][:n, HW * hb + i, :n], rhs=MT[:n, HW * hb + i, :n], start=True, stop=True)) for i in range(HW)],
                         reads=[("W_P", nxt, hb), ("W_MT", hb)], writes=[qk])
                    R.op("dve", [("tensor_tensor", dict(out=MT[:n, HW * hb:HW * hb + HW, :n], in0=qb[:n, :, :n], in1=MT[:n, HW * hb:HW * hb + HW, :n], op=ALU.add))],
                         reads=[qk], writes=[("W_MT", hb)])
                cur = nxt
            if cfg.stop == "W9":
                continue
            akeys = lambda i: [("W_A", i, hb) for hb in range(HB)]
            mtk = [("W_MT", hb) for hb in range(HB)]
            RHS0, Uneg, yo = rt, kkt, kh
            hs = lambda h: slice(h * HN, (h + 1) * HN)
            R.op("pe", [m for h in range(NH) for m in (
                        ("matmul", dict(out=Y2[:n, hs(h)], lhsT=opnd(1, h), rhs=T_bf[:, h // 2, :], start=True, stop=False)),
                        ("matmul", dict(out=Y2[:n, hs(h)], lhsT=AkkT[:n, h, :n], rhs=v_bf[:n, hs(h)], start=False, stop=True)))],
                 reads=fk + ["W_Tbf", BK[6]] + akeys(2), writes=["W_Y2"])
            R.op("act", [("activation", dict(out=RHS0[:n], in_=Y2[:n, :], func=AF.Copy))], reads=["W_Y2"] + fk, writes=[BK[0]])
            if cfg.stop == "W10":
                continue
            R.op("pe", [("matmul", dict(out=Z2[:n, hs(h)], lhsT=MT[:n, h, :n], rhs=RHS0[:n, hs(h)], start=True, stop=True)) for h in range(NH)],
                 reads=mtk + [BK[0]], writes=["W_Z2"])
            R.op("act", [("activation", dict(out=Uneg[:n], in_=Z2[:n, :], func=AF.Copy, scale=-1.0))], reads=["W_Z2"] + fk, writes=[BK[1]])
            if cfg.stop == "W11":
                continue
            R.op("pe", [m for h in range(NH) for m in (
                        ("matmul", dict(out=Y2[:n, hs(h)], lhsT=opnd(0, h), rhs=T_bf[:, h // 2, :], start=True, stop=False)),
                        ("matmul", dict(out=Y2[:n, hs(h)], lhsT=ArkT[:n, h, :n], rhs=v_bf[:n, hs(h)], start=False, stop=False)),
                        ("matmul", dict(out=Y2[:n, hs(h)], lhsT=ArbT[:n, h, :n], rhs=Uneg[:n, hs(h)], start=False, stop=True)))],
                 reads=fk + ["W_Tbf", BK[6], BK[1]] + akeys(0) + akeys(1), writes=["W_Y2"])
            if cfg.stop == "W12":
                continue
            Y2s = Z2[:].rearrange("p (a b c) -> p a b c", a=8, b=2)
            R.op("pe", [m for h in range(NH) for m in (
                        ("matmul", dict(out=Y2s[:, h // 2, h % 2, :], lhsT=kbar[:n, (h // 2) * 128:(h // 2 + 1) * 128], rhs=v_bf[:n, hs(h)], start=True, stop=False)),
                        ("matmul", dict(out=Y2s[:, h // 2, h % 2, :], lhsT=bbar[:n, (h // 2) * 128:(h // 2 + 1) * 128], rhs=Uneg[:n, hs(h)], start=False, stop=True)))],
                 reads=[BK[4], BK[5], BK[6], BK[1]], writes=["W_Z2"])
            R.op("dve", [("tensor_tensor", dict(out=Tst[:], in0=Tst[:], in1=bc(WC[:, :].unsqueeze(2), (128, 8, HN)), op=ALU.mult))], reads=["W_WC"], writes=["W_T"])
            R.op("dve", [("tensor_tensor", dict(out=Tst[0:64], in0=Tst[0:64], in1=Y2s[0:64, :, 0, :], op=ALU.add))], reads=["W_Z2"], writes=["W_T"])
            R.op("dve", [("tensor_tensor", dict(out=Tst[64:128], in0=Tst[64:128], in1=Y2s[64:128, :, 1, :], op=ALU.add))], reads=["W_Z2"], writes=["W_T"])
            R.op("act", [("activation", dict(out=T_bf[:], in_=Tst[:], func=AF.Copy))], reads=["W_T"], writes=["W_Tbf"])
            if cfg.stop == "W13":
                continue
            yc, ysq = F[0], F[1]
            h3 = lambda ap: ap.rearrange("p (h d) -> p h d", h=NH)
            R.op("dve", [("tensor_reduce", dict(out=s16[3][:n], in_=h3(Y2[:n, :]), axis=AX.X, op=ALU.add))], reads=["W_Y2"], writes=[("W_s16", 3)])
            R.op("dve", [("tensor_scalar", dict(out=s16[3][:n], in0=s16[3][:n], scalar1=1.0 / HN, scalar2=None, op0=ALU.mult))], reads=[("W_s16", 3)], writes=[("W_s16", 3)])
            R.op("dve", [("tensor_tensor", dict(out=h3(yc[:n]), in0=h3(Y2[:n, :]), in1=bc(s16[3][:n].unsqueeze(2), (n, NH, HN)), op=ALU.subtract))],
                 reads=["W_Y2", ("W_s16", 3)], writes=[FK[0]])
            R.op("pool", [("tensor_tensor", dict(out=ysq[:n], in0=yc[:n], in1=yc[:n], op=ALU.mult))], reads=[FK[0]], writes=[FK[1]])
            R.op("dve", [("tensor_reduce", dict(out=s16[4][:n], in_=h3(ysq[:n]), axis=AX.X, op=ALU.add))], reads=[FK[1]], writes=[("W_s16", 4)])
            R.op("dve", [("tensor_scalar", dict(out=s16[4][:n], in0=s16[4][:n], scalar1=1.0 / HN, scalar2=64e-5, op0=ALU.mult, op1=ALU.add))], reads=[("W_s16", 4)], writes=[("W_s16", 4)])
            R.op("act", [("activation", dict(out=s16[4][:n], in_=s16[4][:n], func=AF.Sqrt))], reads=[("W_s16", 4)], writes=[("W_s16", 4)])
            R.op("dve", [("reciprocal", dict(out=s16[5][:n], in_=s16[4][:n]))], reads=[("W_s16", 4)], writes=[("W_s16", 5)])
            R.op("dve", [("tensor_tensor", dict(out=h3(yc[:n]), in0=h3(yc[:n]), in1=bc(s16[5][:n].unsqueeze(2), (n, NH, HN)), op=ALU.mult))], reads=[("W_s16", 5)], writes=[FK[0]])
            R.op("pool", [("tensor_tensor", dict(out=yc[:n], in0=yc[:n], in1=bcv(2, n), op=ALU.mult))], reads=["W_vecs"], writes=[FK[0]])
            R.op("pool", [("tensor_tensor", dict(out=yc[:n], in0=yc[:n], in1=bcv(3, n), op=ALU.add))], reads=["W_vecs"], writes=[FK[0]])
            R.op("pool", [("tensor_tensor", dict(out=yc[:n], in0=yc[:n], in1=bo_[:n], op=ALU.add))], reads=[FK[3]], writes=[FK[0]])
            R.op("pool", [("tensor_tensor", dict(out=yo[:n], in0=yc[:n], in1=g_[:n], op=ALU.mult))], reads=[FK[2]] + fk, writes=[BK[2]])
            Z2b8 = Z2b[:, 0:8, :]
            R.op("pe", [("transpose", dict(out=Z2b8[:, k, :n], in_=yo[:n, k * 128:(k + 1) * 128], identity=ident[:n, :n])) for k in range(8)],
                 reads=[BK[2], "ident"], writes=["W_Z2"])
            R.op("act", [("activation", dict(out=yTo[:, :, :n], in_=Z2b8[:, :, :n], func=AF.Copy))], reads=["W_Z2"], writes=["W_yTo"])
            R.dma("sp", "W_yTo", dict(out=YT[RET_W:RET_W + W, r0:r0 + n].rearrange("(k p) t -> p k t", p=128), in_=yTo[:, :, :n]),
                  reads=["W_yTo"], writes=[("YT", t)])
    R.countdown = None
    R.barrier()


def host_consts(cfg):
    c = {}
    L = cfg.L
    half = 64
    inv = (np.float32(10000.0) ** (-np.arange(half, dtype=np.float32) / np.float32(half))).astype(np.float32)
    ang = (np.arange(L, dtype=np.float32)[:, None] * inv[None, :]).astype(np.float32)
    cs, sn = np.cos(ang).astype(np.float32), np.sin(ang).astype(np.float32)
    sc = np.float32(RET_D ** -0.5)
    c["rope"] = np.ascontiguousarray(np.stack([cs, sn, cs * sc, sn * sc], axis=1)).astype(np.float32)
    log_g = np.log(1.0 - 2.0 ** (-5.0 - np.arange(RET_H, dtype=np.float64)))
    idx = np.arange(CH, dtype=np.float64)
    diff = idx[None, :] - idx[:, None]
    m = np.where((diff >= 0)[:, None, :], np.exp(np.maximum(diff, 0)[:, None, :] * log_g[None, :, None]), 0.0)
    c["ret_maskT"] = np.ascontiguousarray(m).astype(np.float32)
    vec = np.zeros((128, 3, RET_H), np.float32)
    vec[:, 0, :] = np.exp((CH - 1 - idx)[:, None] * log_g[None, :])
    vec[:NMETA, 1, :] = np.exp((NMETA - 1 - idx[:NMETA])[:, None] * log_g[None, :])
    vec[:, 2, :] = np.exp((idx + 1.0)[:, None] * log_g[None, :])
    c["ret_vec"] = vec
    tri = np.zeros((128, 3, 128), np.float32)
    ii = np.arange(128)
    tri[:, 0, :] = (ii[:, None] <= ii[None, :])
    tri[:, 1, :] = (ii[:, None] > ii[None, :])
    tri[:, 2, :] = 1.0
    c["tri"] = tri
    mk = np.zeros((128, 5, 128), np.float32)
    mk[:, 0, :] = (ii[:, None] <= ii[None, :])
    mk[:, 1, :] = (ii[:, None] < ii[None, :])
    mk[:, 2, :] = -1.0 * (ii[:, None] < ii[None, :])
    mk[:, 3, :] = -1.0 * (ii[:, None] > ii[None, :])
    mk[:, 4, :] = (ii[:, None] == ii[None, :])
    c["rw_msk"] = mk
    return c


def build_program(cfg):
    nc = bass.Bass("TRN2", target_bir_lowering=False)
    dr = {}

    def din(name, shape, dt=F32):
        dr[name] = nc.dram_tensor(name, list(shape), dt, kind="ExternalInput").ap()

    def dscratch(name, shape, dt=F32):
        kind = "ExternalOutput" if cfg.debug else "Internal"
        dr[name] = nc.dram_tensor(name, list(shape), dt, kind=kind).ap()

    din("x", (cfg.S, D))
    din("meta_tokens", (NMETA, D))
    din("ident_in", (128, 128))
    din("rope", (cfg.L, 4, 64))
    din("ret_maskT", (128, RET_H, 128))
    din("ret_vec", (128, 3, RET_H))
    din("tri", (128, 3, 128))
    din("rw_msk", (128, 5, 128))
    din("rwkv_mu", (cfg.depth, RW_COLS))
    for nm in ("rwkv_w0", "rwkv_a0", "rwkv_k_k", "rwkv_k_a", "rwkv_ln_w", "rwkv_ln_b"):
        din(nm, (cfg.depth, RW_W))
    din("rwkv_w2", (cfg.depth, 64, RW_W))
    din("rwkv_a2", (cfg.depth, 64, RW_W))
    din("rwkv_g2", (cfg.depth, 160, RW_W))
    din("rwkv_r_k", (cfg.depth, RW_H, RW_N))
    din("ssd_conv_w", (cfg.depth, 4, SSD_CONV))
    din("ssd_conv_b", (cfg.depth, SSD_CONV))
    din("ssd_dt_bias", (cfg.depth, SSD_H))
    din("ssd_a_log", (cfg.depth, SSD_H))
    din("ssd_d", (cfg.depth, SSD_H))
    din("ssd_norm_w", (cfg.depth, SSD_W))
    for nm in ("norm_mix_pre", "norm_mix_post", "norm_ffn_pre", "norm_ffn_post"):
        din(nm, (cfg.depth, D))
    din("w_in", (cfg.depth, D, IN_COLS))
    din("w_branch_ret", (cfg.depth, RET_W, D))
    din("w_branch_rwkv", (cfg.depth, RW_W, D))
    din("w_branch_ssd", (cfg.depth, SSD_W, D))
    din("w_out", (cfg.depth, D, D))
    din("ffn_w_gate", (cfg.depth, D, FFN))
    din("ffn_w_up", (cfg.depth, D, FFN))
    din("ffn_w_down", (cfg.depth, FFN, D))
    if cfg.yt_input:
        din("YT", (2 * D, cfg.L), BF16)
    else:
        dscratch("YT", (2 * D, cfg.L), BF16)
    dr["H"] = nc.dram_tensor("H", [cfg.L, D], F32, kind="ExternalOutput").ap()
    dscratch("P_ret", (cfg.L, 4096))
    dscratch("P_rw", (cfg.L, RW_COLS))
    dscratch("P_ssd", (cfg.L, NP_COLS - O_Z))
    dscratch("SGT", (3 * D, cfg.L))
    R = Rec()
    with ExitStack() as es:
        ident_f = es.enter_context(nc.sbuf_tensor("ident_f", [128, 128], F32))
        ident = es.enter_context(nc.sbuf_tensor("ident", [128, 128], BF16))
        R.dma("sp", "ident", dict(out=ident_f[:], in_=dr["ident_in"][:, :]), writes=["ident_f"])
        R.op("dve", [("tensor_copy", dict(out=ident[:], in_=ident_f[:]))], reads=["ident_f"], writes=["ident"])
        R.dma("sp", "Hinit", dict(out=dr["H"][0:NMETA, :], in_=dr["meta_tokens"][:, :]), writes=[("H", 0)])
        for i in range(0, cfg.nchunks, 8):
            r0 = NMETA + i * CH
            nn = min(8, cfg.nchunks - i) * CH
            R.dma("sp", "Hinit", dict(out=dr["H"][r0:r0 + nn, :], in_=dr["x"][i * CH:i * CH + nn, :]), writes=[("Hx", i)])
        for t in range(len(cfg.tiles)):
            R.lastw[("H", t)] = ("d_Hinit", R.cnt["d_Hinit"])
        for li in cfg.layers:
            if cfg.stop == "init":
                break
            if "A" in cfg.phases:
                phase_A(R, nc, cfg, li, dr, ident)
            if "R" in cfg.phases:
                phase_B_ret(R, nc, cfg, li, dr, ident)
            if "W" in cfg.phases:
                phase_B_rwkv(R, nc, cfg, li, dr, ident, ident_f)
            if "S" in cfg.phases:
                phase_B_ssd(R, nc, cfg, li, dr, ident)
            if "C" in cfg.phases:
                phase_C(R, nc, cfg, li, dr, ident)
            if "D" in cfg.phases:
                phase_D(R, nc, cfg, li, dr, ident)
        R.emit(nc)
    return nc, R


def host_inputs(inputs, b, cfg):
    m = {"x": np.ascontiguousarray(inputs["x"][b, :cfg.S]), "meta_tokens": np.ascontiguousarray(inputs["meta_tokens"]),
         "ident_in": np.eye(128, dtype=np.float32)}
    m.update(host_consts(cfg))
    for k in ("norm_mix_pre", "norm_mix_post", "norm_ffn_pre", "norm_ffn_post", "w_in", "w_branch_ret", "w_branch_rwkv",
              "w_branch_ssd", "w_out", "ffn_w_gate", "ffn_w_up", "ffn_w_down",
              "ssd_conv_w", "ssd_conv_b", "ssd_dt_bias", "ssd_a_log", "ssd_d", "ssd_norm_w",
              "rwkv_mu", "rwkv_w0", "rwkv_w2", "rwkv_a0", "rwkv_a2", "rwkv_g2", "rwkv_k_k", "rwkv_k_a", "rwkv_r_k", "rwkv_ln_w", "rwkv_ln_b"):
        m[k] = np.ascontiguousarray(inputs[k])
    return m


_CACHE = {}


def kernel(**inputs):
    cfg = Cfg(nchunks=64, layers=(0, 1, 2, 3))
    if "prog" not in _CACHE:
        _CACHE["prog"] = build_program(cfg)
    nc, _ = _CACHE["prog"]
    nb = inputs["x"].shape[0]
    in_maps = [host_inputs(inputs, b, cfg) for b in range(nb)]
    res = run_bass_kernel_spmd(nc, in_maps, core_ids=list(range(nb)))
    out = np.stack([np.asarray(res.results[b]["H"])[NMETA:] for b in range(nb)]).astype(np.float32)
    return out
```

```python
import numpy as np
from contextlib import ExitStack
import concourse.bass as bass
import concourse.mybir as mybir
from concourse.bass_utils import run_bass_kernel_spmd

F32 = mybir.dt.float32
BF16 = mybir.dt.bfloat16
AF = mybir.ActivationFunctionType
ALU = mybir.AluOpType

D = 2048
NMETA = 16
CH = 128
DEPTH = 4
EPS = 1e-6
RET_H, RET_D, RET_W = 8, 128, 1024
RW_H, RW_N, RW_W = 16, 64, 1024
RW_COLS = 3 * 1024 + 64 + 64 + 160
SSD_H, SSD_P, SSD_W, SSD_G, SSD_N = 32, 64, 2048, 4, 128
SSD_CONV = 3072
FFN = 5632
IN_COLS = 18752
NP_COLS = IN_COLS - 3 * D
O_RQ, O_RK, O_RV, O_RG = 0, 1024, 2048, 3072
O_RW = 4096
O_Z = O_RW + RW_COLS
O_XBC = O_Z + SSD_W
O_DT = O_XBC + SSD_CONV
assert O_DT + SSD_H == NP_COLS


_UID = [0]


def uq(name):
    _UID[0] += 1
    return f"{name}_{_UID[0]}"


class Rec:
    ENGS = ("pe", "dve", "act", "pool", "sp")

    def __init__(self):
        self.ops = {k: [] for k in self.ENGS}
        self.cnt = {}
        self.seen = {k: {} for k in self.ENGS}
        self.lastw = {}
        self.rds = {}
        self.nops = 0

    def _deps(self, eng, reads, writes):
        need = {}

        def want(ev):
            s, v = ev
            if self.seen[eng].get(s, 0) < v and need.get(s, 0) < v:
                need[s] = v
        for k in reads:
            if k in self.lastw:
                want(self.lastw[k])
        for k in writes:
            if k in self.lastw:
                want(self.lastw[k])
            for ev in self.rds.get(k, {}).items():
                want(ev)
        for s, v in need.items():
            self.seen[eng][s] = v
        return list(need.items())

    def _commit(self, ev, reads, writes):
        s, v = ev
        for k in reads:
            d = self.rds.setdefault(k, {})
            if d.get(s, 0) < v:
                d[s] = v
        for k in writes:
            self.lastw[k] = ev
            self.rds[k] = {}

    countdown = None

    def op(self, eng, insts, reads=(), writes=()):
        if self.countdown is not None:
            if self.countdown <= 0:
                return
            self.countdown -= 1
        waits = self._deps(eng, reads, writes)
        s = "c_" + eng
        self.cnt[s] = self.cnt.get(s, 0) + 1
        self.ops[eng].append((waits, insts, s, 1))
        self._commit((s, self.cnt[s]), reads, writes)
        self.nops += len(insts)

    def dma(self, q, semkey, kwargs, reads=(), writes=()):
        if self.countdown is not None and self.countdown <= 0:
            return
        waits = self._deps(q, reads, writes)
        s = "d_" + semkey
        self.cnt[s] = self.cnt.get(s, 0) + 16
        self.ops[q].append((waits, [("dma_start", kwargs)], s, 16))
        self._commit((s, self.cnt[s]), reads, writes)
        self.nops += 1

    def barrier(self):
        for eng in self.ENGS:
            waits = []
            for s, v in self.cnt.items():
                if self.seen[eng].get(s, 0) < v:
                    waits.append((s, v))
                    self.seen[eng][s] = v
            if waits:
                self.ops[eng].append((waits, [], None, 0))
        self.lastw = {}
        self.rds = {}

    def emit(self, nc):
        blockname = {"pe": "tensor", "dve": "vector", "act": "scalar", "pool": "gpsimd", "sp": "sync"}
        with ExitStack() as st:
            sems = {name: st.enter_context(nc.semaphore(name)) for name in self.cnt}
            block = st.enter_context(nc.Block())
            for eng in self.ENGS:
                def body(e, eng=eng):
                    for waits, insts, s, inc in self.ops[eng]:
                        for ws, wv in waits:
                            e.wait_ge(sems[ws], wv)
                        last = None
                        for name, kw in insts:
                            last = getattr(e, name)(**kw)
                        if last is not None:
                            last.then_inc(sems[s], inc)
                    if eng == "sp":
                        for name, v in self.cnt.items():
                            e.wait_ge(sems[name], v)
                getattr(block, blockname[eng])(body)


class Cfg:
    def __init__(self, nchunks=64, layers=(0, 1, 2, 3), debug=False, depth=DEPTH):
        self.nchunks = nchunks
        self.layers = tuple(layers)
        self.debug = debug
        self.depth = depth
        import os
        self.stop = os.environ.get("KSTOP", "")
        self.phases = os.environ.get("KPHASES", "ARSWCD")
        self.yt_input = bool(os.environ.get("KYTIN", ""))
        self.S = nchunks * CH
        self.L = NMETA + self.S
        self.tiles = [(0, NMETA)] + [(NMETA + CH * i, CH) for i in range(nchunks)]


def mm_group(out, pairs):
    n = len(pairs)
    return [("matmul", dict(out=out, lhsT=l, rhs=r, start=(i == 0), stop=(i == n - 1))) for i, (l, r) in enumerate(pairs)]


def passes_of(tiles, per_pass):
    out = []
    i = 0
    while i < len(tiles):
        k = per_pass + 1 if i == 0 else per_pass
        out.append(list(range(i, min(i + k, len(tiles)))))
        i += k
    return out


def norm_transpose_stage(R, nc, st, cfg, tl, Hsrc, wn_bc, xT, ident, pfx, pst):
    offs = []
    off = 0
    for j, t in enumerate(tl):
        r0, n = cfg.tiles[t]
        b = j % 2
        hb = st["h"][b]
        R.dma("sp", f"{pfx}h{b}", dict(out=hb[:n], in_=Hsrc[r0:r0 + n, :]), reads=[("H", t)], writes=[(pfx, "h", b)])
        R.op("act", [("activation", dict(out=st["junk"][:n], in_=hb[:n], func=AF.Square, accum_out=st["ss"][:n]))],
             reads=[(pfx, "h", b)], writes=[(pfx, "junk"), (pfx, "ss")])
        R.op("dve", [("tensor_scalar", dict(out=st["ms"][:n], in0=st["ss"][:n], scalar1=1.0 / D, scalar2=EPS, op0=ALU.mult, op1=ALU.add))],
             reads=[(pfx, "ss")], writes=[(pfx, "ms")])
        R.op("act", [("activation", dict(out=st["rt"][:n], in_=st["ms"][:n], func=AF.Sqrt))], reads=[(pfx, "ms")], writes=[(pfx, "rt")])
        R.op("dve", [("reciprocal", dict(out=st["rstd"][:n], in_=st["rt"][:n]))], reads=[(pfx, "rt")], writes=[(pfx, "rstd")])
        xn = st["xn"][b]
        R.op("dve", [("scalar_tensor_tensor", dict(out=xn[:n], in0=hb[:n], scalar=st["rstd"][:n], in1=wn_bc[:n], op0=ALU.mult, op1=ALU.mult))],
             reads=[(pfx, "h", b), (pfx, "rstd"), (pfx, "wn")], writes=[(pfx, "xn", b)])
        pt = pst[b]
        R.op("pe", [("transpose", dict(out=pt[:, k, :n], in_=xn[:n, k * 128:(k + 1) * 128], identity=ident[:n, :n])) for k in range(D // 128)],
             reads=[(pfx, "xn", b), "ident"], writes=[("psT", b)])
        R.op("dve", [("tensor_copy", dict(out=xT[:, :, off:off + n], in_=pt[:, :, :n]))],
             reads=[("psT", b)], writes=[(pfx, "xT", j)])
        offs.append((t, off, n))
        off += n
    return offs


def phase_A(R, nc, cfg, li, dr, ident):
    TP = 8
    H, SGT, w_in = dr["H"], dr["SGT"], dr["w_in"]
    with ExitStack() as es:
        sb = lambda name, shape, dt: es.enter_context(nc.sbuf_tensor(uq(name), shape, dt))
        ps = lambda name, shape, dt: es.enter_context(nc.psum_tensor(uq(name), shape, dt))
        st = dict(h=[sb(f"A_h{i}", [128, D], F32) for i in range(2)], junk=sb("A_junk", [128, D], BF16),
                  ss=sb("A_ss", [128, 1], F32), ms=sb("A_ms", [128, 1], F32), rt=sb("A_rt", [128, 1], F32),
                  rstd=sb("A_rstd", [128, 1], F32), xn=[sb(f"A_xn{i}", [128, D], BF16) for i in range(2)])
        wn_bc = sb("A_wn", [128, D], F32)
        TMAX = NMETA + TP * CH
        xT = sb("A_xT", [128, D // 128, TMAX], BF16)
        NWB = 3
        wb = [sb(f"A_wb{i}", [128, D // 128, 512], BF16) for i in range(NWB)]
        NSTG = 4
        stg = [sb(f"A_stg{i}", [128, 512], F32) for i in range(NSTG)]
        pst = [ps(f"A_pst{i}", [128, D // 128, 128], BF16) for i in range(2)]
        NPS = 4
        psm = [ps(f"A_psm{i}", [128, 512], F32) for i in range(NPS)]

        R.dma("sp", "A_wn", dict(out=wn_bc[:], in_=dr["norm_mix_pre"][li:li + 1, :].partition_broadcast(128)), writes=[("A", "wn")])
        slabs = []
        for (sec0, secw, tname) in ((0, 4096, "P_ret"), (O_RW, RW_COLS, "P_rw"), (O_Z, NP_COLS - O_Z, "P_ssd")):
            for c in range(0, secw, 512):
                slabs.append((sec0 + c, min(512, secw - c), False, tname, c))
        slabs += [(c, 512, True, None, 0) for c in range(NP_COLS, IN_COLS, 512)]
        wi = 0
        si = 0
        pi = 0
        for tl in passes_of(cfg.tiles, TP):
            offs = norm_transpose_stage(R, nc, st, cfg, tl, H, wn_bc, xT, ident, "A", pst)
            ntok = sum(n for _, _, n in offs)
            if cfg.stop == "A1":
                continue
            row_base = cfg.tiles[tl[0]][0]
            xkeys = [("A", "xT", j) for j in range(len(tl))]
            for (c0, w, is_gate, tname, lc0) in slabs:
                wbuf = wb[wi % NWB]
                wkey = ("A", "wb", wi % NWB)
                wi += 1
                R.dma("pool", f"A_wb{(wi - 1) % NWB}",
                      dict(out=wbuf[:, :, :w], in_=w_in[li, :, c0:c0 + w].rearrange("(k p) n -> p k n", p=128)),
                      writes=[wkey])
                if not is_gate:
                    for j, (t, off, n) in enumerate(offs):
                        pb = psm[pi % NPS]
                        pkey = ("psm", pi % NPS)
                        pi += 1
                        R.op("pe", mm_group(pb[:n, :w], [(xT[:, k, off:off + n], wbuf[:, k, :w]) for k in range(D // 128)]),
                             reads=[wkey, xkeys[j]], writes=[pkey])
                        sg = stg[si % NSTG]
                        skey = ("A", "stg", si % NSTG)
                        if si % 2 == 0:
                            R.op("act", [("activation", dict(out=sg[:n, :w], in_=pb[:n, :w], func=AF.Copy))], reads=[pkey], writes=[skey])
                        else:
                            R.op("dve", [("tensor_copy", dict(out=sg[:n, :w], in_=pb[:n, :w]))], reads=[pkey], writes=[skey])
                        r0 = cfg.tiles[t][0]
                        R.dma("sp", f"A_stg{si % NSTG}", dict(out=dr[tname][r0:r0 + n, lc0:lc0 + w], in_=sg[:n, :w]), reads=[skey], writes=[("P", t)])
                        si += 1
                else:
                    g0 = c0 - NP_COLS
                    for q in range(4):
                        for tg in range(0, ntok, 512):
                            tw = min(512, ntok - tg)
                            pb = psm[pi % NPS]
                            pkey = ("psm", pi % NPS)
                            pi += 1
                            R.op("pe", mm_group(pb[:, :tw], [(wbuf[:, k, q * 128:(q + 1) * 128], xT[:, k, tg:tg + tw]) for k in range(D // 128)]),
                                 reads=[wkey] + xkeys, writes=[pkey])
                            sg = stg[si % NSTG]
                            skey = ("A", "stg", si % NSTG)
                            R.op("act", [("activation", dict(out=sg[:, :tw], in_=pb[:, :tw], func=AF.Sigmoid))], reads=[pkey], writes=[skey])
                            R.dma("sp", f"A_stg{si % NSTG}",
                                  dict(out=SGT[g0 + q * 128:g0 + (q + 1) * 128, row_base + tg:row_base + tg + tw], in_=sg[:, :tw]),
                                  reads=[skey], writes=[("SGT", tl[0])])
                            si += 1
    R.barrier()


def rstd_ops(R, st, pfx, src, n, reads):
    R.op("act", [("activation", dict(out=st["junk"][:n], in_=src, func=AF.Square, accum_out=st["ss"][:n]))],
         reads=reads, writes=[(pfx, "junk"), (pfx, "ss")])
    R.op("dve", [("tensor_scalar", dict(out=st["ms"][:n], in0=st["ss"][:n], scalar1=1.0 / D, scalar2=EPS, op0=ALU.mult, op1=ALU.add))],
         reads=[(pfx, "ss")], writes=[(pfx, "ms")])
    R.op("act", [("activation", dict(out=st["rt"][:n], in_=st["ms"][:n], func=AF.Sqrt))], reads=[(pfx, "ms")], writes=[(pfx, "rt")])
    R.op("dve", [("reciprocal", dict(out=st["rstd"][:n], in_=st["rt"][:n]))], reads=[(pfx, "rt")], writes=[(pfx, "rstd")])


def phase_C(R, nc, cfg, li, dr, ident):
    TP = 4
    H, SGT, YT = dr["H"], dr["SGT"], dr["YT"]
    TMAX = NMETA + TP * CH
    KY = 32
    for tl in passes_of(cfg.tiles, TP):
        offs = []
        off = 0
        for t in tl:
            offs.append((t, off, cfg.tiles[t][1]))
            off += cfg.tiles[t][1]
        ntok = off
        row_base = cfg.tiles[tl[0]][0]
        with ExitStack() as es0:
            mT = es0.enter_context(nc.sbuf_tensor(uq("C_mT"), [128, D // 128, TMAX], BF16))
            with ExitStack() as es:
                sb = lambda name, shape, dt: es.enter_context(nc.sbuf_tensor(uq(name), shape, dt))
                ps = lambda name, shape, dt: es.enter_context(nc.psum_tensor(uq(name), shape, dt))
                yT = sb("C_yT", [128, KY, TMAX], BF16)
                wbc = [sb(f"C_wb{i}", [128, KY, 512], BF16) for i in range(2)]
                sg = [sb(f"C_sg{i}", [128, 3, 512], F32) for i in range(2)]
                m1 = [sb(f"C_m1_{i}", [128, 512], F32) for i in range(2)]
                m2 = [sb(f"C_m2_{i}", [128, 512], F32) for i in range(2)]
                m3 = [sb(f"C_m3_{i}", [128, 512], F32) for i in range(2)]
                psb = [[ps(f"C_ps{i}_{j}", [128, 512], F32) for j in range(3)] for i in range(2)]
                R.dma("sp", "C_yT", dict(out=yT[:, :, :ntok], in_=YT[:, row_base:row_base + ntok].rearrange("(k p) t -> p k t", p=128)),
                      reads=[("YT", t) for t in tl], writes=[("C", "yT")])
                it = 0
                for sl in range(D // 512):
                    wb = wbc[sl % 2]
                    wkey = ("C", "wb", sl % 2)
                    c0 = sl * 512
                    R.dma("pool", f"C_wb{sl % 2}", dict(out=wb[:, 0:8, :], in_=dr["w_branch_ret"][li, :, c0:c0 + 512].rearrange("(k p) n -> p k n", p=128)), writes=[wkey])
                    R.dma("pool", f"C_wb{sl % 2}", dict(out=wb[:, 8:16, :], in_=dr["w_branch_rwkv"][li, :, c0:c0 + 512].rearrange("(k p) n -> p k n", p=128)), writes=[wkey])
                    R.dma("pool", f"C_wb{sl % 2}", dict(out=wb[:, 16:32, :], in_=dr["w_branch_ssd"][li, :, c0:c0 + 512].rearrange("(k p) n -> p k n", p=128)), writes=[wkey])
                    for q in range(4):
                        f0 = c0 + q * 128
                        for tg in range(0, ntok, 512):
                            tw = min(512, ntok - tg)
                            b = it % 2
                            it += 1
                            R.dma("sp", f"C_sg{b}", dict(out=sg[b][:, :, :tw],
                                  in_=SGT[:, row_base + tg:row_base + tg + tw].rearrange("(b f) t -> f b t", b=3)[f0:f0 + 128]),
                                  reads=[("SGT", 0)], writes=[("C", "sg", b)])
                            for br, (k0, k1) in enumerate(((0, 8), (8, 16), (16, 32))):
                                R.op("pe", mm_group(psb[b][br][:, :tw], [(wb[:, k, q * 128:(q + 1) * 128], yT[:, k, tg:tg + tw]) for k in range(k0, k1)]),
                                     reads=[wkey, ("C", "yT")], writes=[("C", "ps", b, br)])
                            for br, mm_ in enumerate((m1, m2, m3)):
                                R.op("dve", [("tensor_tensor", dict(out=mm_[b][:, :tw], in0=psb[b][br][:, :tw], in1=sg[b][:, br, :tw], op=ALU.mult))],
                                     reads=[("C", "ps", b, br), ("C", "sg", b)], writes=[("C", "m", br, b)])
                            R.op("pool", [("tensor_tensor", dict(out=m1[b][:, :tw], in0=m1[b][:, :tw], in1=m2[b][:, :tw], op=ALU.add))],
                                 reads=[("C", "m", 1, b)], writes=[("C", "m", 0, b)])
                            R.op("pool", [("tensor_tensor", dict(out=mT[:, sl * 4 + q, tg:tg + tw], in0=m1[b][:, :tw], in1=m3[b][:, :tw], op=ALU.add))],
                                 reads=[("C", "m", 0, b), ("C", "m", 2, b)], writes=[("C", "mT")])
            R.barrier()
            with ExitStack() as es:
                sb = lambda name, shape, dt: es.enter_context(nc.sbuf_tensor(uq(name), shape, dt))
                ps = lambda name, shape, dt: es.enter_context(nc.psum_tensor(uq(name), shape, dt))
                wo = sb("C_wo", [128, D // 128, D], BF16)
                wn_bc = sb("C_wn", [128, D], F32)
                st = dict(junk=sb("C_junk", [128, D], BF16), ss=sb("C_ss", [128, 1], F32), ms=sb("C_ms", [128, 1], F32),
                          rt=sb("C_rt", [128, 1], F32), rstd=sb("C_rstd", [128, 1], F32))
                hb = [sb(f"C_h{i}", [128, D], F32) for i in range(2)]
                tmp = [sb(f"C_tmp{i}", [128, D], F32) for i in range(2)]
                psm = [ps(f"C_psm{i}", [128, D], F32) for i in range(2)]
                for hh in range(2):
                    R.dma("pool", "C_wo", dict(out=wo[:, :, hh * 1024:(hh + 1) * 1024], in_=dr["w_out"][li, :, hh * 1024:(hh + 1) * 1024].rearrange("(k p) n -> p k n", p=128)),
                          writes=[("C", "wo")])
                R.dma("sp", "C_wn", dict(out=wn_bc[:], in_=dr["norm_mix_post"][li:li + 1, :].partition_broadcast(128)), writes=[("C", "wn")])
                for j, (t, off, n) in enumerate(offs):
                    b = j % 2
                    r0 = cfg.tiles[t][0]
                    R.dma("sp", f"C_h{b}", dict(out=hb[b][:n], in_=H[r0:r0 + n, :]), reads=[("H", t)], writes=[("C", "h", b)])
                    for s4 in range(4):
                        R.op("pe", mm_group(psm[b][:n, s4 * 512:(s4 + 1) * 512], [(mT[:, k, off:off + n], wo[:, k, s4 * 512:(s4 + 1) * 512]) for k in range(D // 128)]),
                             reads=[("C", "mT"), ("C", "wo")], writes=[("C", "psm", b, s4)])
                    pk = [("C", "psm", b, s4) for s4 in range(4)]
                    rstd_ops(R, st, "C", psm[b][:n, :], n, pk)
                    R.op("dve", [("scalar_tensor_tensor", dict(out=tmp[b][:n], in0=psm[b][:n, :], scalar=st["rstd"][:n], in1=wn_bc[:n], op0=ALU.mult, op1=ALU.mult))],
                         reads=pk + [("C", "rstd"), ("C", "wn")], writes=[("C", "tmp", b)])
                    R.op("pool", [("tensor_tensor", dict(out=hb[b][:n], in0=hb[b][:n], in1=tmp[b][:n], op=ALU.add))],
                         reads=[("C", "tmp", b)], writes=[("C", "h", b)])
                    R.dma("sp", f"C_hs{b}", dict(out=H[r0:r0 + n, :], in_=hb[b][:n]), reads=[("C", "h", b)], writes=[("H", t)])
            R.barrier()


def phase_D(R, nc, cfg, li, dr, ident):
    TP = 4
    H = dr["H"]
    TMAX = NMETA + TP * CH
    KF = FFN // 128
    for tl in passes_of(cfg.tiles, TP):
        with ExitStack() as es0:
            aT = es0.enter_context(nc.sbuf_tensor(uq("D_aT"), [128, KF, TMAX], BF16))
            with ExitStack() as es:
                sb = lambda name, shape, dt: es.enter_context(nc.sbuf_tensor(uq(name), shape, dt))
                ps = lambda name, shape, dt: es.enter_context(nc.psum_tensor(uq(name), shape, dt))
                st = dict(h=[sb(f"D_h{i}", [128, D], F32) for i in range(2)], junk=sb("D_junk", [128, D], BF16),
                          ss=sb("D_ss", [128, 1], F32), ms=sb("D_ms", [128, 1], F32), rt=sb("D_rt", [128, 1], F32),
                          rstd=sb("D_rstd", [128, 1], F32), xn=[sb(f"D_xn{i}", [128, D], BF16) for i in range(2)])
                wn_bc = sb("D_wn", [128, D], F32)
                xT = sb("D_xT", [128, D // 128, TMAX], BF16)
                wg = [sb(f"D_wg{i}", [128, D // 128, 512], BF16) for i in range(2)]
                wu = [sb(f"D_wu{i}", [128, D // 128, 512], BF16) for i in range(2)]
                sl_t = [sb(f"D_sl{i}", [128, 512], F32) for i in range(2)]
                pst = [ps(f"D_pst{i}", [128, D // 128, 128], BF16) for i in range(2)]
                psg = [ps(f"D_psg{i}", [128, 512], F32) for i in range(2)]
                psu = [ps(f"D_psu{i}", [128, 512], F32) for i in range(2)]
                R.dma("sp", "D_wn", dict(out=wn_bc[:], in_=dr["norm_ffn_pre"][li:li + 1, :].partition_broadcast(128)), writes=[("D", "wn")])
                offs = norm_transpose_stage(R, nc, st, cfg, tl, H, wn_bc, xT, ident, "D", pst)
                ntok = sum(n for _, _, n in offs)
                xkeys = [("D", "xT", j) for j in range(len(tl))]
                it = 0
                for sl in range(FFN // 512):
                    b2 = sl % 2
                    c0 = sl * 512
                    R.dma("pool", f"D_wg{b2}", dict(out=wg[b2][:], in_=dr["ffn_w_gate"][li, :, c0:c0 + 512].rearrange("(k p) n -> p k n", p=128)), writes=[("D", "wg", b2)])
                    R.dma("pool", f"D_wu{b2}", dict(out=wu[b2][:], in_=dr["ffn_w_up"][li, :, c0:c0 + 512].rearrange("(k p) n -> p k n", p=128)), writes=[("D", "wu", b2)])
                    for q in range(4):
                        for tg in range(0, ntok, 512):
                            tw = min(512, ntok - tg)
                            b = it % 2
                            it += 1
                            R.op("pe", mm_group(psg[b][:, :tw], [(wg[b2][:, k, q * 128:(q + 1) * 128], xT[:, k, tg:tg + tw]) for k in range(D // 128)]),
                                 reads=[("D", "wg", b2)] + xkeys, writes=[("D", "psg", b)])
                            R.op("pe", mm_group(psu[b][:, :tw], [(wu[b2][:, k, q * 128:(q + 1) * 128], xT[:, k, tg:tg + tw]) for k in range(D // 128)]),
                                 reads=[("D", "wu", b2)] + xkeys, writes=[("D", "psu", b)])
                            R.op("act", [("activation", dict(out=sl_t[b][:, :tw], in_=psg[b][:, :tw], func=AF.Silu))], reads=[("D", "psg", b)], writes=[("D", "sl", b)])
                            R.op("dve", [("tensor_tensor", dict(out=aT[:, sl * 4 + q, tg:tg + tw], in0=psu[b][:, :tw], in1=sl_t[b][:, :tw], op=ALU.mult))],
                                 reads=[("D", "psu", b), ("D", "sl", b)], writes=[("D", "aT")])
            R.barrier()
            with ExitStack() as es:
                sb = lambda name, shape, dt: es.enter_context(nc.sbuf_tensor(uq(name), shape, dt))
                ps = lambda name, shape, dt: es.enter_context(nc.psum_tensor(uq(name), shape, dt))
                CW = 256
                wd = [sb(f"D_wd{i}", [128, KF, CW], BF16) for i in range(2)]
                fst = sb("D_f", [128, TP + 1, D], F32)
                wn_bc = sb("D_wn2", [128, D], F32)
                st = dict(junk=sb("D_junk2", [128, D], BF16), ss=sb("D_ss2", [128, 1], F32), ms=sb("D_ms2", [128, 1], F32),
                          rt=sb("D_rt2", [128, 1], F32), rstd=sb("D_rstd2", [128, 1], F32))
                hb = [sb(f"D_hb{i}", [128, D], F32) for i in range(2)]
                psf = [ps(f"D_psf{i}", [128, 512], F32) for i in range(4)]
                R.dma("sp", "D_wn2", dict(out=wn_bc[:], in_=dr["norm_ffn_post"][li:li + 1, :].partition_broadcast(128)), writes=[("D", "wn2")])
                it = 0
                for sl in range(D // CW):
                    b2 = sl % 2
                    c0 = sl * CW
                    for hh in range(2):
                        R.dma("pool", f"D_wd{b2}", dict(out=wd[b2][:, hh * 22:(hh + 1) * 22, :],
                              in_=dr["ffn_w_down"][li, hh * 22 * 128:(hh + 1) * 22 * 128, c0:c0 + CW].rearrange("(k p) n -> p k n", p=128)), writes=[("D", "wd", b2)])
                    for j, (t, off, n) in enumerate(offs):
                        b = it % 4
                        it += 1
                        R.op("pe", mm_group(psf[b][:n, :CW], [(aT[:, k, off:off + n], wd[b2][:, k, :]) for k in range(KF)]),
                             reads=[("D", "aT"), ("D", "wd", b2)], writes=[("D", "psf", b)])
                        if it % 2 == 0:
                            R.op("act", [("activation", dict(out=fst[:n, j, c0:c0 + CW], in_=psf[b][:n, :CW], func=AF.Copy))], reads=[("D", "psf", b)], writes=[("D", "f", j, sl)])
                        else:
                            R.op("dve", [("tensor_copy", dict(out=fst[:n, j, c0:c0 + CW], in_=psf[b][:n, :CW]))], reads=[("D", "psf", b)], writes=[("D", "f", j, sl)])
                for j, (t, off, n) in enumerate(offs):
                    b = j % 2
                    r0 = cfg.tiles[t][0]
                    fk = [("D", "f", j, sl) for sl in range(D // CW)]
                    R.dma("sp", f"D_hb{b}", dict(out=hb[b][:n], in_=H[r0:r0 + n, :]), reads=[("H", t)], writes=[("D", "hb", b)])
                    rstd_ops(R, st, "D3", fst[:n, j, :], n, fk)
                    R.op("dve", [("scalar_tensor_tensor", dict(out=fst[:n, j, :], in0=fst[:n, j, :], scalar=st["rstd"][:n], in1=wn_bc[:n], op0=ALU.mult, op1=ALU.mult))],
                         reads=fk + [("D3", "rstd"), ("D", "wn2")], writes=[("D", "f2", j)])
                    R.op("pool", [("tensor_tensor", dict(out=hb[b][:n], in0=hb[b][:n], in1=fst[:n, j, :], op=ALU.add))],
                         reads=[("D", "f2", j), ("D", "hb", b)], writes=[("D", "hb", b)])
                    R.dma("sp", f"D_hs{b}", dict(out=H[r0:r0 + n, :], in_=hb[b][:n]), reads=[("D", "hb", b)], writes=[("H", t)])
            R.barrier()


AX = mybir.AxisListType


def bc(ap, shape):
    return ap.to_broadcast(list(shape))


def phase_B_ret(R, nc, cfg, li, dr, ident):
    P, YT = dr["P_ret"], dr["YT"]
    Hh, Dh = RET_H, RET_D
    with ExitStack() as es:
        sb = lambda name, shape, dt: es.enter_context(nc.sbuf_tensor(uq(name), shape, dt))
        ps = lambda name, shape, dt: es.enter_context(nc.psum_tensor(uq(name), shape, dt))
        maskT = sb("R_maskT", [128, Hh, 128], F32)
        vec = sb("R_vec", [128, 3, Hh], F32)
        cdec = sb("R_cdec", [128, Hh, Dh], F32)
        S = sb("R_S", [128, Hh, Dh], F32)
        S_bf = sb("R_Sbf", [128, Hh, Dh], BF16)
        pr = [sb(f"R_pr{i}", [128, 4 * RET_W], F32) for i in range(2)]
        rope = [sb(f"R_rope{i}", [128, 4, 64], F32) for i in range(2)]
        tt = [sb(f"R_t{i}", [128, Hh, 64], F32) for i in range(8)]
        qr = sb("R_qr", [128, Hh, Dh], BF16)
        kr = sb("R_kr", [128, Hh, Dh], BF16)
        kd = sb("R_kd", [128, Hh, Dh], BF16)
        v_bf = sb("R_vbf", [128, Hh, Dh], BF16)
        qkT = sb("R_qkT", [128, 2 * Hh, 128], BF16)
        sT = sb("R_sT", [128, Hh, 128], BF16)
        yo = sb("R_yo", [128, Hh, Dh], F32)
        y = sb("R_y", [128, Hh, Dh], F32)
        ysq = sb("R_ysq", [128, Hh, Dh], F32)
        sgl = sb("R_sg", [128, Hh, Dh], F32)
        sm = [sb(f"R_sm{i}", [128, Hh], F32) for i in range(4)]
        yb = sb("R_yb", [128, Hh * Dh], BF16)
        yTo = sb("R_yTo", [128, Hh, 128], BF16)
        psT = ps("R_psT", [128, 2 * Hh, 128], BF16)
        ps_sc = ps("R_pssc", [128, Hh, 128], F32)
        ps_yi = ps("R_psyi", [128, Hh, Dh], F32)
        ps_yo = ps("R_psyo", [128, Hh, Dh], F32)

        R.dma("sp", "R_maskT", dict(out=maskT[:], in_=dr["ret_maskT"][:, :, :]), writes=["R_maskT"])
        R.dma("sp", "R_vec", dict(out=vec[:], in_=dr["ret_vec"][:, :, :]), writes=["R_vec"])
        for h in range(Hh):
            g = 1.0 - 2.0 ** (-5.0 - h)
            R.op("pool", [("memset", dict(ap=cdec[:, h, :], constant=float(g ** CH)))], writes=["R_cdec"])
        R.op("pool", [("memset", dict(ap=S[:], constant=0.0))], writes=["R_S"])
        R.op("pool", [("memset", dict(ap=S_bf[:], constant=0.0))], writes=["R_Sbf"])

        def r_loads(t):
            r0, n = cfg.tiles[t]
            b = t % 2
            R.dma("sp", f"R_pr{b}", dict(out=pr[b][:n], in_=P[r0:r0 + n, O_RQ:O_RQ + 4 * RET_W]), reads=[("P", t)], writes=[("R_pr", b)])
            R.dma("sp", f"R_rope{b}", dict(out=rope[b][:n], in_=dr["rope"][r0:r0 + n, :, :]), writes=[("R_rope", b)])
        r_loads(0)
        for t, (r0, n) in enumerate(cfg.tiles):
            b = t % 2
            if t + 1 < len(cfg.tiles):
                r_loads(t + 1)
            prv = pr[b]
            for which, (src0, dst, ci, si, e1, e2) in enumerate(((0, qr, 0, 1, "dve", "pool"), (RET_W, kr, 2, 3, "pool", "dve"))):
                x = prv[:n, src0:src0 + RET_W].rearrange("p (h t d) -> p h t d", h=Hh, t=2)
                x1, x2 = x[:, :, 0, :], x[:, :, 1, :]
                cs_ = bc(rope[b][:n, ci:ci + 1, :], (n, Hh, 64))
                sn_ = bc(rope[b][:n, si:si + 1, :], (n, Hh, 64))
                o = dst[:n].rearrange("p h (t d) -> p h t d", t=2)
                T = tt[4 * which:4 * which + 4]
                tk = [("R_t", 4 * which + i) for i in range(4)]
                rd = [("R_pr", b), ("R_rope", b)]
                R.op(e1, [("tensor_tensor", dict(out=T[0][:n], in0=x1, in1=cs_, op=ALU.mult))], reads=rd, writes=[tk[0]])
                R.op(e2, [("tensor_tensor", dict(out=T[1][:n], in0=x2, in1=sn_, op=ALU.mult))], reads=rd, writes=[tk[1]])
                R.op(e1, [("tensor_tensor", dict(out=T[2][:n], in0=x1, in1=sn_, op=ALU.mult))], reads=rd, writes=[tk[2]])
                R.op(e2, [("tensor_tensor", dict(out=T[3][:n], in0=x2, in1=cs_, op=ALU.mult))], reads=rd, writes=[tk[3]])
                R.op(e1, [("tensor_tensor", dict(out=o[:, :, 0, :], in0=T[0][:n], in1=T[1][:n], op=ALU.subtract))], reads=[tk[0], tk[1]], writes=[("R_rot", which, 0)])
                R.op(e2, [("tensor_tensor", dict(out=o[:, :, 1, :], in0=T[2][:n], in1=T[3][:n], op=ALU.add))], reads=[tk[2], tk[3]], writes=[("R_rot", which, 1)])
            qk_keys = [("R_rot", w_, i_) for w_ in range(2) for i_ in range(2)]
            kcol = 0 if n == CH else 1
            R.op("dve", [("tensor_tensor", dict(out=kd[:n], in0=kr[:n], in1=bc(vec[:n, kcol, :].unsqueeze(2), (n, Hh, Dh)), op=ALU.mult))],
                 reads=qk_keys + ["R_vec"], writes=["R_kd"])
            R.op("act", [("activation", dict(out=v_bf[:n], in_=prv[:n, O_RV:O_RV + RET_W].rearrange("p (h d) -> p h d", h=Hh), func=AF.Copy))],
                 reads=[("R_pr", b)], writes=["R_vbf"])
            R.op("pe", [("transpose", dict(out=psT[:, h, :n], in_=qr[:n, h, :], identity=ident[:n, :n])) for h in range(Hh)] +
                       [("transpose", dict(out=psT[:, Hh + h, :n], in_=kr[:n, h, :], identity=ident[:n, :n])) for h in range(Hh)],
                 reads=qk_keys + ["ident"], writes=["R_psT"])
            R.op("dve", [("tensor_copy", dict(out=qkT[:, :, :n], in_=psT[:, :, :n]))], reads=["R_psT"], writes=["R_qkT"])
            R.op("pe", [("matmul", dict(out=ps_sc[:n, h, :n], lhsT=qkT[:, Hh + h, :n], rhs=qkT[:, h, :n], start=True, stop=True)) for h in range(Hh)],
                 reads=["R_qkT"], writes=["R_pssc"])
            R.op("dve", [("tensor_tensor", dict(out=sT[:n, :, :n], in0=ps_sc[:n, :, :n], in1=maskT[:n, :, :n], op=ALU.mult))],
                 reads=["R_pssc", "R_maskT"], writes=["R_sT"])
            R.op("pe", [("matmul", dict(out=ps_yi[:n, h, :], lhsT=sT[:n, h, :n], rhs=v_bf[:n, h, :], start=True, stop=True)) for h in range(Hh)],
                 reads=["R_sT", "R_vbf"], writes=["R_psyi"])
            R.op("pe", [("matmul", dict(out=ps_yo[:n, h, :], lhsT=qkT[:, h, :n], rhs=S_bf[:, h, :], start=True, stop=True)) for h in range(Hh)],
                 reads=["R_qkT", "R_Sbf"], writes=["R_psyo"])
            R.op("dve", [("tensor_tensor", dict(out=yo[:n], in0=ps_yo[:n], in1=bc(vec[:n, 2, :].unsqueeze(2), (n, Hh, Dh)), op=ALU.mult))],
                 reads=["R_psyo", "R_vec"], writes=["R_yo"])
            R.op("dve", [("tensor_tensor", dict(out=y[:n], in0=ps_yi[:n], in1=yo[:n], op=ALU.add))], reads=["R_psyi", "R_yo"], writes=["R_y"])
            R.op("pe", [("matmul", dict(out=ps_sc[:, h, :], lhsT=kd[:n, h, :], rhs=v_bf[:n, h, :], start=True, stop=True)) for h in range(Hh)],
                 reads=["R_kd", "R_vbf"], writes=["R_pssc"])
            R.op("dve", [("tensor_tensor", dict(out=S[:], in0=S[:], in1=cdec[:], op=ALU.mult))], reads=["R_cdec"], writes=["R_S"])
            R.op("dve", [("tensor_tensor", dict(out=S[:], in0=S[:], in1=ps_sc[:], op=ALU.add))], reads=["R_pssc"], writes=["R_S"])
            R.op("act", [("activation", dict(out=S_bf[:], in_=S[:], func=AF.Copy))], reads=["R_S"], writes=["R_Sbf"])
            R.op("dve", [("tensor_tensor", dict(out=ysq[:n], in0=y[:n], in1=y[:n], op=ALU.mult))], reads=["R_y"], writes=["R_ysq"])
            R.op("dve", [("tensor_reduce", dict(out=sm[0][:n], in_=ysq[:n], axis=AX.X, op=ALU.add))], reads=["R_ysq"], writes=[("R_sm", 0)])
            R.op("dve", [("tensor_scalar", dict(out=sm[1][:n], in0=sm[0][:n], scalar1=1.0 / Dh, scalar2=EPS, op0=ALU.mult, op1=ALU.add))],
                 reads=[("R_sm", 0)], writes=[("R_sm", 1)])
            R.op("act", [("activation", dict(out=sm[2][:n], in_=sm[1][:n], func=AF.Sqrt))], reads=[("R_sm", 1)], writes=[("R_sm", 2)])
            R.op("dve", [("reciprocal", dict(out=sm[3][:n], in_=sm[2][:n]))], reads=[("R_sm", 2)], writes=[("R_sm", 3)])
            R.op("act", [("activation", dict(out=sgl[:n], in_=prv[:n, O_RG:O_RG + RET_W].rearrange("p (h d) -> p h d", h=Hh), func=AF.Silu))],
                 reads=[("R_pr", b)], writes=["R_sg"])
            R.op("dve", [("tensor_tensor", dict(out=y[:n], in0=y[:n], in1=bc(sm[3][:n].unsqueeze(2), (n, Hh, Dh)), op=ALU.mult))],
                 reads=[("R_sm", 3)], writes=["R_y"])
            R.op("dve", [("tensor_tensor", dict(out=yb[:n].rearrange("p (h d) -> p h d", h=Hh), in0=y[:n], in1=sgl[:n], op=ALU.mult))],
                 reads=["R_y", "R_sg"], writes=["R_yb"])
            R.op("pe", [("transpose", dict(out=psT[:, h, :n], in_=yb[:n, h * 128:(h + 1) * 128], identity=ident[:n, :n])) for h in range(Hh)],
                 reads=["R_yb", "ident"], writes=["R_psT"])
            R.op("act", [("activation", dict(out=yTo[:, :, :n], in_=psT[:, 0:Hh, :n], func=AF.Copy))], reads=["R_psT"], writes=["R_yTo"])
            R.dma("sp", "R_yTo", dict(out=YT[0:RET_W, r0:r0 + n].rearrange("(k p) t -> p k t", p=128), in_=yTo[:, :, :n]),
                  reads=["R_yTo"], writes=[("YT", t)])
    R.barrier()


def phase_B_ssd(R, nc, cfg, li, dr, ident):
    P, YT = dr["P_ssd"], dr["YT"]
    O_Z, O_XBC, O_DT = 0, SSD_W, SSD_W + SSD_CONV
    NH, HP, G = SSD_H, SSD_P, SSD_G
    with ExitStack() as es:
        sb = lambda name, shape, dt: es.enter_context(nc.sbuf_tensor(uq(name), shape, dt))
        ps = lambda name, shape, dt: es.enter_context(nc.psum_tensor(uq(name), shape, dt))
        wconv = sb("S_wconv", [128, 4, SSD_CONV], F32)
        convb = sb("S_convb", [128, SSD_CONV], F32)
        normw = sb("S_normw", [128, SSD_W], F32)
        hv = sb("S_hv", [128, 3, NH], F32)
        tri = sb("S_tri", [128, 3, 128], F32)
        SLb = sb("S_SLb", [128, 128], BF16)
        ST = sb("S_ST", [128, NH, HP], F32)
        ST_bf = sb("S_STbf", [128, NH, HP], BF16)
        xs = [sb(f"S_xs{i}", [128, 4, 1024], F32) for i in range(2)]
        z = sb("S_z", [128, SSD_W], F32)
        dtr = sb("S_dtr", [128, NH], F32)
        xa = sb("S_xa", [128, SSD_CONV], F32)
        Bb = sb("S_Bb", [128, 512], BF16)
        Cb = sb("S_Cb", [128, 512], BF16)
        sm = [sb(f"S_sm{i}", [128, NH], F32) for i in range(8)]
        cst = sb("S_cst", [128, 2 * NH], F32)
        g4 = [sb(f"S_g4{i}", [128, G], F32) for i in range(4)]
        xdt = sb("S_xdt", [128, SSD_W], BF16)
        xdd = sb("S_xdd", [128, SSD_W], BF16)
        bcT = sb("S_bcT", [128, 8, 128], BF16)
        cbm = sb("S_cbm", [128, G, 128], F32)
        rseg = sb("S_rseg", [128, 16, 128], BF16)
        ed = sb("S_ed", [128, 16, 128], BF16)
        scT = sb("S_scT", [128, NH, 128], BF16)
        t1 = sb("S_t1", [128, SSD_W], F32)
        y = sb("S_y", [128, SSD_W], F32)
        yb = sb("S_yb", [128, SSD_W], BF16)
        yTo = sb("S_yTo", [128, 16, 128], BF16)
        psT = ps("S_psT", [128, 8, 128], BF16)
        ps_cb = ps("S_pscb", [128, G, 128], F32)
        psA = ps("S_psA", [128, SSD_W], F32)
        psB = ps("S_psB", [128, 1024], F32)

        R.dma("sp", "S_wconv", dict(out=wconv[:].rearrange("p k c -> p (k c)"), in_=dr["ssd_conv_w"][li:li + 1].rearrange("o k c -> o (k c)").partition_broadcast(128)), writes=["S_wconv"])
        R.dma("sp", "S_convb", dict(out=convb[:], in_=dr["ssd_conv_b"][li:li + 1, :].partition_broadcast(128)), writes=["S_convb"])
        R.dma("sp", "S_normw", dict(out=normw[:], in_=dr["ssd_norm_w"][li:li + 1, :].partition_broadcast(128)), writes=["S_normw"])
        for i, nm in enumerate(("ssd_dt_bias", "ssd_a_log", "ssd_d")):
            R.dma("sp", "S_hv", dict(out=hv[:, i, :], in_=dr[nm][li:li + 1, :].partition_broadcast(128)), writes=["S_hv"])
        R.dma("sp", "S_tri", dict(out=tri[:], in_=dr["tri"][:, :, :]), writes=["S_tri"])
        R.op("act", [("activation", dict(out=hv[:, 1, :], in_=hv[:, 1, :], func=AF.Exp))], reads=["S_hv"], writes=["S_hv"])
        R.op("dve", [("tensor_scalar", dict(out=hv[:, 1, :], in0=hv[:, 1, :], scalar1=-1.0, scalar2=None, op0=ALU.mult))], reads=["S_hv"], writes=["S_hv"])
        R.op("dve", [("tensor_copy", dict(out=SLb[:], in_=tri[:, 1, :]))], reads=["S_tri"], writes=["S_SLb"])
        R.op("pool", [("memset", dict(ap=ST[:], constant=0.0))], writes=["S_ST"])
        R.op("pool", [("memset", dict(ap=ST_bf[:], constant=0.0))], writes=["S_STbf"])
        U = tri[:, 0, :]
        ONES = tri[:, 2, :]
        xi = 0
        for t, (r0, n) in enumerate(cfg.tiles):
            pk = [("P", t)] + ([("P", t - 1)] if t > 0 else [])
            R.dma("pool", "S_z", dict(out=z[:n], in_=P[r0:r0 + n, O_Z:O_Z + SSD_W]), reads=pk, writes=["S_z"])
            R.dma("pool", "S_dtr", dict(out=dtr[:n], in_=P[r0:r0 + n, O_DT:O_DT + NH]), reads=pk, writes=["S_dtr"])

            def xs_loads(tt_, blk_):
                rr0, nn = cfg.tiles[tt_]
                gi = 3 * tt_ + blk_
                xb_ = xs[gi % 2]
                xk_ = ("S_xs", gi % 2)
                cc0 = O_XBC + blk_ * 1024
                pk_ = [("P", tt_)] + ([("P", tt_ - 1)] if tt_ > 0 else [])
                if tt_ == 0:
                    R.op("pool", [("memset", dict(ap=xb_[:nn].rearrange("p k c -> p (k c)"), constant=0.0))], writes=[xk_])
                for k in range(4):
                    sh = 3 - k
                    lo = max(0, sh - rr0)
                    R.dma("sp", f"S_xs{gi % 2}", dict(out=xb_[lo:nn, k, :], in_=P[rr0 - sh + lo:rr0 - sh + nn, cc0:cc0 + 1024]), reads=pk_, writes=[xk_])
            if t == 0:
                xs_loads(0, 0)
            for blk in range(3):
                xb = xs[xi % 2]
                xk = ("S_xs", xi % 2)
                xi += 1
                c0 = O_XBC + blk * 1024
                if blk < 2:
                    xs_loads(t, blk + 1)
                elif t + 1 < len(cfg.tiles):
                    xs_loads(t + 1, 0)
                e = ["pool", "dve"]
                for k in range(4):
                    R.op(e[k % 2], [("tensor_tensor", dict(out=xb[:n, k, :], in0=xb[:n, k, :], in1=wconv[:n, k, blk * 1024:(blk + 1) * 1024], op=ALU.mult))],
                         reads=["S_wconv"], writes=[xk])
                R.op("dve", [("tensor_tensor", dict(out=xb[:n, 0, :], in0=xb[:n, 0, :], in1=xb[:n, 1, :], op=ALU.add))], writes=[xk])
                R.op("dve", [("tensor_tensor", dict(out=xb[:n, 2, :], in0=xb[:n, 2, :], in1=xb[:n, 3, :], op=ALU.add))], writes=[xk])
                R.op("dve", [("tensor_tensor", dict(out=xb[:n, 0, :], in0=xb[:n, 0, :], in1=convb[:n, blk * 1024:(blk + 1) * 1024], op=ALU.add))], reads=["S_convb"], writes=[xk])
                R.op("dve", [("tensor_tensor", dict(out=xb[:n, 0, :], in0=xb[:n, 0, :], in1=xb[:n, 2, :], op=ALU.add))], writes=[xk])
                R.op("act", [("activation", dict(out=xa[:n, blk * 1024:(blk + 1) * 1024], in_=xb[:n, 0, :], func=AF.Silu))], reads=[xk], writes=[("S_xa", blk)])
            xak = [("S_xa", i) for i in range(3)]
            R.op("act", [("activation", dict(out=Bb[:n], in_=xa[:n, 2048:2560], func=AF.Copy))], reads=xak, writes=["S_Bb"])
            R.op("dve", [("tensor_copy", dict(out=Cb[:n], in_=xa[:n, 2560:3072]))], reads=xak, writes=["S_Cb"])
            R.op("dve", [("tensor_tensor", dict(out=sm[0][:n], in0=dtr[:n], in1=hv[:n, 0, :], op=ALU.add))], reads=["S_dtr", "S_hv"], writes=[("S_sm", 0)])
            R.op("act", [("activation", dict(out=sm[1][:n], in_=sm[0][:n], func=AF.Exp))], reads=[("S_sm", 0)], writes=[("S_sm", 1)])
            R.op("act", [("activation", dict(out=sm[2][:n], in_=sm[1][:n], func=AF.Ln, bias=1.0))], reads=[("S_sm", 1)], writes=[("S_sm", 2)])
            R.op("dve", [("tensor_tensor", dict(out=sm[3][:n], in0=sm[2][:n], in1=hv[:n, 1, :], op=ALU.mult))], reads=[("S_sm", 2), "S_hv"], writes=[("S_sm", 3)])
            R.op("pe", [("matmul", dict(out=psB[:n, 0:NH], lhsT=U[:n, :n], rhs=sm[3][:n], start=True, stop=True)),
                        ("matmul", dict(out=psB[:n, NH:2 * NH], lhsT=ONES[:n, :n], rhs=sm[3][:n], start=True, stop=True))],
                 reads=["S_tri", ("S_sm", 3)], writes=["S_psB"])
            R.op("act", [("activation", dict(out=cst[:n], in_=psB[:n, 0:2 * NH], func=AF.Copy))], reads=["S_psB"], writes=["S_cst"])
            R.op("act", [("activation", dict(out=sm[4][:n], in_=cst[:n, 0:NH], func=AF.Exp))], reads=["S_cst"], writes=[("S_sm", 4)])
            R.op("dve", [("tensor_tensor", dict(out=sm[5][:n], in0=cst[:n, NH:2 * NH], in1=cst[:n, 0:NH], op=ALU.subtract))], reads=["S_cst"], writes=[("S_sm", 5)])
            R.op("act", [("activation", dict(out=sm[5][:n], in_=sm[5][:n], func=AF.Exp))], reads=[("S_sm", 5)], writes=[("S_sm", 5)])
            R.op("act", [("activation", dict(out=sm[6][:n], in_=cst[:n, NH:2 * NH], func=AF.Exp))], reads=["S_cst"], writes=[("S_sm", 6)])
            R.op("dve", [("tensor_tensor", dict(out=sm[7][:n], in0=sm[2][:n], in1=sm[5][:n], op=ALU.mult))], reads=[("S_sm", 2), ("S_sm", 5)], writes=[("S_sm", 7)])
            x3 = xa[:n, 0:SSD_W].rearrange("p (h d) -> p h d", h=NH)
            R.op("dve", [("tensor_tensor", dict(out=xdt[:n].rearrange("p (h d) -> p h d", h=NH), in0=x3, in1=bc(sm[2][:n].unsqueeze(2), (n, NH, HP)), op=ALU.mult))],
                 reads=xak + [("S_sm", 2)], writes=["S_xdt"])
            R.op("dve", [("tensor_tensor", dict(out=xdd[:n].rearrange("p (h d) -> p h d", h=NH), in0=x3, in1=bc(sm[7][:n].unsqueeze(2), (n, NH, HP)), op=ALU.mult))],
                 reads=xak + [("S_sm", 7)], writes=["S_xdd"])
            R.op("pe", [("transpose", dict(out=psT[:, g, :n], in_=Bb[:n, g * 128:(g + 1) * 128], identity=ident[:n, :n])) for g in range(G)] +
                       [("transpose", dict(out=psT[:, G + g, :n], in_=Cb[:n, g * 128:(g + 1) * 128], identity=ident[:n, :n])) for g in range(G)],
                 reads=["S_Bb", "S_Cb", "ident"], writes=["S_psT"])
            R.op("dve", [("tensor_copy", dict(out=bcT[:, :, :n], in_=psT[:, :, :n]))], reads=["S_psT"], writes=["S_bcT"])
            R.op("pe", [("matmul", dict(out=ps_cb[:n, g, :n], lhsT=bcT[:, g, :n], rhs=bcT[:, G + g, :n], start=True, stop=True)) for g in range(G)],
                 reads=["S_bcT"], writes=["S_pscb"])
            R.op("dve", [("tensor_tensor", dict(out=cbm[:n, :, :n], in0=ps_cb[:n, :, :n], in1=bc(U[:n, :n].unsqueeze(1), (n, G, n)), op=ALU.mult))],
                 reads=["S_pscb", "S_tri"], writes=["S_cbm"])
            for half in range(2):
                h0 = half * 16
                R.op("dve", [("tensor_tensor", dict(out=rseg[:n, :, :n], in0=bc(sm[3][:n, h0:h0 + 16].unsqueeze(2), (n, 16, n)), in1=bc(U[:n, :n].unsqueeze(1), (n, 16, n)), op=ALU.mult))],
                     reads=[("S_sm", 3), "S_tri"], writes=["S_rseg"])
                R.op("pe", [("matmul", dict(out=psA[:n, q * 512:q * 512 + 4 * n].rearrange("p (h i) -> p h i", h=4), lhsT=SLb[:n, :n], rhs=rseg[:n, 4 * q:4 * q + 4, :n], start=True, stop=True)) for q in range(4)],
                     reads=["S_rseg", "S_SLb"], writes=["S_psA"])
                for q in range(4):
                    R.op("act", [("activation", dict(out=ed[:n, 4 * q:4 * q + 4, :n], in_=psA[:n, q * 512:q * 512 + 4 * n].rearrange("p (h i) -> p h i", h=4), func=AF.Exp))],
                         reads=["S_psA"], writes=[("S_ed", q)])
                R.op("dve", [("tensor_tensor", dict(out=scT[:n, h0:h0 + 16, :n].rearrange("p (g r) i -> p g r i", g=2),
                                                    in0=ed[:n, :, :n].rearrange("p (g r) i -> p g r i", g=2),
                                                    in1=bc(cbm[:n, 2 * half:2 * half + 2, :n].unsqueeze(2), (n, 2, 8, n)), op=ALU.mult))],
                     reads=[("S_ed", q) for q in range(4)] + ["S_cbm"], writes=[("S_scT", half)])
            R.op("pe", [("matmul", dict(out=psA[:n, h * HP:(h + 1) * HP], lhsT=scT[:n, h, :n], rhs=xdt[:n, h * HP:(h + 1) * HP], start=True, stop=True)) for h in range(NH)],
                 reads=[("S_scT", 0), ("S_scT", 1), "S_xdt"], writes=["S_psA"])
            for half in range(2):
                R.op("pe", [("matmul", dict(out=psB[:n, gg * 512:(gg + 1) * 512], lhsT=bcT[:, G + 2 * half + gg, :n],
                                            rhs=ST_bf[:, (2 * half + gg) * 8:(2 * half + gg + 1) * 8, :].rearrange("p h d -> p (h d)"), start=True, stop=True)) for gg in range(2)],
                     reads=["S_bcT", "S_STbf"], writes=["S_psB"])
                R.op("dve", [("tensor_tensor", dict(out=t1[:n, half * 1024:(half + 1) * 1024].rearrange("p (h d) -> p h d", h=16),
                                                    in0=psB[:n, :].rearrange("p (h d) -> p h d", h=16),
                                                    in1=bc(sm[4][:n, half * 16:(half + 1) * 16].unsqueeze(2), (n, 16, HP)), op=ALU.mult))],
                     reads=["S_psB", ("S_sm", 4)], writes=[("S_t1", half)])
            R.op("dve", [("tensor_tensor", dict(out=y[:n], in0=psA[:n, :], in1=t1[:n], op=ALU.add))], reads=["S_psA", ("S_t1", 0), ("S_t1", 1)], writes=["S_y"])
            R.op("pe", [("matmul", dict(out=psA[:, g * 512:(g + 1) * 512], lhsT=Bb[:n, g * 128:(g + 1) * 128], rhs=xdd[:n, g * 512:(g + 1) * 512], start=True, stop=True)) for g in range(G)],
                 reads=["S_Bb", "S_xdd"], writes=["S_psA"])
            if n == CH:
                R.op("dve", [("tensor_tensor", dict(out=ST[:], in0=ST[:], in1=bc(sm[6][:, :].unsqueeze(2), (128, NH, HP)), op=ALU.mult))],
                     reads=[("S_sm", 6)], writes=["S_ST"])
            R.op("dve", [("tensor_tensor", dict(out=ST[:].rearrange("p h d -> p (h d)"), in0=ST[:].rearrange("p h d -> p (h d)"), in1=psA[:, :], op=ALU.add))],
                 reads=["S_psA"], writes=["S_ST"])
            R.op("act", [("activation", dict(out=ST_bf[:], in_=ST[:], func=AF.Copy))], reads=["S_ST"], writes=["S_STbf"])
            R.op("dve", [("tensor_tensor", dict(out=x3, in0=x3, in1=bc(hv[:n, 2, :].unsqueeze(2), (n, NH, HP)), op=ALU.mult))], reads=["S_hv", "S_xdt", "S_xdd"], writes=xak)
            R.op("dve", [("tensor_tensor", dict(out=y[:n], in0=y[:n], in1=xa[:n, 0:SSD_W], op=ALU.add))], reads=xak, writes=["S_y"])
            R.op("act", [("activation", dict(out=z[:n], in_=z[:n], func=AF.Silu))], writes=["S_z"])
            R.op("dve", [("tensor_tensor", dict(out=y[:n], in0=y[:n], in1=z[:n], op=ALU.mult))], reads=["S_z"], writes=["S_y"])
            R.op("dve", [("tensor_tensor", dict(out=t1[:n], in0=y[:n], in1=y[:n], op=ALU.mult))], reads=["S_y"], writes=[("S_t1", 0), ("S_t1", 1)])
            R.op("dve", [("tensor_reduce", dict(out=g4[0][:n], in_=t1[:n].rearrange("p (g d) -> p g d", g=G), axis=AX.X, op=ALU.add))], reads=[("S_t1", 0), ("S_t1", 1)], writes=[("S_g4", 0)])
            R.op("dve", [("tensor_scalar", dict(out=g4[1][:n], in0=g4[0][:n], scalar1=1.0 / (SSD_W // G), scalar2=EPS, op0=ALU.mult, op1=ALU.add))], reads=[("S_g4", 0)], writes=[("S_g4", 1)])
            R.op("act", [("activation", dict(out=g4[2][:n], in_=g4[1][:n], func=AF.Sqrt))], reads=[("S_g4", 1)], writes=[("S_g4", 2)])
            R.op("dve", [("reciprocal", dict(out=g4[3][:n], in_=g4[2][:n]))], reads=[("S_g4", 2)], writes=[("S_g4", 3)])
            R.op("dve", [("tensor_tensor", dict(out=y[:n].rearrange("p (g d) -> p g d", g=G), in0=y[:n].rearrange("p (g d) -> p g d", g=G), in1=bc(g4[3][:n].unsqueeze(2), (n, G, SSD_W // G)), op=ALU.mult))],
                 reads=[("S_g4", 3)], writes=["S_y"])
            R.op("dve", [("tensor_tensor", dict(out=yb[:n], in0=y[:n], in1=normw[:n], op=ALU.mult))], reads=["S_y", "S_normw"], writes=["S_yb"])
            for half in range(2):
                R.op("pe", [("transpose", dict(out=psT[:, k, :n], in_=yb[:n, (half * 8 + k) * 128:(half * 8 + k + 1) * 128], identity=ident[:n, :n])) for k in range(8)],
                     reads=["S_yb", "ident"], writes=["S_psT"])
                R.op("act", [("activation", dict(out=yTo[:, half * 8:(half + 1) * 8, :n], in_=psT[:, :, :n], func=AF.Copy))], reads=["S_psT"], writes=[("S_yTo", half)])
            R.dma("sp", "S_yTo", dict(out=YT[2048:4096, r0:r0 + n].rearrange("(k p) t -> p k t", p=128), in_=yTo[:, :, :n]),
                  reads=[("S_yTo", 0), ("S_yTo", 1)], writes=[("YT", t)])
    R.barrier()


def phase_B_rwkv(R, nc, cfg, li, dr, ident, ident_f):
    P, YT = dr["P_rw"], dr["YT"]
    O_RW = 0
    NH, HN, W = RW_H, RW_N, RW_W
    C0 = float(np.exp(-0.5))
    with ExitStack() as es:
        sb = lambda name, shape, dt: es.enter_context(nc.sbuf_tensor(uq(name), shape, dt))
        ps = lambda name, shape, dt: es.enter_context(nc.psum_tensor(uq(name), shape, dt))
        mu = sb("W_mu", [128, RW_COLS], F32)
        vecs = sb("W_vecs", [128, 5, W], F32)
        w2a = sb("W_w2a", [128, W], F32)
        a2a = sb("W_a2a", [128, W], F32)
        g2 = sb("W_g2", [128, 2, W], F32)
        msk = sb("W_msk", [128, 5, 128], F32)
        ones = sb("W_ones", [128, 128], F32)
        Tst = sb("W_T", [128, 8, HN], F32)
        T_bf = sb("W_Tbf", [128, 8, HN], BF16)
        cols = sb("W_cols", [128, RW_COLS], F32)
        prev = sb("W_prev", [128, RW_COLS], F32)
        lin = sb("W_lin", [128, 288], F32)
        loT = sb("W_loT", [128, 4, 128], F32)
        F = [sb(f"W_F{i}", [128, W], F32) for i in range(10)] + [prev[:, 0:W], prev[:, W:2 * W]]
        Bq = [sb(f"W_B{i}", [128, W], BF16) for i in range(7)]
        fT2 = sb("W_fT2", [128, 2, 8, 128], BF16)
        rP = sb("W_rP", [128, NH, 128], BF16)
        kP = sb("W_kP", [128, NH, 128], BF16)
        Am = [sb(f"W_A{i}", [128, NH, 128], BF16) for i in range(3)]
        Pm = [sb(f"W_P{i}", [128, NH, 128], BF16) for i in range(2)]
        PTm = [sb(f"W_PT{i}", [128, NH, 128], BF16) for i in range(2)]
        MT = sb("W_MT", [128, NH, 128], BF16)
        s16 = [sb(f"W_s16_{i}", [128, NH], F32) for i in range(6)]
        WC = sb("W_WC", [128, 8], F32)
        yTo = sb("W_yTo", [128, 8, 128], BF16)
        X0 = ps("W_X0", [128, 512], F32)
        Y2 = ps("W_Y2", [128, W], F32)
        Z2 = ps("W_Z2", [128, W], F32)
        QQ = ps("W_QQ", [128, 8, 128], F32)

        def bcv(i, n):
            return vecs[:n, i, :]

        R.dma("sp", "W_mu", dict(out=mu[:], in_=dr["rwkv_mu"][li:li + 1, :].partition_broadcast(128)), writes=["W_mu"])
        for i, nm in enumerate(("rwkv_k_k", "rwkv_k_a", "rwkv_ln_w", "rwkv_ln_b")):
            R.dma("sp", "W_vecs", dict(out=vecs[:, i, :], in_=dr[nm][li:li + 1, :].partition_broadcast(128)), writes=["W_vecs"])
        R.dma("sp", "W_vecs", dict(out=vecs[:, 4, :], in_=dr["rwkv_r_k"][li:li + 1].rearrange("o h d -> o (h d)").partition_broadcast(128)), writes=["W_vecs"])
        R.dma("sp", "W_w2a", dict(out=w2a[0:64, :], in_=dr["rwkv_w2"][li, :, :]), writes=["W_w2a"])
        R.dma("sp", "W_w2a", dict(out=w2a[64:65, :], in_=dr["rwkv_w0"][li:li + 1, :]), writes=["W_w2a"])
        R.dma("sp", "W_a2a", dict(out=a2a[0:64, :], in_=dr["rwkv_a2"][li, :, :]), writes=["W_a2a"])
        R.dma("sp", "W_a2a", dict(out=a2a[64:65, :], in_=dr["rwkv_a0"][li:li + 1, :]), writes=["W_a2a"])
        R.dma("sp", "W_g2", dict(out=g2[:, 0, :], in_=dr["rwkv_g2"][li, 0:128, :]), writes=["W_g2"])
        R.dma("sp", "W_g2", dict(out=g2[0:32, 1, :], in_=dr["rwkv_g2"][li, 128:160, :]), writes=["W_g2"])
        R.dma("sp", "W_msk", dict(out=msk[:], in_=dr["rw_msk"][:, :, :]), writes=["W_msk"])
        R.op("pool", [("memset", dict(ap=ones[:], constant=1.0))], writes=["W_ones"])
        R.op("pool", [("memset", dict(ap=rP[:].rearrange("p a b -> p (a b)"), constant=0.0))], writes=[("W_fT", 0)])
        R.op("pool", [("memset", dict(ap=kP[:].rearrange("p a b -> p (a b)"), constant=0.0))], writes=[("W_fT", 0)])
        R.op("pool", [("memset", dict(ap=loT[:].rearrange("p a b -> p (a b)"), constant=1.0))], writes=["W_loT"])
        R.op("pool", [("memset", dict(ap=Tst[:].rearrange("p a b -> p (a b)"), constant=0.0))], writes=["W_T"])
        R.op("pool", [("memset", dict(ap=T_bf[:].rearrange("p a b -> p (a b)"), constant=0.0))], writes=["W_Tbf"])
        MU, MS, MSn, MLn, MI = (msk[:, i, :] for i in range(5))
        qi = 0
        for t, (r0, n) in enumerate(cfg.tiles):
            pk = [("P", t)] + ([("P", t - 1)] if t > 0 else [])
            R.countdown = None
            FK = [("W_F", i) for i in range(12)]
            def w_loads(tt_):
                rr0, nn = cfg.tiles[tt_]
                pk_ = [("P", tt_)] + ([("P", tt_ - 1)] if tt_ > 0 else [])
                R.dma("sp", "W_cols", dict(out=cols[:nn], in_=P[rr0:rr0 + nn, O_RW:O_RW + RW_COLS]), reads=pk_, writes=["W_cols"])
                if tt_ == 0:
                    R.op("pool", [("memset", dict(ap=prev[:nn], constant=0.0))], writes=["W_prev", FK[10], FK[11]])
                    R.dma("sp", "W_prev", dict(out=prev[1:nn], in_=P[0:nn - 1, O_RW:O_RW + RW_COLS]), reads=pk_, writes=["W_prev", FK[10], FK[11]])
                else:
                    R.dma("sp", "W_prev", dict(out=prev[:nn], in_=P[rr0 - 1:rr0 + nn - 1, O_RW:O_RW + RW_COLS]), reads=pk_, writes=["W_prev", FK[10], FK[11]])
            if t == 0:
                w_loads(0)
            R.op("pool", [("tensor_tensor", dict(out=prev[:n], in0=prev[:n], in1=cols[:n], op=ALU.subtract))], reads=["W_cols"], writes=["W_prev"])
            R.op("pool", [("tensor_tensor", dict(out=prev[:n], in0=prev[:n], in1=mu[:n], op=ALU.mult))], reads=["W_mu"], writes=["W_prev"])
            R.op("dve", [("tensor_tensor", dict(out=cols[:n], in0=cols[:n], in1=prev[:n], op=ALU.add))], reads=["W_prev"], writes=["W_cols"])
            if cfg.stop == "W1":
                continue
            r_, k_, v_ = cols[:n, 0:W], cols[:n, W:2 * W], cols[:n, 2 * W:3 * W]
            R.op("act", [("activation", dict(out=lin[:n, 0:64], in_=cols[:n, 3072:3136], func=AF.Tanh))], reads=["W_cols"], writes=[("W_lin", 0)])
            R.op("act", [("activation", dict(out=lin[:n, 64:128], in_=cols[:n, 3136:3200], func=AF.Copy))], reads=["W_cols"], writes=[("W_lin", 1)])
            R.op("act", [("activation", dict(out=lin[:n, 128:288], in_=cols[:n, 3200:3360], func=AF.Sigmoid))], reads=["W_cols"], writes=[("W_lin", 2)])
            X0t = X0[:].rearrange("p (a b) -> p a b", a=4)
            R.op("pe", [("transpose", dict(out=X0t[0:64, 0, :n], in_=lin[:n, 0:64], identity=ident_f[:n, :n])),
                        ("transpose", dict(out=X0t[0:64, 1, :n], in_=lin[:n, 64:128], identity=ident_f[:n, :n])),
                        ("transpose", dict(out=X0t[:, 2, :n], in_=lin[:n, 128:256], identity=ident_f[:n, :n])),
                        ("transpose", dict(out=X0t[0:32, 3, :n], in_=lin[:n, 256:288], identity=ident_f[:n, :n]))],
                 reads=[("W_lin", 0), ("W_lin", 1), ("W_lin", 2), "ident_f"], writes=["W_X0"])
            R.op("dve", [("tensor_copy", dict(out=loT[0:64, 0:2, :n], in_=X0t[0:64, 0:2, :n]))], reads=["W_X0"], writes=["W_loT"])
            R.op("dve", [("tensor_copy", dict(out=loT[:, 2, :n], in_=X0t[:, 2, :n]))], reads=["W_X0"], writes=["W_loT"])
            R.op("dve", [("tensor_copy", dict(out=loT[0:32, 3, :n], in_=X0t[0:32, 3, :n]))], reads=["W_X0"], writes=["W_loT"])
            if cfg.stop == "W2":
                continue
            e_, a_, g_ = F[0], F[1], F[2]
            R.op("pe", [("matmul", dict(out=Y2[:n, hh * 512:(hh + 1) * 512], lhsT=loT[0:65, 0, :n], rhs=w2a[0:65, hh * 512:(hh + 1) * 512], start=True, stop=True)) for hh in range(2)],
                 reads=["W_loT", "W_w2a"], writes=["W_Y2"])
            R.op("act", [("activation", dict(out=e_[:n], in_=Y2[:n, :], func=AF.Sigmoid))], reads=["W_Y2"], writes=[FK[0]])
            R.op("pe", [("matmul", dict(out=Z2[:n, hh * 512:(hh + 1) * 512], lhsT=loT[0:65, 1, :n], rhs=a2a[0:65, hh * 512:(hh + 1) * 512], start=True, stop=True)) for hh in range(2)],
                 reads=["W_loT", "W_a2a"], writes=["W_Z2"])
            R.op("act", [("activation", dict(out=a_[:n], in_=Z2[:n, :], func=AF.Sigmoid))], reads=["W_Z2"], writes=[FK[1]])
            R.op("pe", [m for hh in range(2) for m in (
                        ("matmul", dict(out=Y2[:n, hh * 512:(hh + 1) * 512], lhsT=loT[:, 2, :n], rhs=g2[:, 0, hh * 512:(hh + 1) * 512], start=True, stop=False)),
                        ("matmul", dict(out=Y2[:n, hh * 512:(hh + 1) * 512], lhsT=loT[0:32, 3, :n], rhs=g2[0:32, 1, hh * 512:(hh + 1) * 512], start=False, stop=True)))],
                 reads=["W_loT", "W_g2"], writes=["W_Y2"])
            R.op("act", [("activation", dict(out=g_[:n], in_=Y2[:n, :], func=AF.Copy))], reads=["W_Y2"], writes=[FK[2]])
            if cfg.stop == "W3":
                continue
            if cfg.stop.startswith("CUT"):
                R.countdown = int(cfg.stop[3:])
            kk_, k2_, b_, tmp_ = F[7], F[8], F[9], F[10]
            R.op("pool", [("tensor_tensor", dict(out=kk_[:n], in0=k_, in1=bcv(0, n), op=ALU.mult))], reads=["W_cols", "W_vecs"], writes=[FK[7]])
            R.op("pool", [("tensor_tensor", dict(out=tmp_[:n], in0=kk_[:n], in1=kk_[:n], op=ALU.mult))], reads=[FK[7]], writes=[FK[10]])
            R.op("dve", [("tensor_reduce", dict(out=s16[0][:n], in_=tmp_[:n].rearrange("p (h d) -> p h d", h=NH), axis=AX.X, op=ALU.add))], reads=[FK[10]], writes=[("W_s16", 0)])
            R.op("act", [("activation", dict(out=s16[0][:n], in_=s16[0][:n], func=AF.Sqrt))], reads=[("W_s16", 0)], writes=[("W_s16", 0)])
            R.op("dve", [("tensor_scalar", dict(out=s16[0][:n], in0=s16[0][:n], scalar1=1e-12, scalar2=None, op0=ALU.max))], reads=[("W_s16", 0)], writes=[("W_s16", 0)])
            R.op("dve", [("reciprocal", dict(out=s16[1][:n], in_=s16[0][:n]))], reads=[("W_s16", 0)], writes=[("W_s16", 1)])
            R.op("dve", [("tensor_tensor", dict(out=kk_[:n].rearrange("p (h d) -> p h d", h=NH), in0=kk_[:n].rearrange("p (h d) -> p h d", h=NH), in1=bc(s16[1][:n].unsqueeze(2), (n, NH, HN)), op=ALU.mult))],
                 reads=[("W_s16", 1)], writes=[FK[7]])
            R.op("dve", [("scalar_tensor_tensor", dict(out=tmp_[:n], in0=a_[:n], scalar=-1.0, in1=bcv(1, n), op0=ALU.add, op1=ALU.mult))], reads=[FK[1], "W_vecs"], writes=[FK[10]])
            R.op("dve", [("tensor_scalar", dict(out=tmp_[:n], in0=tmp_[:n], scalar1=1.0, scalar2=None, op0=ALU.add))], reads=[FK[10]], writes=[FK[10]])
            R.op("pool", [("tensor_tensor", dict(out=k2_[:n], in0=k_, in1=tmp_[:n], op=ALU.mult))], reads=["W_cols", FK[10]], writes=[FK[8]])
            R.op("pool", [("tensor_tensor", dict(out=b_[:n], in0=kk_[:n], in1=a_[:n], op=ALU.mult))], reads=[FK[7], FK[1]], writes=[FK[9]])
            bo_ = F[3]
            R.op("pool", [("tensor_tensor", dict(out=tmp_[:n], in0=r_, in1=k2_[:n], op=ALU.mult))], reads=["W_cols", FK[8]], writes=[FK[10]])
            R.op("pool", [("tensor_tensor", dict(out=tmp_[:n], in0=tmp_[:n], in1=bcv(4, n), op=ALU.mult))], reads=["W_vecs"], writes=[FK[10]])
            R.op("dve", [("tensor_reduce", dict(out=s16[2][:n], in_=tmp_[:n].rearrange("p (h d) -> p h d", h=NH), axis=AX.X, op=ALU.add))], reads=[FK[10]], writes=[("W_s16", 2)])
            if cfg.stop == "W4":
                continue
            R.op("pe", [("matmul", dict(out=Y2[:n, hh * 512:(hh + 1) * 512], lhsT=MU[:n, :n], rhs=e_[:n, hh * 512:(hh + 1) * 512], start=True, stop=True)) for hh in range(2)],
                 reads=["W_msk", FK[0]], writes=["W_Y2"])
            R.op("pe", [("matmul", dict(out=Z2[:n, hh * 512:(hh + 1) * 512], lhsT=ones[:n, :n], rhs=e_[:n, hh * 512:(hh + 1) * 512], start=True, stop=True)) for hh in range(2)],
                 reads=["W_ones", FK[0]], writes=["W_Z2"])
            R.op("pe", [("matmul", dict(out=X0[:, hp:hp + 1], lhsT=e_[:n, hp * 128:(hp + 1) * 128], rhs=ones[:n, 0:1], start=True, stop=True)) for hp in range(8)],
                 reads=["W_ones", FK[0]], writes=["W_X0"])
            R.op("act", [("activation", dict(out=WC[:], in_=X0[:, 0:8], func=AF.Exp, scale=-C0))], reads=["W_X0"], writes=["W_WC"])
            Wt, Winv, Wprev, Wrel = F[3], F[4], F[5], F[6]
            R.op("act", [("activation", dict(out=Wt[:n], in_=Y2[:n, :], func=AF.Exp, scale=-C0))], reads=["W_Y2"], writes=[FK[3]])
            R.op("act", [("activation", dict(out=Winv[:n], in_=Y2[:n, :], func=AF.Exp, scale=C0))], reads=["W_Y2"], writes=[FK[4]])
            R.op("dve", [("tensor_tensor", dict(out=Wprev[:n], in0=e_[:n], in1=Y2[:n, :], op=ALU.subtract))], reads=["W_Y2", FK[0]], writes=[FK[5]])
            R.op("act", [("activation", dict(out=Wprev[:n], in_=Wprev[:n], func=AF.Exp, scale=C0))], reads=[FK[5]], writes=[FK[5]])
            R.op("act", [("activation", dict(out=Wrel[:n], in_=Z2[:n, :], func=AF.Exp, scale=-C0))], reads=["W_Z2"], writes=[FK[6]])
            R.op("pool", [("tensor_tensor", dict(out=Wrel[:n], in0=Wrel[:n], in1=Winv[:n], op=ALU.mult))], reads=[FK[4]], writes=[FK[6]])
            if cfg.stop == "W5":
                continue
            BK = [("W_B", i) for i in range(7)]
            rt, kkt, kh, bh, kbar, bbar, v_bf = Bq
            specs = ((rt, r_, Wt, "W_cols", FK[3]), (kkt, kk_[:n], Wprev, FK[7], FK[5]), (kh, k2_[:n], Winv, FK[8], FK[4]), (bh, b_[:n], Winv, FK[9], FK[4]),
                     (kbar, k2_[:n], Wrel, FK[8], FK[6]), (bbar, b_[:n], Wrel, FK[9], FK[6]))
            for i, (dst, src, wgt, k1, k2k) in enumerate(specs):
                R.op("pool" if i % 2 else "dve", [("tensor_tensor", dict(out=dst[:n], in0=src, in1=wgt[:n], op=ALU.mult))], reads=[k1, k2k], writes=[BK[i]])
            R.op("act", [("activation", dict(out=v_bf[:n], in_=v_, func=AF.Copy))], reads=["W_cols"], writes=[BK[6]])
            R.op("dve", [("tensor_tensor", dict(out=bo_[:n].rearrange("p (h d) -> p h d", h=NH), in0=v_.rearrange("p (h d) -> p h d", h=NH), in1=bc(s16[2][:n].unsqueeze(2), (n, NH, HN)), op=ALU.mult))],
                 reads=["W_cols", ("W_s16", 2)], writes=[FK[3]])
            if t + 1 < len(cfg.tiles):
                w_loads(t + 1)
            if cfg.stop == "W6":
                continue
            Z2b = Z2[:].bitcast(BF16).rearrange("p (k t) -> p k t", t=128)
            for half in range(2):
                srcs = (rt, kkt) if half == 0 else (kh, bh)
                R.op("pe", [("transpose", dict(out=Z2b[:, qq * 8 + hp, :n], in_=srcs[qq][:n, hp * 128:(hp + 1) * 128], identity=ident[:n, :n])) for qq in range(2) for hp in range(8)],
                     reads=[BK[2 * half], BK[2 * half + 1], "ident"], writes=["W_Z2"])
                if half == 0:
                    for qq, dstp in enumerate((rP, kP)):
                        d4 = dstp[:].rearrange("p (a b) t -> p a b t", b=2)
                        R.op("dve", [("tensor_copy", dict(out=d4[0:64, :, 0, :n], in_=Z2b[0:64, qq * 8:(qq + 1) * 8, :n]))], reads=["W_Z2"], writes=[("W_fT", 0)])
                        R.op("act", [("activation", dict(out=d4[64:128, :, 1, :n], in_=Z2b[64:128, qq * 8:(qq + 1) * 8, :n], func=AF.Copy))], reads=["W_Z2"], writes=[("W_fT", 0)])
                else:
                    R.op("dve", [("tensor_copy", dict(out=fT2[:, :, :, :n], in_=Z2b[:, :, :n].rearrange("p (a b) t -> p a b t", a=2)))], reads=["W_Z2"], writes=[("W_fT", 1)])
            fk = [("W_fT", 0), ("W_fT", 1)]

            def opnd(q, h):
                if q == 0:
                    return rP[:, h, :n]
                if q == 1:
                    return kP[:, h, :n]
                return fT2[:, q - 2, h // 2, :n]
            ArkT, ArbT, AkkT = Am
            jobs = ((ArkT, 2, 0, MU, ("W_A", 0)), (ArbT, 3, 0, MU, ("W_A", 1)), (AkkT, 2, 1, MS, ("W_A", 2)),
                    (PTm[0], 3, 1, MSn, ("W_PT", 0)), (Pm[0], 1, 3, MLn, ("W_P", 0)))
            units = [(QQ[:, :, :], "W_QQ"), (Y2[:].rearrange("p (h i) -> p h i", h=8), "W_Y2"), (Z2[:].rearrange("p (h i) -> p h i", h=8), "W_Z2")]
            HB, HW = 2, 8
            for dst, ql, qr_, mk, key in jobs:
                for hb in range(HB):
                    qb, qk = units[qi % 3]
                    qi += 1
                    R.op("pe", [("matmul", dict(out=qb[:n, i, :n], lhsT=opnd(ql, HW * hb + i), rhs=opnd(qr_, HW * hb + i), start=True, stop=True)) for i in range(HW)],
                         reads=fk, writes=[qk])
                    R.op("dve", [("tensor_tensor", dict(out=dst[:n, HW * hb:HW * hb + HW, :n], in0=qb[:n, :, :n], in1=bc(mk[:n, :n].unsqueeze(1), (n, HW, n)), op=ALU.mult))],
                         reads=[qk, "W_msk"], writes=[key + (hb,)])
            if cfg.stop == "W8":
                continue
            R.op("pool", [("tensor_tensor", dict(out=MT[:n, :, :n], in0=PTm[0][:n, :, :n], in1=bc(MI[:n, :n].unsqueeze(1), (n, NH, n)), op=ALU.add))],
                 reads=[("W_PT", 0, hb) for hb in range(HB)] + ["W_msk"], writes=[("W_MT", hb) for hb in range(HB)])
            cur = 0
            NSQ = 6
            for kq in range(NSQ):
                nxt = 1 - cur
                for hb in range(HB):
                    qb, qk = units[qi % 3]
                    qi += 1
                    R.op("pe", [("matmul", dict(out=qb[:n, i, :n], lhsT=PTm[cur][:n, HW * hb + i, :n], rhs=Pm[cur][:n, HW * hb + i, :n], start=True, stop=True)) for i in range(HW)],
                         reads=[("W_PT", cur, hb), ("W_P", cur, hb)], writes=[qk])
                    R.op("act", [("activation", dict(out=Pm[nxt][:n, HW * hb:HW * hb + HW, :n], in_=qb[:n, :, :n], func=AF.Copy))], reads=[qk], writes=[("W_P", nxt, hb)])
                    if kq < NSQ - 1:
                        qb, qk = units[qi % 3]
                        qi += 1
                        R.op("pe", [("matmul", dict(out=qb[:n, i, :n], lhsT=Pm[cur][:n, HW * hb + i, :n], rhs=PTm[cur][:n, HW * hb + i, :n], start=True, stop=True)) for i in range(HW)],
                             reads=[("W_PT", cur, hb), ("W_P", cur, hb)], writes=[qk])
                        R.op("dve", [("tensor_copy", dict(out=PTm[nxt][:n, HW * hb:HW * hb + HW, :n], in_=qb[:n, :, :n]))], reads=[qk], writes=[("W_PT", nxt, hb)])
                for hb in range(HB):
                    qb, qk = units[qi % 3]
                    qi += 1
                    R.op("pe", [("matmul", dict(out=qb[:n, i, :n], lhsT=Pm[nxt][:n, HW * hb + i, :n], rhs=MT[:n, HW * hb + i, :n], start=True, stop=True)) for i in range(HW)],
                         reads=[("W_P", nxt, hb), ("W_MT", hb)], writes=[qk])
                    R.op("dve", [("tensor_tensor", dict(out=MT[:n, HW * hb:HW * hb + HW, :n], in0=qb[:n, :, :n], in1=MT[:n, HW * hb:HW * hb + HW, :n], op=ALU.add))],
                         reads=[qk], writes=[("W_MT", hb)])
                cur = nxt
            if cfg.stop == "W9":
                continue
            akeys = lambda i: [("W_A", i, hb) for hb in range(HB)]
            mtk = [("W_MT", hb) for hb in range(HB)]
            RHS0, Uneg, yo = rt, kkt, kh
            hs = lambda h: slice(h * HN, (h + 1) * HN)
            R.op("pe", [m for h in range(NH) for m in (
                        ("matmul", dict(out=Y2[:n, hs(h)], lhsT=opnd(1, h), rhs=T_bf[:, h // 2, :], start=True, stop=False)),
                        ("matmul", dict(out=Y2[:n, hs(h)], lhsT=AkkT[:n, h, :n], rhs=v_bf[:n, hs(h)], start=False, stop=True)))],
                 reads=fk + ["W_Tbf", BK[6]] + akeys(2), writes=["W_Y2"])
            R.op("act", [("activation", dict(out=RHS0[:n], in_=Y2[:n, :], func=AF.Copy))], reads=["W_Y2"] + fk, writes=[BK[0]])
            if cfg.stop == "W10":
                continue
            R.op("pe", [("matmul", dict(out=Z2[:n, hs(h)], lhsT=MT[:n, h, :n], rhs=RHS0[:n, hs(h)], start=True, stop=True)) for h in range(NH)],
                 reads=mtk + [BK[0]], writes=["W_Z2"])
            R.op("act", [("activation", dict(out=Uneg[:n], in_=Z2[:n, :], func=AF.Copy, scale=-1.0))], reads=["W_Z2"] + fk, writes=[BK[1]])
            if cfg.stop == "W11":
                continue
            R.op("pe", [m for h in range(NH) for m in (
                        ("matmul", dict(out=Y2[:n, hs(h)], lhsT=opnd(0, h), rhs=T_bf[:, h // 2, :], start=True, stop=False)),
                        ("matmul", dict(out=Y2[:n, hs(h)], lhsT=ArkT[:n, h, :n], rhs=v_bf[:n, hs(h)], start=False, stop=False)),
                        ("matmul", dict(out=Y2[:n, hs(h)], lhsT=ArbT[:n, h, :n], rhs=Uneg[:n, hs(h)], start=False, stop=True)))],
                 reads=fk + ["W_Tbf", BK[6], BK[1]] + akeys(0) + akeys(1), writes=["W_Y2"])
            if cfg.stop == "W12":
                continue
            Y2s = Z2[:].rearrange("p (a b c) -> p a b c", a=8, b=2)
            R.op("pe", [m for h in range(NH) for m in (
                        ("matmul", dict(out=Y2s[:, h // 2, h % 2, :], lhsT=kbar[:n, (h // 2) * 128:(h // 2 + 1) * 128], rhs=v_bf[:n, hs(h)], start=True, stop=False)),
                        ("matmul", dict(out=Y2s[:, h // 2, h % 2, :], lhsT=bbar[:n, (h // 2) * 128:(h // 2 + 1) * 128], rhs=Uneg[:n, hs(h)], start=False, stop=True)))],
                 reads=[BK[4], BK[5], BK[6], BK[1]], writes=["W_Z2"])
            R.op("dve", [("tensor_tensor", dict(out=Tst[:], in0=Tst[:], in1=bc(WC[:, :].unsqueeze(2), (128, 8, HN)), op=ALU.mult))], reads=["W_WC"], writes=["W_T"])
            R.op("dve", [("tensor_tensor", dict(out=Tst[0:64], in0=Tst[0:64], in1=Y2s[0:64, :, 0, :], op=ALU.add))], reads=["W_Z2"], writes=["W_T"])
            R.op("dve", [("tensor_tensor", dict(out=Tst[64:128], in0=Tst[64:128], in1=Y2s[64:128, :, 1, :], op=ALU.add))], reads=["W_Z2"], writes=["W_T"])
            R.op("act", [("activation", dict(out=T_bf[:], in_=Tst[:], func=AF.Copy))], reads=["W_T"], writes=["W_Tbf"])
            if cfg.stop == "W13":
                continue
            yc, ysq = F[0], F[1]
            h3 = lambda ap: ap.rearrange("p (h d) -> p h d", h=NH)
            R.op("dve", [("tensor_reduce", dict(out=s16[3][:n], in_=h3(Y2[:n, :]), axis=AX.X, op=ALU.add))], reads=["W_Y2"], writes=[("W_s16", 3)])
            R.op("dve", [("tensor_scalar", dict(out=s16[3][:n], in0=s16[3][:n], scalar1=1.0 / HN, scalar2=None, op0=ALU.mult))], reads=[("W_s16", 3)], writes=[("W_s16", 3)])
            R.op("dve", [("tensor_tensor", dict(out=h3(yc[:n]), in0=h3(Y2[:n, :]), in1=bc(s16[3][:n].unsqueeze(2), (n, NH, HN)), op=ALU.subtract))],
                 reads=["W_Y2", ("W_s16", 3)], writes=[FK[0]])
            R.op("pool", [("tensor_tensor", dict(out=ysq[:n], in0=yc[:n], in1=yc[:n], op=ALU.mult))], reads=[FK[0]], writes=[FK[1]])
            R.op("dve", [("tensor_reduce", dict(out=s16[4][:n], in_=h3(ysq[:n]), axis=AX.X, op=ALU.add))], reads=[FK[1]], writes=[("W_s16", 4)])
            R.op("dve", [("tensor_scalar", dict(out=s16[4][:n], in0=s16[4][:n], scalar1=1.0 / HN, scalar2=64e-5, op0=ALU.mult, op1=ALU.add))], reads=[("W_s16", 4)], writes=[("W_s16", 4)])
            R.op("act", [("activation", dict(out=s16[4][:n], in_=s16[4][:n], func=AF.Sqrt))], reads=[("W_s16", 4)], writes=[("W_s16", 4)])
            R.op("dve", [("reciprocal", dict(out=s16[5][:n], in_=s16[4][:n]))], reads=[("W_s16", 4)], writes=[("W_s16", 5)])
            R.op("dve", [("tensor_tensor", dict(out=h3(yc[:n]), in0=h3(yc[:n]), in1=bc(s16[5][:n].unsqueeze(2), (n, NH, HN)), op=ALU.mult))], reads=[("W_s16", 5)], writes=[FK[0]])
            R.op("pool", [("tensor_tensor", dict(out=yc[:n], in0=yc[:n], in1=bcv(2, n), op=ALU.mult))], reads=["W_vecs"], writes=[FK[0]])
            R.op("pool", [("tensor_tensor", dict(out=yc[:n], in0=yc[:n], in1=bcv(3, n), op=ALU.add))], reads=["W_vecs"], writes=[FK[0]])
            R.op("pool", [("tensor_tensor", dict(out=yc[:n], in0=yc[:n], in1=bo_[:n], op=ALU.add))], reads=[FK[3]], writes=[FK[0]])
            R.op("pool", [("tensor_tensor", dict(out=yo[:n], in0=yc[:n], in1=g_[:n], op=ALU.mult))], reads=[FK[2]] + fk, writes=[BK[2]])
            Z2b8 = Z2b[:, 0:8, :]
            R.op("pe", [("transpose", dict(out=Z2b8[:, k, :n], in_=yo[:n, k * 128:(k + 1) * 128], identity=ident[:n, :n])) for k in range(8)],
                 reads=[BK[2], "ident"], writes=["W_Z2"])
            R.op("act", [("activation", dict(out=yTo[:, :, :n], in_=Z2b8[:, :, :n], func=AF.Copy))], reads=["W_Z2"], writes=["W_yTo"])
            R.dma("sp", "W_yTo", dict(out=YT[RET_W:RET_W + W, r0:r0 + n].rearrange("(k p) t -> p k t", p=128), in_=yTo[:, :, :n]),
                  reads=["W_yTo"], writes=[("YT", t)])
    R.countdown = None
    R.barrier()


def host_consts(cfg):
    c = {}
    L = cfg.L
    half = 64
    inv = (np.float32(10000.0) ** (-np.arange(half, dtype=np.float32) / np.float32(half))).astype(np.float32)
    ang = (np.arange(L, dtype=np.float32)[:, None] * inv[None, :]).astype(np.float32)
    cs, sn = np.cos(ang).astype(np.float32), np.sin(ang).astype(np.float32)
    sc = np.float32(RET_D ** -0.5)
    c["rope"] = np.ascontiguousarray(np.stack([cs, sn, cs * sc, sn * sc], axis=1)).astype(np.float32)
    log_g = np.log(1.0 - 2.0 ** (-5.0 - np.arange(RET_H, dtype=np.float64)))
    idx = np.arange(CH, dtype=np.float64)
    diff = idx[None, :] - idx[:, None]
    m = np.where((diff >= 0)[:, None, :], np.exp(np.maximum(diff, 0)[:, None, :] * log_g[None, :, None]), 0.0)
    c["ret_maskT"] = np.ascontiguousarray(m).astype(np.float32)
    vec = np.zeros((128, 3, RET_H), np.float32)
    vec[:, 0, :] = np.exp((CH - 1 - idx)[:, None] * log_g[None, :])
    vec[:NMETA, 1, :] = np.exp((NMETA - 1 - idx[:NMETA])[:, None] * log_g[None, :])
    vec[:, 2, :] = np.exp((idx + 1.0)[:, None] * log_g[None, :])
    c["ret_vec"] = vec
    tri = np.zeros((128, 3, 128), np.float32)
    ii = np.arange(128)
    tri[:, 0, :] = (ii[:, None] <= ii[None, :])
    tri[:, 1, :] = (ii[:, None] > ii[None, :])
    tri[:, 2, :] = 1.0
    c["tri"] = tri
    mk = np.zeros((128, 5, 128), np.float32)
    mk[:, 0, :] = (ii[:, None] <= ii[None, :])
    mk[:, 1, :] = (ii[:, None] < ii[None, :])
    mk[:, 2, :] = -1.0 * (ii[:, None] < ii[None, :])
    mk[:, 3, :] = -1.0 * (ii[:, None] > ii[None, :])
    mk[:, 4, :] = (ii[:, None] == ii[None, :])
    c["rw_msk"] = mk
    return c


def build_program(cfg):
    nc = bass.Bass("TRN2", target_bir_lowering=False)
    dr = {}

    def din(name, shape, dt=F32):
        dr[name] = nc.dram_tensor(name, list(shape), dt, kind="ExternalInput").ap()

    def dscratch(name, shape, dt=F32):
        kind = "ExternalOutput" if cfg.debug else "Internal"
        dr[name] = nc.dram_tensor(name, list(shape), dt, kind=kind).ap()

    din("x", (cfg.S, D))
    din("meta_tokens", (NMETA, D))
    din("ident_in", (128, 128))
    din("rope", (cfg.L, 4, 64))
    din("ret_maskT", (128, RET_H, 128))
    din("ret_vec", (128, 3, RET_H))
    din("tri", (128, 3, 128))
    din("rw_msk", (128, 5, 128))
    din("rwkv_mu", (cfg.depth, RW_COLS))
    for nm in ("rwkv_w0", "rwkv_a0", "rwkv_k_k", "rwkv_k_a", "rwkv_ln_w", "rwkv_ln_b"):
        din(nm, (cfg.depth, RW_W))
    din("rwkv_w2", (cfg.depth, 64, RW_W))
    din("rwkv_a2", (cfg.depth, 64, RW_W))
    din("rwkv_g2", (cfg.depth, 160, RW_W))
    din("rwkv_r_k", (cfg.depth, RW_H, RW_N))
    din("ssd_conv_w", (cfg.depth, 4, SSD_CONV))
    din("ssd_conv_b", (cfg.depth, SSD_CONV))
    din("ssd_dt_bias", (cfg.depth, SSD_H))
    din("ssd_a_log", (cfg.depth, SSD_H))
    din("ssd_d", (cfg.depth, SSD_H))
    din("ssd_norm_w", (cfg.depth, SSD_W))
    for nm in ("norm_mix_pre", "norm_mix_post", "norm_ffn_pre", "norm_ffn_post"):
        din(nm, (cfg.depth, D))
    din("w_in", (cfg.depth, D, IN_COLS))
    din("w_branch_ret", (cfg.depth, RET_W, D))
    din("w_branch_rwkv", (cfg.depth, RW_W, D))
    din("w_branch_ssd", (cfg.depth, SSD_W, D))
    din("w_out", (cfg.depth, D, D))
    din("ffn_w_gate", (cfg.depth, D, FFN))
    din("ffn_w_up", (cfg.depth, D, FFN))
    din("ffn_w_down", (cfg.depth, FFN, D))
    if cfg.yt_input:
        din("YT", (2 * D, cfg.L), BF16)
    else:
        dscratch("YT", (2 * D, cfg.L), BF16)
    dr["H"] = nc.dram_tensor("H", [cfg.L, D], F32, kind="ExternalOutput").ap()
    dscratch("P_ret", (cfg.L, 4096))
    dscratch("P_rw", (cfg.L, RW_COLS))
    dscratch("P_ssd", (cfg.L, NP_COLS - O_Z))
    dscratch("SGT", (3 * D, cfg.L))
    R = Rec()
    with ExitStack() as es:
        ident_f = es.enter_context(nc.sbuf_tensor("ident_f", [128, 128], F32))
        ident = es.enter_context(nc.sbuf_tensor("ident", [128, 128], BF16))
        R.dma("sp", "ident", dict(out=ident_f[:], in_=dr["ident_in"][:, :]), writes=["ident_f"])
        R.op("dve", [("tensor_copy", dict(out=ident[:], in_=ident_f[:]))], reads=["ident_f"], writes=["ident"])
        R.dma("sp", "Hinit", dict(out=dr["H"][0:NMETA, :], in_=dr["meta_tokens"][:, :]), writes=[("H", 0)])
        for i in range(0, cfg.nchunks, 8):
            r0 = NMETA + i * CH
            nn = min(8, cfg.nchunks - i) * CH
            R.dma("sp", "Hinit", dict(out=dr["H"][r0:r0 + nn, :], in_=dr["x"][i * CH:i * CH + nn, :]), writes=[("Hx", i)])
        for t in range(len(cfg.tiles)):
            R.lastw[("H", t)] = ("d_Hinit", R.cnt["d_Hinit"])
        for li in cfg.layers:
            if cfg.stop == "init":
                break
            if "A" in cfg.phases:
                phase_A(R, nc, cfg, li, dr, ident)
            if "R" in cfg.phases:
                phase_B_ret(R, nc, cfg, li, dr, ident)
            if "W" in cfg.phases:
                phase_B_rwkv(R, nc, cfg, li, dr, ident, ident_f)
            if "S" in cfg.phases:
                phase_B_ssd(R, nc, cfg, li, dr, ident)
            if "C" in cfg.phases:
                phase_C(R, nc, cfg, li, dr, ident)
            if "D" in cfg.phases:
                phase_D(R, nc, cfg, li, dr, ident)
        R.emit(nc)
    return nc, R


def host_inputs(inputs, b, cfg):
    m = {"x": np.ascontiguousarray(inputs["x"][b, :cfg.S]), "meta_tokens": np.ascontiguousarray(inputs["meta_tokens"]),
         "ident_in": np.eye(128, dtype=np.float32)}
    m.update(host_consts(cfg))
    for k in ("norm_mix_pre", "norm_mix_post", "norm_ffn_pre", "norm_ffn_post", "w_in", "w_branch_ret", "w_branch_rwkv",
              "w_branch_ssd", "w_out", "ffn_w_gate", "ffn_w_up", "ffn_w_down",
              "ssd_conv_w", "ssd_conv_b", "ssd_dt_bias", "ssd_a_log", "ssd_d", "ssd_norm_w",
              "rwkv_mu", "rwkv_w0", "rwkv_w2", "rwkv_a0", "rwkv_a2", "rwkv_g2", "rwkv_k_k", "rwkv_k_a", "rwkv_r_k", "rwkv_ln_w", "rwkv_ln_b"):
        m[k] = np.ascontiguousarray(inputs[k])
    return m


_CACHE = {}


def kernel(**inputs):
    cfg = Cfg(nchunks=64, layers=(0, 1, 2, 3))
    if "prog" not in _CACHE:
        _CACHE["prog"] = build_program(cfg)
    nc, _ = _CACHE["prog"]
    nb = inputs["x"].shape[0]
    in_maps = [host_inputs(inputs, b, cfg) for b in range(nb)]
    res = run_bass_kernel_spmd(nc, in_maps, core_ids=list(range(nb)))
    out = np.stack([np.asarray(res.results[b]["H"])[NMETA:] for b in range(nb)]).astype(np.float32)
    return out
```
